# Optimizing a Trainium2 kernel written in Bass

```python
import jax, jax.numpy as jnp
from jax import lax
import numpy as np

D_MODEL = 4096
BATCH = 4
SEQ = 2048
DEPTH = 2

CHUNK = 64
N_HEADS_DN = 16
HEAD_K = 128
HEAD_V = 128
D_DN_K = N_HEADS_DN * HEAD_K
D_DN_V = N_HEADS_DN * HEAD_V
DN_CONV = 4
D_SC = D_MODEL // 2
SC_CONV = 3
D_FF = 11008
EPS = 1e-6

SPLIT_SIZES = (2 * D_DN_K + D_DN_V,
               N_HEADS_DN,
               N_HEADS_DN,
               D_DN_V,
               D_SC,
               D_SC,
               D_SC,
               D_MODEL,
               D_MODEL)
D_IN = 2 * D_DN_K + D_DN_V + 2 * N_HEADS_DN + D_DN_V + 3 * D_SC + 2 * D_MODEL

kernel_name = "hybrid_gdn_shortconv_macaron_sandwich"


def _split(t, sizes):
    idx = [int(s) for s in np.cumsum(sizes)[:-1]]
    return jnp.split(t, idx, axis=-1)


def rmsnorm(x, g):
    xf = x.astype(jnp.float32)
    y = xf * lax.rsqrt(jnp.mean(xf * xf, axis=-1, keepdims=True) + EPS)
    return (y * g.astype(jnp.float32)).astype(x.dtype)


def l2norm(x):
    return x * lax.rsqrt(jnp.sum(x * x, axis=-1, keepdims=True) + EPS)


def causal_depthwise_conv(x, w):
    width = w.shape[-1]
    s = x.shape[1]
    xp = jnp.pad(x, ((0, 0), (width - 1, 0), (0, 0)))
    return sum(xp[:, j:j + s, :] * w[:, j] for j in range(width))


def swiglu_ffn(x, w_in, w_out):
    gate, up = jnp.split(x @ w_in, 2, axis=-1)
    return (jax.nn.silu(gate) * up) @ w_out


def gated_delta_rule(q, k, v, g, beta):
    bsz, s, h, _ = q.shape
    n = s // CHUNK

    def to_chunks(t):
        return t.reshape(bsz, n, CHUNK, h, -1).transpose(1, 0, 3, 2, 4)

    q, k, v = to_chunks(q), to_chunks(k), to_chunks(v)
    g = g.reshape(bsz, n, CHUNK, h).transpose(1, 0, 3, 2)
    beta = beta.reshape(bsz, n, CHUNK, h).transpose(1, 0, 3, 2)
    gc = jnp.cumsum(g, axis=-1)

    idx = jnp.arange(CHUNK)
    incl = idx[:, None] >= idx[None, :]
    strict = idx[:, None] > idx[None, :]
    diff = gc[..., :, None] - gc[..., None, :]
    decay = jnp.exp(jnp.where(incl, diff, -jnp.inf))

    k_beta = k * beta[..., None]
    v_beta = v * beta[..., None]
    kk = jnp.einsum('nbhid,nbhjd->nbhij', k_beta, k)
    lower = jnp.where(strict, kk * decay, 0.0)
    eye = jnp.eye(CHUNK, dtype=q.dtype)
    t_mat = lax.linalg.triangular_solve(eye + lower, jnp.broadcast_to(eye, lower.shape),
                                        left_side=True, lower=True, unit_diagonal=True)
    u_val = jnp.einsum('nbhij,nbhjd->nbhid', t_mat, v_beta)
    k_cum = jnp.einsum('nbhij,nbhjd->nbhid', t_mat, k_beta * jnp.exp(gc)[..., None])
    attn_intra = jnp.einsum('nbhid,nbhjd->nbhij', q, k) * decay
    q_dec = q * jnp.exp(gc)[..., None]
    g_last = gc[..., -1]
    k_dec = k * jnp.exp(g_last[..., None] - gc)[..., None]

    def step(state, xs):
        q_d, k_d, u, kc, a_in, gl = xs
        v_new = u - jnp.einsum('bhck,bhkv->bhcv', kc, state)
        o = jnp.einsum('bhck,bhkv->bhcv', q_d, state) + jnp.einsum('bhij,bhjv->bhiv', a_in, v_new)
        state = state * jnp.exp(gl)[..., None, None] + jnp.einsum('bhck,bhcv->bhkv', k_d, v_new)
        return state, o

    s0 = jnp.zeros((bsz, h, q.shape[-1], v.shape[-1]), dtype=q.dtype)
    _, o = lax.scan(step, s0, (q_dec, k_dec, u_val, k_cum, attn_intra, g_last))
    return o.transpose(1, 0, 3, 2, 4).reshape(bsz, s, h, -1)


def hybrid_mixer(u, w_in, dn_conv_w, dn_a_log, dn_dt_bias, dn_norm_g, w_dn_out,
                 sc_conv_w, w_sc_out, w_o):
    bsz, s, _ = u.shape
    qkv, a, b, z, sc_b, sc_c, sc_x, gate_a, gate_b = _split(u @ w_in, SPLIT_SIZES)

    qkv = jax.nn.silu(causal_depthwise_conv(qkv, dn_conv_w))
    q, k, v = _split(qkv, (D_DN_K, D_DN_K, D_DN_V))
    q = l2norm(q.reshape(bsz, s, N_HEADS_DN, HEAD_K).astype(jnp.float32)) * (HEAD_K ** -0.5)
    k = l2norm(k.reshape(bsz, s, N_HEADS_DN, HEAD_K).astype(jnp.float32))
    v = v.reshape(bsz, s, N_HEADS_DN, HEAD_V).astype(jnp.float32)
    g = -jnp.exp(dn_a_log.astype(jnp.float32)) * jax.nn.softplus(
        a.astype(jnp.float32) + dn_dt_bias.astype(jnp.float32))
    beta = jax.nn.sigmoid(b.astype(jnp.float32))
    o = gated_delta_rule(q, k, v, g, beta).astype(u.dtype)
    o = rmsnorm(o, dn_norm_g) * jax.nn.silu(z.reshape(bsz, s, N_HEADS_DN, HEAD_V))
    y_a = o.reshape(bsz, s, D_DN_V) @ w_dn_out

    y_b = (sc_b * causal_depthwise_conv(sc_c * sc_x, sc_conv_w)) @ w_sc_out

    merged = jax.nn.sigmoid(gate_a) * y_a + jax.nn.sigmoid(gate_b) * y_b
    return merged @ w_o


def setup_inputs(seed: int = 0) -> dict:
    key = jax.random.key(seed)
    ks = jax.random.split(key, 24)

    def nrm(k, shape, fan_in):
        return jax.random.normal(k, shape, jnp.float32) * (fan_in ** -0.5)

    def gain(k, shape):
        return 1.0 + 0.05 * jax.random.normal(k, shape, jnp.float32)

    L = DEPTH
    dt = jnp.exp(jax.random.uniform(ks[10], (L, N_HEADS_DN), jnp.float32,
                                    np.log(1e-3), np.log(1e-1)))
    return {
        "x": jax.random.normal(ks[0], (BATCH, SEQ, D_MODEL), jnp.float32),
        "ffn1_pre_g": gain(ks[1], (L, D_MODEL)),
        "ffn1_post_g": gain(ks[2], (L, D_MODEL)),
        "w_ffn1_in": nrm(ks[3], (L, D_MODEL, 2 * D_FF), D_MODEL),
        "w_ffn1_out": nrm(ks[4], (L, D_FF, D_MODEL), D_FF),
        "mix_pre_g": gain(ks[5], (L, D_MODEL)),
        "mix_post_g": gain(ks[6], (L, D_MODEL)),
        "w_in": nrm(ks[7], (L, D_MODEL, D_IN), D_MODEL),
        "dn_conv_w": nrm(ks[8], (L, 2 * D_DN_K + D_DN_V, DN_CONV), DN_CONV),
        "dn_a_log": jnp.log(jax.random.uniform(ks[9], (L, N_HEADS_DN), jnp.float32, 1.0, 16.0)),
        "dn_dt_bias": jnp.log(jnp.expm1(dt)),
        "dn_norm_g": gain(ks[11], (L, HEAD_V)),
        "w_dn_out": nrm(ks[12], (L, D_DN_V, D_MODEL), D_DN_V),
        "sc_conv_w": nrm(ks[13], (L, D_SC, SC_CONV), SC_CONV),
        "w_sc_out": nrm(ks[14], (L, D_SC, D_MODEL), D_SC),
        "w_o": nrm(ks[15], (L, D_MODEL, D_MODEL), D_MODEL),
        "ffn2_pre_g": gain(ks[16], (L, D_MODEL)),
        "ffn2_post_g": gain(ks[17], (L, D_MODEL)),
        "w_ffn2_in": nrm(ks[18], (L, D_MODEL, 2 * D_FF), D_MODEL),
        "w_ffn2_out": nrm(ks[19], (L, D_FF, D_MODEL), D_FF),
    }


def reference(x, ffn1_pre_g, ffn1_post_g, w_ffn1_in, w_ffn1_out, mix_pre_g, mix_post_g,
              w_in, dn_conv_w, dn_a_log, dn_dt_bias, dn_norm_g, w_dn_out, sc_conv_w,
              w_sc_out, w_o, ffn2_pre_g, ffn2_post_g, w_ffn2_in, w_ffn2_out):
    h = x
    for l in range(DEPTH):
        f1 = swiglu_ffn(rmsnorm(h, ffn1_pre_g[l]), w_ffn1_in[l], w_ffn1_out[l])
        h = h + 0.5 * rmsnorm(f1, ffn1_post_g[l])
        m = hybrid_mixer(rmsnorm(h, mix_pre_g[l]), w_in[l], dn_conv_w[l], dn_a_log[l],
                         dn_dt_bias[l], dn_norm_g[l], w_dn_out[l], sc_conv_w[l],
                         w_sc_out[l], w_o[l])
        h = h + rmsnorm(m, mix_post_g[l])
        f2 = swiglu_ffn(rmsnorm(h, ffn2_pre_g[l]), w_ffn2_in[l], w_ffn2_out[l])
        h = h + 0.5 * rmsnorm(f2, ffn2_post_g[l])
    return h
```

```python
import numpy as np
import concourse.bass as bass
import concourse.mybir as mybir
from concourse.bass_utils import run_bass_kernel_spmd

F32 = mybir.dt.float32
BF16 = mybir.dt.bfloat16
AF = mybir.ActivationFunctionType
ALU = mybir.AluOpType

NCORES = 4
D = 4096
KC = 32
DFF = 11008
FC = 86
T = 512
NH = 16
EPS = 1e-6
NEG = -30000.0
ENGS = ("sync", "act", "dve", "pe", "pool")
SEM_ROTATE = 30000
SLOT = 8192
NSLOT = 3

C_Q, C_K, C_V = 0, 2048, 4096
C_A, C_B = 6144, 6160
C_Z = 6176
C_SB, C_SC, C_SX = 8224, 10272, 12320
C_GA, C_GB = 14368, 18464

SP_G = {"ffn1_pre_g": 0, "ffn1_post_g": 32, "mix_pre_g": 64, "mix_post_g": 96,
        "ffn2_pre_g": 128, "ffn2_post_g": 160}
SP_DNW = 192
SP_SCW = 384
SP_NG = 432
SP_ALOG = 433
SP_DTB = 434
SP_L = 435
CS_ID = 0
CS_MTI = 128
CS_M01 = 640
CS_MLS = 1152
CS_I8 = 1664
CS_N = 2176


class Sem:
    def __init__(self, nc, name):
        self.h = nc.alloc_semaphore(name=name)
        self.val = 0


class Res:
    __slots__ = ("w", "r")

    def __init__(self):
        self.w = None
        self.r = []


class Prog:
    def __init__(self, nc):
        self.nc = nc
        self.q = {e: [] for e in ENGS}
        self.esem = {}
        self.esem_ids = {}
        self.nsem = 0
        self.planning = False
        self.fence_toks = []
        self.last = {}
        self.dmasems = []

    def newsem(self, name):
        self.nsem += 1
        s = Sem(self.nc, f"{name}_{self.nsem}")
        return s

    def _engine_sig(self, eng):
        s = self.esem.get(eng)
        if s is None or s.val >= SEM_ROTATE:
            s = self.newsem("e" + eng)
            self.esem[eng] = s
            self.esem_ids.setdefault(eng, set()).add(id(s))
        s.val += 1
        return (s, s.val)

    def emit(self, eng, fn, reads=(), writes=(), dmasem=None):
        if self.planning:
            return None
        waits = list(self.fence_toks)
        rawset = set((id(s), v) for (s, v) in self.fence_toks)
        for r in reads:
            if r.w is not None:
                waits.append(r.w)
                rawset.add((id(r.w[0]), r.w[1]))
        for w in writes:
            if w.w is not None:
                waits.append(w.w)
            waits.extend(w.r)
        if dmasem is not None:
            dmasem.val += 16
            tok = (dmasem, dmasem.val)
            inc = 16
        else:
            tok = self._engine_sig(eng)
            inc = 1
        best = {}
        mine = self.esem_ids.setdefault(eng, set())
        for (s, v) in waits:
            if id(s) in mine and (id(s), v) not in rawset:
                continue
            if id(s) not in best or best[id(s)][1] < v:
                best[id(s)] = (s, v)
        self.q[eng].append((fn, list(best.values()), tok, inc))
        self.last[id(tok[0])] = tok
        for r in reads:
            r.r.append(tok)
            if len(r.r) > 48:
                r.r = r.r[-48:]
        for w in writes:
            w.w = tok
            w.r = []
        return tok

    def fence(self):
        if self.planning:
            return
        self.fence_toks = list(self.last.values())

    def final_wait(self, engname):
        self.q[engname].append((None, list(self.last.values()), None, 0))

    def run_block(self):
        nc = self.nc
        P = self
        with nc.Block() as block:
            def mk(engname):
                def f(eng):
                    waited = {}
                    for (fn, waits, tok, inc) in P.q[engname]:
                        for (s, v) in waits:
                            if waited.get(id(s), 0) >= v:
                                continue
                            eng.wait_ge(s.h, v)
                            waited[id(s)] = v
                        if fn is None:
                            continue
                        ins = fn(eng)
                        ins.then_inc(tok[0].h, inc)
                return f
            block.sync(mk("sync"))
            block.scalar(mk("act"))
            block.vector(mk("dve"))
            block.tensor(mk("pe"))
            block.gpsimd(mk("pool"))


class WeightStream:
    def __init__(self, prog):
        self.p = prog
        self.buf = prog.nc.alloc_sbuf_tensor("wring", [128, NSLOT * SLOT], BF16)
        self.sems = [prog.newsem(f"w{i}") for i in range(NSLOT)]
        self.res = [Res() for _ in range(NSLOT)]
        self.plan = []
        self.issued = 0
        self.consumed = 0

    def _issue(self, idx):
        s = idx % NSLOT
        base = s * SLOT
        first = True
        tok = None
        for (src, off, kn, cols) in self.plan[idx]:
            dst = self.buf[:, base + off: base + off + kn * cols].rearrange("p (k c) -> p k c", c=cols)
            tok = self.p.emit("pool", (lambda eng, d=dst, sr=src: eng.dma_start(out=d, in_=sr)),
                              writes=[self.res[s]] if first else [], dmasem=self.sems[s])
            first = False
        self.res[s].w = tok
        self.res[s].r = []

    def prefetch(self):
        while self.issued < len(self.plan) and self.issued < self.consumed + NSLOT:
            self._issue(self.issued)
            self.issued += 1

    def get(self, pieces):
        if self.p.planning:
            self.plan.append(pieces)
            return 0, None
        idx = self.consumed
        self.prefetch()
        s = idx % NSLOT
        return s * SLOT, self.res[s]

    def done(self):
        if self.p.planning:
            return
        self.consumed += 1
        self.prefetch()


class Buf:
    def __init__(self, ap):
        self.ap = ap
        self.res = Res()


def build_program(NT, dbg=False):
    nc = bass.Bass("TRN2", target_bir_lowering=False)
    NTOK = NT * T
    L = 2

    def din(name, shape):
        return nc.dram_tensor(name, list(shape), F32, kind="ExternalInput").ap()

    x = din("x", [NTOK, D])
    W = {
        "w_ffn1_in": din("w_ffn1_in", [L, D, 2 * DFF]), "w_ffn1_out": din("w_ffn1_out", [L, DFF, D]),
        "w_in": din("w_in", [L, D, 22560]), "w_dn_out": din("w_dn_out", [L, 2048, D]),
        "w_sc_out": din("w_sc_out", [L, 2048, D]), "w_o": din("w_o", [L, D, D]),
        "w_ffn2_in": din("w_ffn2_in", [L, D, 2 * DFF]), "w_ffn2_out": din("w_ffn2_out", [L, DFF, D]),
    }
    spd = din("sp", [128, L * SP_L])
    cstd = din("cst", [128, CS_N])
    out = nc.dram_tensor("out", [NTOK, D], F32, kind="ExternalOutput").ap()
    dbgd = nc.dram_tensor("dbg", [6, KC, 128, T], F32, kind="ExternalOutput").ap() if dbg else None
    dbg2 = nc.dram_tensor("dbg2", [128, 64 * T], BF16, kind="ExternalOutput").ap() if dbg else None
    hT = nc.dram_tensor("hT_scr", [KC, 128, NTOK], F32).ap()
    fT = nc.dram_tensor("fT_scr", [KC, 128, NTOK], F32).ap()
    hT_res = [[Res() for _ in range(KC)] for _ in range(NT)]
    fT_res = [[Res() for _ in range(KC)] for _ in range(NT)]

    P = Prog(nc)
    ws = WeightStream(P)

    def sb(name, shape, dt=F32):
        return Buf(nc.alloc_sbuf_tensor("sb_" + name, list(shape), dt))

    sp = sb("sp", [128, L * SP_L])
    ident = sb("ident", [128, 128])
    ones_bf = sb("ones_bf", [128, 128], BF16)
    ones_f = sb("ones_f", [128, 128])
    epsc = sb("epsc", [128, 1])
    onec = sb("onec", [128, 1])
    xn = sb("xn", [128, KC, T], BF16)
    NSTG = 9
    stg = [sb(f"stg{i}", [128, T]) for i in range(NSTG)]
    rstd = [sb(f"rstd{i}", [128, T]) for i in range(2)]
    ARENA = 96 * 1024
    arena = nc.alloc_sbuf_tensor("arena", [128, ARENA // 4], F32)
    arena_off = [0]

    def carve(nbytes, dt=F32):
        o = arena_off[0]
        arena_off[0] += (nbytes + 31) // 32 * 32
        assert arena_off[0] <= ARENA, arena_off[0]
        a = arena[:, o // 4:(o + nbytes) // 4]
        if dt == BF16:
            a = a.bitcast(BF16)
        return a

    hidden = Buf(carve(FC * T * 2, BF16).rearrange("p (f t) -> p f t", t=T))
    arena_off[0] = 0
    ogT = Buf(carve(NH * T * 2, BF16).rearrange("p (f t) -> p f t", t=T))
    Sst = [Buf(carve(512)) for _ in range(NH)]
    Stmp = [Buf(carve(512)) for _ in range(2)]
    mk_mti = Buf(carve(2048)); mk_m01 = Buf(carve(2048)); mk_mls = Buf(carve(2048)); mk_i8 = Buf(carve(2048))
    halo_q = Buf(carve(48 * 3 * 4).rearrange("p (c w) -> p c w", w=3))
    halo_s = Buf(carve(16 * 2 * 4).rearrange("p (c w) -> p c w", w=2))
    pf = {n: Buf(carve(512).rearrange("p (c h) -> p c h", h=16)) for n in ("gcP", "btP", "bgP", "kdP")}
    selb = Buf(carve(512))
    raw = [Buf(carve(520 * 4)) for _ in range(2)]
    ff = {n: Buf(carve(2048)) for n in ("gF", "btF")}
    dr_mark = arena_off[0]
    mT = Buf(carve(KC * T * 2, BF16).rearrange("p (f t) -> p f t", t=T))
    scT = Buf(carve(NH * T * 2, BF16).rearrange("p (f t) -> p f t", t=T))
    arena_off[0] = dr_mark
    qT = Buf(carve(2048)); kT = Buf(carve(2048)); vT = Buf(carve(2048)); zs = Buf(carve(2048))
    kP = Buf(carve(4096).rearrange("p (c d) -> p c d", d=128))
    kbgP = Buf(carve(4096).rearrange("p (c d) -> p c d", d=128))
    vP = Buf(carve(4096).rearrange("p (c d) -> p c d", d=128))
    uP = vP
    gcB = Buf(carve(2048)); btB = Buf(carve(2048)); egB = Buf(carve(2048))
    dTm = Buf(carve(2048)); lT = Buf(carve(2048)); lM = Buf(carve(2048)); attnT = Buf(carve(2048))
    PTx = Buf(carve(2048))
    TTb = [Buf(carve(2048)) for _ in range(2)]
    wT = Buf(carve(2048))
    vnew = [Buf(carve(512)) for _ in range(2)]
    ff["gF2"] = TTb[0]; ff["egF"] = TTb[1]; ff["bgF"] = dTm; ff["kdF"] = PTx
    print("arena used (mixer)", arena_off[0], "of", ARENA)

    banks = [Buf(nc.alloc_psum_tensor(f"ps{i}", [128, T], F32)) for i in range(8)]
    bank_ctr = [0]
    stg_ctr = [0]

    def getbanks(n):
        r = []
        for _ in range(n):
            r.append(banks[bank_ctr[0] % 7])
            bank_ctr[0] += 1
        return r

    def getstg():
        b = stg[stg_ctr[0] % NSTG]
        stg_ctr[0] += 1
        return b

    def E(eng, fn, r=(), w=()):
        return P.emit(eng, fn, reads=[b.res for b in r], writes=[b.res for b in w])

    ld_sems = [P.newsem(f"ld{i}") for i in range(NSTG)]
    st_sems = [P.newsem(f"st{i}") for i in range(NSTG)]
    misc_sem = P.newsem("misc")

    def v3(buf, c=64):
        return buf.ap.rearrange("p (a b) -> p a b", b=c)

    def job(groups, evac):
        allbanks = []
        for (pieces, K, rhs_fn) in groups:
            nb_tot = sum(nc_ // 128 for (_, _, nc_) in pieces)
            gb = getbanks(nb_tot) if not P.planning else [None] * nb_tot
            allbanks.extend(gb)
            colsum = sum(nc_ for (_, _, nc_) in pieces)
            kb = min(K, SLOT // colsum)
            k0 = 0
            while k0 < K:
                kn = min(kb, K - k0)
                tiles = []
                off = 0
                offs = []
                for (mat, c0, ncols) in pieces:
                    src = mat[k0 * 128:(k0 + kn) * 128, c0:c0 + ncols].rearrange("(k p) n -> p k n", p=128)
                    tiles.append((src, off, kn, ncols))
                    offs.append(off)
                    off += kn * ncols
                base, wres = ws.get(tiles)
                if not P.planning:
                    bi = 0
                    for pi, (mat, c0, ncols) in enumerate(pieces):
                        for b in range(ncols // 128):
                            bank = gb[bi]
                            bi += 1

                            def pe_fn(e, base=base, o=offs[pi], ncols=ncols, b=b, kn=kn, k0=k0, K=K,
                                      bank=bank, rhs_fn=rhs_fn):
                                ins = None
                                for kk in range(kn):
                                    a = base + o + kk * ncols + b * 128
                                    ins = e.matmul(bank.ap[:, :], lhsT=ws.buf[:, a:a + 128], rhs=rhs_fn(k0 + kk),
                                                   start=(k0 + kk == 0), stop=(k0 + kk == K - 1))
                                return ins
                            rres = [wres] + list(rhs_fn.res)
                            if k0 == 0:
                                P.emit("pe", pe_fn, reads=rres, writes=[bank.res])
                            else:
                                tok = P.emit("pe", pe_fn, reads=rres)
                                bank.res.w = tok
                ws.done()
                k0 += kn
        if not P.planning:
            evac(allbanks)

    def rhs_of(buf):
        def f(k):
            return buf.ap[:, k, :]
        f.res = [buf.res]
        return f

    def tokslice(t):
        return slice(t * T, (t + 1) * T)

    def init_pass(t):
        xv = x[t * T:(t + 1) * T, :].rearrange("(tb p) d -> p tb d", p=128)
        for c in range(KC):
            si = stg_ctr[0] % NSTG
            s_in = getstg()
            P.emit("sync", lambda e, s_in=s_in, c=c: e.dma_start(
                out=s_in.ap.rearrange("p (tb d) -> p tb d", d=128), in_=xv[:, :, c * 128:(c + 1) * 128]),
                writes=[s_in.res], dmasem=ld_sems[si])
            (bk,) = getbanks(1)

            def pe_fn(e, s_in=s_in, bk=bk):
                ins = None
                for tb in range(4):
                    ins = e.transpose(bk.ap[:, tb * 128:(tb + 1) * 128], s_in.ap[:, tb * 128:(tb + 1) * 128], ident.ap[:, :])
                return ins
            E("pe", pe_fn, r=[s_in, ident], w=[bk])
            so = stg_ctr[0] % NSTG
            s_out = getstg()
            E("act", lambda e, s_out=s_out, bk=bk: e.activation(out=s_out.ap[:, :], in_=bk.ap[:, :], func=AF.Copy),
              r=[bk], w=[s_out])
            P.emit("sync", lambda e, s_out=s_out, c=c: e.dma_start(out=hT[c, :, tokslice(t)], in_=s_out.ap[:, :]),
                   reads=[s_out.res], writes=[hT_res[t][c]], dmasem=st_sems[so])

    def final_pass(t):
        ov = out[t * T:(t + 1) * T, :].rearrange("(tb p) d -> p tb d", p=128)
        for c in range(KC):
            si = stg_ctr[0] % NSTG
            s_in = getstg()
            P.emit("sync", lambda e, s_in=s_in, c=c: e.dma_start(out=s_in.ap[:, :], in_=hT[c, :, tokslice(t)]),
                   reads=[hT_res[t][c]], writes=[s_in.res], dmasem=ld_sems[si])
            (bk,) = getbanks(1)

            def pe_fn(e, s_in=s_in, bk=bk):
                ins = None
                for tb in range(4):
                    ins = e.transpose(bk.ap[:, tb * 128:(tb + 1) * 128], s_in.ap[:, tb * 128:(tb + 1) * 128], ident.ap[:, :])
                return ins
            E("pe", pe_fn, r=[s_in, ident], w=[bk])
            so = stg_ctr[0] % NSTG
            s_out = getstg()
            E("act", lambda e, s_out=s_out, bk=bk: e.activation(out=s_out.ap[:, :], in_=bk.ap[:, :], func=AF.Copy),
              r=[bk], w=[s_out])
            P.emit("sync", lambda e, s_out=s_out, c=c: e.dma_start(
                out=ov[:, :, c * 128:(c + 1) * 128], in_=s_out.ap.rearrange("p (tb d) -> p tb d", d=128)),
                reads=[s_out.res], dmasem=st_sems[so])

    def rstd_from_bank(bk, dst, scale):
        E("act", lambda e: e.activation(out=dst.ap[:, :], in_=bk.ap[:, :], func=AF.Sqrt, bias=epsc.ap[:, 0:1], scale=scale),
          r=[bk, epsc], w=[dst])
        E("dve", lambda e: e.reciprocal(out=dst.ap[:, :], in_=dst.ap[:, :]), r=[dst], w=[dst])

    def norm_pass(l, t, gname):
        g0 = l * SP_L + SP_G[gname]
        sbk = banks[7]
        for c in range(KC):
            si = stg_ctr[0] % NSTG
            hs = getstg()
            P.emit("sync", lambda e, hs=hs, c=c: e.dma_start(out=hs.ap[:, :], in_=hT[c, :, tokslice(t)]),
                   reads=[hT_res[t][c]], writes=[hs.res], dmasem=ld_sems[si])
            E("act", lambda e, hs=hs, c=c: e.activation(out=xn.ap[:, c, :], in_=hs.ap[:, :], func=AF.Copy,
                                                       scale=sp.ap[:, g0 + c:g0 + c + 1]),
              r=[hs, sp], w=[xn])
            sq = getstg()
            sqv = sq.ap[:, 0:T // 2].bitcast(BF16)
            E("act", lambda e, hs=hs, sqv=sqv: e.activation(out=sqv, in_=hs.ap[:, :], func=AF.Square), r=[hs], w=[sq])
            tokm = P.emit("pe", lambda e, sqv=sqv, c=c: e.matmul(sbk.ap[:, :], lhsT=ones_bf.ap[:, :], rhs=sqv,
                                                              start=(c == 0), stop=(c == KC - 1)),
                          reads=[sq.res, ones_bf.res], writes=[sbk.res] if c == 0 else [])
            if c > 0 and tokm is not None:
                sbk.res.w = tokm
        rs = rstd[0]
        rstd_from_bank(sbk, rs, 1.0 / D)
        for c in range(KC):
            E("dve", lambda e, c=c: e.tensor_tensor(out=xn.ap[:, c, :], in0=xn.ap[:, c, :], in1=rs.ap[:, :], op=ALU.mult),
              r=[xn, rs], w=[xn])

    class OutStats:
        def __init__(self, t):
            self.t = t
            self.sbk = banks[7]
            self.pending = []
            self.n = 0

        def flush(self):
            for (sq, sqv) in self.pending:
                c = self.n
                tokm = P.emit("pe", lambda e, sqv=sqv, c=c: e.matmul(self.sbk.ap[:, :], lhsT=ones_bf.ap[:, :], rhs=sqv,
                                                                  start=(c == 0), stop=(c == KC - 1)),
                              reads=[sq.res, ones_bf.res], writes=[self.sbk.res] if c == 0 else [])
                if c > 0:
                    self.sbk.res.w = tokm
                self.n += 1
            self.pending = []

        def evac(self, c, bk):
            so = stg_ctr[0] % NSTG
            fs = getstg()
            E("act", lambda e, fs=fs, bk=bk: e.activation(out=fs.ap[:, :], in_=bk.ap[:, :], func=AF.Copy), r=[bk], w=[fs])
            P.emit("sync", lambda e, fs=fs, c=c: e.dma_start(out=fT[c, :, tokslice(self.t)], in_=fs.ap[:, :]),
                   reads=[fs.res], writes=[fT_res[self.t][c]], dmasem=st_sems[so])
            sq = getstg()
            sqv = sq.ap[:, 0:T // 2].bitcast(BF16)
            E("dve", lambda e, sqv=sqv, fs=fs: e.tensor_tensor(out=sqv, in0=fs.ap[:, :], in1=fs.ap[:, :], op=ALU.mult),
              r=[fs], w=[sq])
            self.pending.append((sq, sqv))

    def update_pass(l, t, gname, coef, ost):
        ost.flush()
        g0 = l * SP_L + SP_G[gname]
        rs = rstd[1]
        rstd_from_bank(ost.sbk, rs, 1.0 / D)
        for c in range(KC):
            si = stg_ctr[0] % NSTG
            hs = getstg()
            P.emit("sync", lambda e, hs=hs, c=c: e.dma_start(out=hs.ap[:, :], in_=hT[c, :, tokslice(t)]),
                   reads=[hT_res[t][c]], writes=[hs.res], dmasem=ld_sems[si])
            sj = stg_ctr[0] % NSTG
            fs = getstg()
            P.emit("sync", lambda e, fs=fs, c=c: e.dma_start(out=fs.ap[:, :], in_=fT[c, :, tokslice(t)]),
                   reads=[fT_res[t][c]], writes=[fs.res], dmasem=ld_sems[sj])
            E("dve", lambda e, fs=fs, c=c: e.scalar_tensor_tensor(out=fs.ap[:, :], in0=fs.ap[:, :],
                                                                 scalar=sp.ap[:, g0 + c:g0 + c + 1], in1=rs.ap[:, :],
                                                                 op0=ALU.mult, op1=ALU.mult),
              r=[fs, sp, rs], w=[fs])
            E("dve", lambda e, fs=fs, hs=hs: e.scalar_tensor_tensor(out=hs.ap[:, :], in0=fs.ap[:, :], scalar=float(coef),
                                                                   in1=hs.ap[:, :], op0=ALU.mult, op1=ALU.add),
              r=[fs, hs], w=[hs])
            P.emit("sync", lambda e, hs=hs, c=c: e.dma_start(out=hT[c, :, tokslice(t)], in_=hs.ap[:, :]),
                   reads=[hs.res], writes=[hT_res[t][c]], dmasem=st_sems[si])

    def ffn(l, t, which):
        w_in_ = W[f"w_ffn{which}_in"][l]
        w_out_ = W[f"w_ffn{which}_out"][l]
        norm_pass(l, t, f"ffn{which}_pre_g")
        rx = rhs_of(xn)
        for fp in range(FC // 2):
            def evac(bks, fp=fp):
                for fi in range(2):
                    f = fp * 2 + fi
                    st_ = getstg()
                    E("act", lambda e, st_=st_, bk=bks[fi]: e.activation(out=st_.ap[:, :], in_=bk.ap[:, :], func=AF.Silu),
                      r=[bks[fi]], w=[st_])
                    E("dve", lambda e, st_=st_, bk=bks[2 + fi], f=f: e.tensor_tensor(out=hidden.ap[:, f, :], in0=bk.ap[:, :],
                                                                                     in1=st_.ap[:, :], op=ALU.mult),
                      r=[bks[2 + fi], st_], w=[hidden])
            job([([(w_in_, fp * 256, 256), (w_in_, DFF + fp * 256, 256)], KC, rx)], evac)
        ost = OutStats(t)
        rh = rhs_of(hidden)
        for og in range(8):
            def evac2(bks, og=og):
                ost.flush()
                for j in range(4):
                    ost.evac(og * 4 + j, bks[j])
            job([([(w_out_, og * 512, 512)], FC, rh)], evac2)
        update_pass(l, t, f"ffn{which}_post_g", 0.5, ost)

    def mixer_setup(l):
        o = l * SP_L
        for (b, c0) in ((mk_mti, CS_MTI), (mk_m01, CS_M01), (mk_mls, CS_MLS), (mk_i8, CS_I8)):
            P.emit("sync", lambda e, b=b, c0=c0: e.dma_start(out=b.ap[0:64, :], in_=cstd[0:64, c0:c0 + 512]),
                   writes=[b.res], dmasem=misc_sem)
        for h in range(NH):
            E("dve", lambda e, h=h: e.memset(Sst[h].ap[:, :], 0.0), w=[Sst[h]])
        E("dve", lambda e: e.memset(halo_q.ap[:, :, :], 0.0), w=[halo_q])
        E("dve", lambda e: e.memset(halo_s.ap[:, :, :], 0.0), w=[halo_s])
        E("act", lambda e: e.activation(out=sp.ap[0:16, o + SP_ALOG:o + SP_ALOG + 1], in_=sp.ap[0:16, o + SP_ALOG:o + SP_ALOG + 1],
                                        func=AF.Exp), r=[sp], w=[sp])
        E("dve", lambda e: e.tensor_scalar(out=sp.ap[0:16, o + SP_ALOG:o + SP_ALOG + 1], in0=sp.ap[0:16, o + SP_ALOG:o + SP_ALOG + 1],
                                           scalar1=-1.0, scalar2=None, op0=ALU.mult), r=[sp], w=[sp])
        P.fence()

    def conv_taps(rawb, dst, wcol0, width):
        E("dve", lambda e: e.tensor_scalar(out=dst.ap[:, :], in0=rawb.ap[:, 0:T], scalar1=sp.ap[:, wcol0:wcol0 + 1],
                                           scalar2=None, op0=ALU.mult), r=[rawb, sp], w=[dst])
        for j in range(1, width):
            E("dve", lambda e, j=j: e.scalar_tensor_tensor(out=dst.ap[:, :], in0=rawb.ap[:, j:j + T],
                                                          scalar=sp.ap[:, wcol0 + j:wcol0 + j + 1], in1=dst.ap[:, :],
                                                          op0=ALU.mult, op1=ALU.add), r=[rawb, sp, dst], w=[dst])

    raw_ctr = [0]

    def mixer(l, t):
        o = l * SP_L
        w_in_ = W["w_in"][l]
        norm_pass(l, t, "mix_pre_g")
        rx = rhs_of(xn)

        def evac_ab(bks):
            pass
        abk = getbanks(2) if not P.planning else [None, None]
        tiles = [(w_in_[:, C_A:C_A + 32].rearrange("(k p) n -> p k n", p=128), 0, KC, 32)]
        base, wres = ws.get(tiles)
        if not P.planning:
            for which in range(2):
                def pe_fn(e, which=which, base=base):
                    ins = None
                    for k in range(KC):
                        a = base + k * 32 + which * 16
                        ins = e.matmul(abk[which].ap[0:16, :], lhsT=ws.buf[:, a:a + 16], rhs=xn.ap[:, k, :],
                                       start=(k == 0), stop=(k == KC - 1))
                    return ins
                P.emit("pe", pe_fn, reads=[wres, xn.res], writes=[abk[which].res])
        ws.done()
        if not P.planning:
            gF, gF2, btF, egF, bgF, kdF = (ff[n] for n in ("gF", "gF2", "btF", "egF", "bgF", "kdF"))
            R16 = slice(0, 16)
            E("act", lambda e: e.activation(out=gF.ap[R16, :], in_=abk[0].ap[R16, :], func=AF.Exp,
                                            bias=sp.ap[R16, o + SP_DTB:o + SP_DTB + 1]), r=[abk[0], sp], w=[gF])
            E("act", lambda e: e.activation(out=gF.ap[R16, :], in_=gF.ap[R16, :], func=AF.Ln, bias=onec.ap[R16, 0:1]),
              r=[gF, onec], w=[gF])
            E("dve", lambda e: e.tensor_scalar(out=gF.ap[R16, :], in0=gF.ap[R16, :], scalar1=sp.ap[R16, o + SP_ALOG:o + SP_ALOG + 1],
                                               scalar2=None, op0=ALU.mult), r=[gF, sp], w=[gF])
            E("act", lambda e: e.activation(out=btF.ap[R16, :], in_=abk[1].ap[R16, :], func=AF.Sigmoid), r=[abk[1]], w=[btF])
            src, dst = gF, gF2
            s = 1
            while s < 64:
                sv, dv = v3(src), v3(dst)
                E("dve", lambda e, sv=sv, dv=dv, s=s: e.tensor_copy(out=dv[R16, :, 0:s], in_=sv[R16, :, 0:s]), r=[src], w=[dst])
                E("dve", lambda e, sv=sv, dv=dv, s=s: e.tensor_tensor(out=dv[R16, :, s:64], in0=sv[R16, :, s:64],
                                                                     in1=sv[R16, :, 0:64 - s], op=ALU.add), r=[src], w=[dst])
                src, dst = dst, src
                s *= 2
            gcF = src
            E("act", lambda e: e.activation(out=egF.ap[R16, :], in_=gcF.ap[R16, :], func=AF.Exp), r=[gcF], w=[egF])
            E("dve", lambda e: e.tensor_tensor(out=bgF.ap[R16, :], in0=btF.ap[R16, :], in1=egF.ap[R16, :], op=ALU.mult),
              r=[btF, egF], w=[bgF])
            E("dve", lambda e: e.tensor_tensor(out=v3(kdF)[R16, :, :], in0=v3(gcF)[R16, :, 63:64].to_broadcast([16, 8, 64]),
                                               in1=v3(gcF)[R16, :, :], op=ALU.subtract), r=[gcF], w=[kdF])
            E("act", lambda e: e.activation(out=kdF.ap[R16, :], in_=kdF.ap[R16, :], func=AF.Exp), r=[kdF], w=[kdF])
            for (srcb, name) in ((gcF, "gcP"), (btF, "btP"), (bgF, "bgP"), (kdF, "kdP")):
                (bk,) = getbanks(1)

                def pe_fn(e, srcb=srcb, bk=bk):
                    ins = None
                    for c in range(8):
                        ins = e.transpose(bk.ap[0:64, c * 16:(c + 1) * 16], srcb.ap[R16, c * 64:(c + 1) * 64], ident.ap[0:16, 0:16])
                    return ins
                E("pe", pe_fn, r=[srcb, ident], w=[bk])
                E("act", lambda e, bk=bk, name=name: e.activation(out=pf[name].ap.rearrange("p c h -> p (c h)")[0:64, :],
                                                                 in_=bk.ap[0:64, 0:128], func=AF.Copy), r=[bk], w=[pf[name]])
        else:
            gcF = btF = None

        for h in range(NH):
            def evac_head(bks, h=h):
                R64 = slice(0, 64)
                outs = (qT, kT, vT)
                for xi in range(3):
                    rb = raw[raw_ctr[0] % 2]
                    raw_ctr[0] += 1
                    hc = xi * 16 + h
                    E("act", lambda e, rb=rb, bk=bks[xi]: e.activation(out=rb.ap[:, 3:3 + T], in_=bk.ap[:, :], func=AF.Copy),
                      r=[bks[xi]], w=[rb])
                    E("dve", lambda e, rb=rb, hc=hc: e.tensor_copy(out=rb.ap[:, 0:3], in_=halo_q.ap[:, hc, :]), r=[halo_q], w=[rb])
                    E("dve", lambda e, rb=rb, hc=hc: e.tensor_copy(out=halo_q.ap[:, hc, :], in_=rb.ap[:, T:T + 3]), r=[rb], w=[halo_q])
                    conv_taps(rb, outs[xi], o + SP_DNW + hc * 4, 4)
                    E("act", lambda e, d=outs[xi]: e.activation(out=d.ap[:, :], in_=d.ap[:, :], func=AF.Silu), r=[outs[xi]], w=[outs[xi]])
                E("act", lambda e: e.activation(out=zs.ap[:, :], in_=bks[3].ap[:, :], func=AF.Silu), r=[bks[3]], w=[zs])
                for (d, sc) in ((qT, 128 ** -0.5), (kT, 1.0)):
                    sq = getstg()
                    E("act", lambda e, d=d, sq=sq: e.activation(out=sq.ap[:, :], in_=d.ap[:, :], func=AF.Square), r=[d], w=[sq])
                    (bk,) = getbanks(1)
                    E("pe", lambda e, sq=sq, bk=bk: e.matmul(bk.ap[:, :], lhsT=ones_f.ap[:, :], rhs=sq.ap[:, :], start=True, stop=True),
                      r=[sq, ones_f], w=[bk])
                    rstd_from_bank(bk, sq, 1.0)
                    E("dve", lambda e, d=d, sq=sq, sc=sc: e.scalar_tensor_tensor(out=d.ap[:, :], in0=d.ap[:, :], scalar=float(sc),
                                                                                in1=sq.ap[:, :], op0=ALU.mult, op1=ALU.mult),
                      r=[d, sq], w=[d])
                E("dve", lambda e: e.tensor_copy(out=selb.ap[0:16, :], in_=ident.ap[0:16, h:h + 1].to_broadcast([16, 128])),
                  r=[ident], w=[selb])
                for (srcb, dstb, fn) in ((gcF, gcB, AF.Copy), (gcF, egB, AF.Exp), (btF, btB, AF.Copy)):
                    (bk,) = getbanks(1)
                    E("pe", lambda e, srcb=srcb, bk=bk: e.matmul(bk.ap[:, :], lhsT=selb.ap[0:16, :], rhs=srcb.ap[0:16, :],
                                                                start=True, stop=True), r=[selb, srcb], w=[bk])
                    E("act", lambda e, bk=bk, dstb=dstb, fn=fn: e.activation(out=dstb.ap[:, :], in_=bk.ap[:, :], func=fn),
                      r=[bk], w=[dstb])
                for (srcb, dstb) in ((kT, kP), (vT, vP)):
                    bk2 = getbanks(2)
                    for half in range(2):
                        def pe_fn(e, srcb=srcb, bk=bk2[half], half=half):
                            ins = None
                            for cc in range(4):
                                c = half * 4 + cc
                                ins = e.transpose(bk.ap[0:64, cc * 128:(cc + 1) * 128], srcb.ap[:, c * 64:(c + 1) * 64], ident.ap[:, :])
                            return ins
                        E("pe", pe_fn, r=[srcb, ident], w=[bk2[half]])
                        E("act", lambda e, bk=bk2[half], dstb=dstb, half=half: e.activation(
                            out=dstb.ap[R64, half * 4:(half + 1) * 4, :].rearrange("p c d -> p (c d)"), in_=bk.ap[R64, :], func=AF.Copy),
                          r=[bk2[half]], w=[dstb])
                def bc(pn):
                    return pf[pn].ap[R64, :, h:h + 1].to_broadcast([64, 8, 128])
                E("dve", lambda e: e.tensor_tensor(out=kbgP.ap[R64, :, :], in0=kP.ap[R64, :, :], in1=bc("bgP"), op=ALU.mult),
                  r=[kP, pf["bgP"]], w=[kbgP])
                E("dve", lambda e: e.tensor_tensor(out=kP.ap[R64, :, :], in0=kP.ap[R64, :, :], in1=bc("kdP"), op=ALU.mult),
                  r=[kP, pf["kdP"]], w=[kP])
                E("dve", lambda e: e.tensor_tensor(out=vP.ap[R64, :, :], in0=vP.ap[R64, :, :], in1=bc("btP"), op=ALU.mult),
                  r=[vP, pf["btP"]], w=[vP])
                bkk, bqk = getbanks(2)

                def pe_kk(e):
                    ins = None
                    for c in range(8):
                        cs = slice(c * 64, (c + 1) * 64)
                        ins = e.matmul(bkk.ap[R64, cs], lhsT=kT.ap[:, cs], rhs=kT.ap[:, cs], start=True, stop=True)
                    return ins

                def pe_qk(e):
                    ins = None
                    for c in range(8):
                        cs = slice(c * 64, (c + 1) * 64)
                        ins = e.matmul(bqk.ap[R64, cs], lhsT=kT.ap[:, cs], rhs=qT.ap[:, cs], start=True, stop=True)
                    return ins
                E("pe", pe_kk, r=[kT], w=[bkk])
                E("pe", pe_qk, r=[kT, qT], w=[bqk])
                gcPb = pf["gcP"].ap[R64, :, h:h + 1].to_broadcast([64, 8, 64])
                btPb = pf["btP"].ap[R64, :, h:h + 1].to_broadcast([64, 8, 64])
                E("dve", lambda e: e.tensor_tensor(out=v3(dTm)[R64, :, :], in0=v3(gcB)[R64, :, :], in1=gcPb, op=ALU.subtract),
                  r=[gcB, pf["gcP"]], w=[dTm])
                E("dve", lambda e: e.tensor_tensor(out=dTm.ap[R64, :], in0=dTm.ap[R64, :], in1=mk_mti.ap[R64, :], op=ALU.add),
                  r=[dTm, mk_mti], w=[dTm])
                E("act", lambda e: e.activation(out=dTm.ap[R64, :], in_=dTm.ap[R64, :], func=AF.Exp), r=[dTm], w=[dTm])
                E("dve", lambda e: e.tensor_tensor(out=attnT.ap[R64, :], in0=bqk.ap[R64, :], in1=dTm.ap[R64, :], op=ALU.mult),
                  r=[bqk, dTm], w=[attnT])
                E("dve", lambda e: e.tensor_tensor(out=lT.ap[R64, :], in0=bkk.ap[R64, :], in1=dTm.ap[R64, :], op=ALU.mult),
                  r=[bkk, dTm], w=[lT])
                E("dve", lambda e: e.tensor_tensor(out=lT.ap[R64, :], in0=lT.ap[R64, :], in1=mk_m01.ap[R64, :], op=ALU.mult),
                  r=[lT, mk_m01], w=[lT])
                E("dve", lambda e: e.tensor_tensor(out=lT.ap[R64, :], in0=lT.ap[R64, :], in1=btB.ap[R64, :], op=ALU.mult),
                  r=[lT, btB], w=[lT])
                E("dve", lambda e: e.tensor_tensor(out=v3(lM)[R64, :, :], in0=gcPb, in1=v3(gcB)[R64, :, :], op=ALU.subtract),
                  r=[gcB, pf["gcP"]], w=[lM])
                E("dve", lambda e: e.tensor_tensor(out=lM.ap[R64, :], in0=lM.ap[R64, :], in1=mk_mls.ap[R64, :], op=ALU.add),
                  r=[lM, mk_mls], w=[lM])
                E("act", lambda e: e.activation(out=lM.ap[R64, :], in_=lM.ap[R64, :], func=AF.Exp), r=[lM], w=[lM])
                E("dve", lambda e: e.tensor_tensor(out=v3(lM)[R64, :, :], in0=v3(lM)[R64, :, :], in1=btPb, op=ALU.mult),
                  r=[lM, pf["btP"]], w=[lM])
                E("dve", lambda e: e.tensor_tensor(out=lM.ap[R64, :], in0=lM.ap[R64, :], in1=bkk.ap[R64, :], op=ALU.mult),
                  r=[lM, bkk], w=[lM])
                E("dve", lambda e: e.tensor_tensor(out=TTb[0].ap[R64, :], in0=mk_i8.ap[R64, :], in1=lT.ap[R64, :], op=ALU.subtract),
                  r=[mk_i8, lT], w=[TTb[0]])
                Pc, PTc, TTc = lM, lT, TTb[0]
                for s in range(5):
                    Pn = (dTm, lM)[s % 2]
                    PTn = (PTx, lT)[s % 2]
                    TTn = TTb[(s + 1) % 2]
                    ba, bb_, bc_ = getbanks(3)

                    def pe_sq(e, Pc=Pc, PTc=PTc, bk=ba):
                        ins = None
                        for c in range(8):
                            cs = slice(c * 64, (c + 1) * 64)
                            ins = e.matmul(bk.ap[R64, cs], lhsT=PTc.ap[R64, cs], rhs=Pc.ap[R64, cs], start=True, stop=True)
                        return ins
                    E("pe", pe_sq, r=[Pc, PTc], w=[ba])
                    E("act", lambda e, Pn=Pn, bk=ba: e.activation(out=Pn.ap[R64, :], in_=bk.ap[R64, :], func=AF.Copy), r=[ba], w=[Pn])
                    if s < 4:
                        def pe_sqT(e, Pc=Pc, PTc=PTc, bk=bb_):
                            ins = None
                            for c in range(8):
                                cs = slice(c * 64, (c + 1) * 64)
                                ins = e.matmul(bk.ap[R64, cs], lhsT=Pc.ap[R64, cs], rhs=PTc.ap[R64, cs], start=True, stop=True)
                            return ins
                        E("pe", pe_sqT, r=[Pc, PTc], w=[bb_])
                        E("dve", lambda e, PTn=PTn, bk=bb_: e.tensor_copy(out=PTn.ap[R64, :], in_=bk.ap[R64, :]), r=[bb_], w=[PTn])

                    def pe_tt(e, Pn=Pn, TTc=TTc, bk=bc_):
                        ins = None
                        for c in range(8):
                            cs = slice(c * 64, (c + 1) * 64)
                            ins = e.matmul(bk.ap[R64, cs], lhsT=Pn.ap[R64, cs], rhs=TTc.ap[R64, cs], start=True, stop=True)
                        return ins
                    E("pe", pe_tt, r=[Pn, TTc], w=[bc_])
                    E("dve", lambda e, TTn=TTn, TTc=TTc, bk=bc_: e.tensor_tensor(out=TTn.ap[R64, :], in0=bk.ap[R64, :], in1=TTc.ap[R64, :],
                                                                               op=ALU.add), r=[bc_, TTc], w=[TTn])
                    Pc, PTc, TTc = Pn, PTn, TTn
                TT = TTc
                (bw,) = getbanks(1)

                def pe_w(e):
                    ins = None
                    for c in range(8):
                        cs = slice(c * 64, (c + 1) * 64)
                        ins = e.matmul(bw.ap[:, cs], lhsT=kbgP.ap[R64, c, :], rhs=TT.ap[R64, cs], start=True, stop=True)
                    return ins
                E("pe", pe_w, r=[kbgP, TT], w=[bw])
                E("act", lambda e: e.activation(out=wT.ap[:, :], in_=bw.ap[:, :], func=AF.Copy), r=[bw], w=[wT])
                bu = getbanks(2)
                for half in range(2):
                    def pe_u(e, half=half):
                        ins = None
                        for cc in range(4):
                            c = half * 4 + cc
                            ins = e.matmul(bu[half].ap[R64, cc * 128:(cc + 1) * 128], lhsT=TT.ap[R64, c * 64:(c + 1) * 64],
                                           rhs=vP.ap[R64, c, :], start=True, stop=True)
                        return ins
                    E("pe", pe_u, r=[TT, vP], w=[bu[half]])
                    E("act", lambda e, half=half: e.activation(out=uP.ap[R64, half * 4:(half + 1) * 4, :].rearrange("p c d -> p (c d)"),
                                                              in_=bu[half].ap[R64, :], func=AF.Copy), r=[bu[half]], w=[uP])
                E("dve", lambda e: e.tensor_tensor(out=qT.ap[:, :], in0=qT.ap[:, :], in1=egB.ap[:, :], op=ALU.mult), r=[qT, egB], w=[qT])
                oT = vT
                seq = [Sst[h], Stmp[0], Stmp[1]]
                cur = Sst[h]
                for c in range(8):
                    cs = slice(c * 64, (c + 1) * 64)
                    nxt = Sst[h] if c == 7 else Stmp[c % 2]
                    vn = vnew[c % 2]
                    b1, b2, b3 = getbanks(3)
                    E("pe", lambda e, b1=b1, cs=cs, cur=cur: e.matmul(b1.ap[R64, 0:128], lhsT=wT.ap[:, cs], rhs=cur.ap[:, :], start=True, stop=True),
                      r=[wT, cur], w=[b1])
                    E("dve", lambda e, b1=b1, vn=vn, c=c: e.tensor_tensor(out=vn.ap[R64, :], in0=uP.ap[R64, c, :], in1=b1.ap[R64, 0:128],
                                                                         op=ALU.subtract), r=[uP, b1], w=[vn])

                    def pe_o(e, b2=b2, cs=cs, cur=cur, vn=vn):
                        e.matmul(b2.ap[:, 0:64], lhsT=cur.ap[:, :], rhs=qT.ap[:, cs], start=True, stop=False)
                        return e.matmul(b2.ap[:, 0:64], lhsT=vn.ap[R64, :], rhs=attnT.ap[R64, cs], start=False, stop=True)
                    E("pe", pe_o, r=[cur, qT, vn, attnT], w=[b2])
                    E("act", lambda e, b2=b2, cs=cs: e.activation(out=oT.ap[:, cs], in_=b2.ap[:, 0:64], func=AF.Copy), r=[b2], w=[oT])
                    E("pe", lambda e, b3=b3, c=c, vn=vn: e.matmul(b3.ap[:, 0:128], lhsT=kP.ap[R64, c, :], rhs=vn.ap[R64, :], start=True, stop=True),
                      r=[kP, vn], w=[b3])
                    E("dve", lambda e, b3=b3, cur=cur, nxt=nxt, c=c: e.scalar_tensor_tensor(
                        out=nxt.ap[:, :], in0=cur.ap[:, :], scalar=egB.ap[:, c * 64 + 63:c * 64 + 64], in1=b3.ap[:, 0:128],
                        op0=ALU.mult, op1=ALU.add), r=[cur, egB, b3], w=[nxt])
                    cur = nxt
                sq = getstg()
                E("act", lambda e, sq=sq: e.activation(out=sq.ap[:, :], in_=oT.ap[:, :], func=AF.Square), r=[oT], w=[sq])
                (bk,) = getbanks(1)
                E("pe", lambda e, sq=sq, bk=bk: e.matmul(bk.ap[:, :], lhsT=ones_f.ap[:, :], rhs=sq.ap[:, :], start=True, stop=True),
                  r=[sq, ones_f], w=[bk])
                rstd_from_bank(bk, sq, 1.0 / 128)
                E("dve", lambda e, sq=sq: e.scalar_tensor_tensor(out=sq.ap[:, :], in0=oT.ap[:, :], scalar=sp.ap[:, o + SP_NG:o + SP_NG + 1],
                                                                in1=sq.ap[:, :], op0=ALU.mult, op1=ALU.mult), r=[oT, sp, sq], w=[sq])
                E("dve", lambda e, sq=sq: e.tensor_tensor(out=ogT.ap[:, h, :], in0=sq.ap[:, :], in1=zs.ap[:, :], op=ALU.mult),
                  r=[sq, zs], w=[ogT])
            job([([(w_in_, C_Q + h * 128, 128), (w_in_, C_K + h * 128, 128), (w_in_, C_V + h * 128, 128),
                   (w_in_, C_Z + h * 128, 128)], KC, rx)], evac_head)

        P.fence()
        for c in range(16):
            def evac_sc(bks, c=c):
                tmp = getstg()
                rb = raw[raw_ctr[0] % 2]
                raw_ctr[0] += 1
                E("act", lambda e, tmp=tmp: e.activation(out=tmp.ap[:, :], in_=bks[1].ap[:, :], func=AF.Copy), r=[bks[1]], w=[tmp])
                E("dve", lambda e, tmp=tmp, rb=rb: e.tensor_tensor(out=rb.ap[:, 2:2 + T], in0=tmp.ap[:, :], in1=bks[2].ap[:, :], op=ALU.mult),
                  r=[tmp, bks[2]], w=[rb])
                E("dve", lambda e, rb=rb: e.tensor_copy(out=rb.ap[:, 0:2], in_=halo_s.ap[:, c, :]), r=[halo_s], w=[rb])
                E("dve", lambda e, rb=rb: e.tensor_copy(out=halo_s.ap[:, c, :], in_=rb.ap[:, T:T + 2]), r=[rb], w=[halo_s])
                conv_taps(rb, tmp, o + SP_SCW + c * 3, 3)
                E("dve", lambda e, tmp=tmp: e.tensor_tensor(out=scT.ap[:, c, :], in0=tmp.ap[:, :], in1=bks[0].ap[:, :], op=ALU.mult),
                  r=[tmp, bks[0]], w=[scT])
            job([([(w_in_, C_SB + c * 128, 128), (w_in_, C_SC + c * 128, 128), (w_in_, C_SX + c * 128, 128)], KC, rx)], evac_sc)

        rog, rsc = rhs_of(ogT), rhs_of(scT)
        for c in range(KC):
            def evac_m(bks, c=c):
                t1, t2 = getstg(), getstg()
                E("act", lambda e, t1=t1: e.activation(out=t1.ap[:, :], in_=bks[0].ap[:, :], func=AF.Sigmoid), r=[bks[0]], w=[t1])
                E("dve", lambda e, t1=t1: e.tensor_tensor(out=t1.ap[:, :], in0=t1.ap[:, :], in1=bks[2].ap[:, :], op=ALU.mult),
                  r=[t1, bks[2]], w=[t1])
                E("act", lambda e, t2=t2: e.activation(out=t2.ap[:, :], in_=bks[1].ap[:, :], func=AF.Sigmoid), r=[bks[1]], w=[t2])
                E("dve", lambda e, t2=t2: e.tensor_tensor(out=t2.ap[:, :], in0=t2.ap[:, :], in1=bks[3].ap[:, :], op=ALU.mult),
                  r=[t2, bks[3]], w=[t2])
                E("dve", lambda e, t1=t1, t2=t2: e.tensor_tensor(out=mT.ap[:, c, :], in0=t1.ap[:, :], in1=t2.ap[:, :], op=ALU.add),
                  r=[t1, t2], w=[mT])
            job([([(w_in_, C_GA + c * 128, 128), (w_in_, C_GB + c * 128, 128)], KC, rx),
                 ([(W["w_dn_out"][l], c * 128, 128)], 16, rog),
                 ([(W["w_sc_out"][l], c * 128, 128)], 16, rsc)], evac_m)
        if dbg2 is not None and not P.planning and l == 0 and t == 0:
            P.emit("sync", lambda e: e.dma_start(out=dbg2[:, 0:16 * T], in_=ogT.ap.rearrange("p f t -> p (f t)")), reads=[ogT.res], dmasem=dbg_sem)
            P.emit("sync", lambda e: e.dma_start(out=dbg2[:, 16 * T:32 * T], in_=scT.ap.rearrange("p f t -> p (f t)")), reads=[scT.res], dmasem=dbg_sem)
            P.emit("sync", lambda e: e.dma_start(out=dbg2[:, 32 * T:64 * T], in_=mT.ap.rearrange("p f t -> p (f t)")), reads=[mT.res], dmasem=dbg_sem)
        ost = OutStats(t)
        rm = rhs_of(mT)
        for og in range(8):
            def evac2(bks, og=og):
                ost.flush()
                for j in range(4):
                    ost.evac(og * 4 + j, bks[j])
            job([([(W["w_o"][l], og * 512, 512)], KC, rm)], evac2)
        update_pass(l, t, "mix_post_g", 1.0, ost)
        P.fence()

    dbg_sem = P.newsem("dbg")
    dbg_ctr = [0]

    def dump():
        if P.planning or dbgd is None:
            return
        i = dbg_ctr[0]
        dbg_ctr[0] += 1
        for c in range(KC):
            P.emit("sync", lambda e, c=c, i=i: e.dma_start(out=dbgd[i, c], in_=hT[c, :, 0:T]),
                   reads=[hT_res[0][c]], dmasem=dbg_sem)

    def program():
        if not P.planning:
            P.emit("sync", lambda e: e.dma_start(out=sp.ap[:, :], in_=spd), writes=[sp.res], dmasem=misc_sem)
            P.emit("sync", lambda e: e.dma_start(out=ident.ap[:, :], in_=cstd[:, CS_ID:CS_ID + 128]), writes=[ident.res], dmasem=misc_sem)
            E("dve", lambda e: e.memset(ones_bf.ap[:, :], 1.0), w=[ones_bf])
            E("dve", lambda e: e.memset(ones_f.ap[:, :], 1.0), w=[ones_f])
            E("dve", lambda e: e.memset(epsc.ap[:, :], EPS), w=[epsc])
            E("dve", lambda e: e.memset(onec.ap[:, :], 1.0), w=[onec])
            P.fence()
            for t in range(NT):
                init_pass(t)
        for l in range(L):
            for t in range(NT):
                ffn(l, t, 1)
            dump()
            P.fence()
            if not P.planning:
                mixer_setup(l)
            for t in range(NT):
                mixer(l, t)
            dump()
            P.fence()
            for t in range(NT):
                ffn(l, t, 2)
            dump()
        if not P.planning:
            for t in range(NT):
                final_pass(t)
            P.final_wait("sync")

    P.planning = True
    program()
    P.planning = False
    bank_ctr[0] = 0
    stg_ctr[0] = 0
    raw_ctr[0] = 0
    program()
    assert ws.consumed == len(ws.plan), (ws.consumed, len(ws.plan))
    P.run_block()
    print("ops", {k: len(v) for k, v in P.q.items()}, "sems", P.nsem, "wtiles", len(ws.plan))
    return nc


def _pack_small(inputs):
    L = 2
    sp = np.zeros((128, L * SP_L), np.float32)
    for l in range(L):
        o = l * SP_L
        for n, c0 in SP_G.items():
            sp[:, o + c0:o + c0 + 32] = np.asarray(inputs[n][l], np.float32).reshape(32, 128).T
        sp[:, o + SP_DNW:o + SP_DNW + 192] = np.asarray(inputs["dn_conv_w"][l], np.float32).reshape(48, 128, 4).transpose(1, 0, 2).reshape(128, 192)
        sp[:, o + SP_SCW:o + SP_SCW + 48] = np.asarray(inputs["sc_conv_w"][l], np.float32).reshape(16, 128, 3).transpose(1, 0, 2).reshape(128, 48)
        sp[:, o + SP_NG] = np.asarray(inputs["dn_norm_g"][l], np.float32)
        sp[0:16, o + SP_ALOG] = np.asarray(inputs["dn_a_log"][l], np.float32)
        sp[0:16, o + SP_DTB] = np.asarray(inputs["dn_dt_bias"][l], np.float32)
    return sp


def _consts():
    c = np.zeros((128, CS_N), np.float32)
    c[:, CS_ID:CS_ID + 128] = np.eye(128, dtype=np.float32)
    p = np.arange(64)[:, None]
    f = np.arange(64)[None, :]
    mti = np.where(f >= p, 0.0, NEG).astype(np.float32)
    m01 = np.where(f > p, 1.0, 0.0).astype(np.float32)
    mls = np.where(p > f, 0.0, NEG).astype(np.float32)
    i8 = np.eye(64, dtype=np.float32)
    c[0:64, CS_MTI:CS_MTI + 512] = np.tile(mti, (1, 8))
    c[0:64, CS_M01:CS_M01 + 512] = np.tile(m01, (1, 8))
    c[0:64, CS_MLS:CS_MLS + 512] = np.tile(mls, (1, 8))
    c[0:64, CS_I8:CS_I8 + 512] = np.tile(i8, (1, 8))
    return c


_NT = 4
_DBG = False
_LAST = {}


def kernel(**inputs):
    x = np.asarray(inputs["x"], np.float32)
    B, S, _ = x.shape
    NT = _NT
    nc = build_program(NT, _DBG)
    sp = _pack_small(inputs)
    cst = _consts()
    wnames = ["w_ffn1_in", "w_ffn1_out", "w_in", "w_dn_out", "w_sc_out", "w_o", "w_ffn2_in", "w_ffn2_out"]
    shared = {n: np.ascontiguousarray(np.asarray(inputs[n], np.float32)) for n in wnames}
    in_maps = []
    for b in range(NCORES):
        m = dict(shared)
        m["x"] = np.ascontiguousarray(x[b, :NT * T])
        m["sp"] = sp
        m["cst"] = cst
        in_maps.append(m)
    res = run_bass_kernel_spmd(nc, in_maps, core_ids=list(range(NCORES)))
    out = np.zeros((B, S, D), np.float32)
    for b in range(NCORES):
        out[b, :NT * T] = np.asarray(res.results[b]["out"], np.float32)
    if _DBG:
        _LAST["dbg"] = np.asarray(res.results[0]["dbg"], np.float32)
        _LAST["dbg2"] = np.asarray(res.results[0]["dbg2"]).astype(np.float32)
    return out
```

```python
import numpy as np
import concourse.bass as bass
import concourse.mybir as mybir
from concourse.bass_utils import run_bass_kernel_spmd

F32 = mybir.dt.float32
BF16 = mybir.dt.bfloat16
AF = mybir.ActivationFunctionType
ALU = mybir.AluOpType

NCORES = 4
D = 4096
KC = 32
DFF = 11008
FC = 86
T = 512
NH = 16
EPS = 1e-6
NEG = -30000.0
ENGS = ("sync", "act", "dve", "pe", "pool")
SEM_ROTATE = 30000
SLOT = 8192
NSLOT = 3

C_Q, C_K, C_V = 0, 2048, 4096
C_A, C_B = 6144, 6160
C_Z = 6176
C_SB, C_SC, C_SX = 8224, 10272, 12320
C_GA, C_GB = 14368, 18464

SP_G = {"ffn1_pre_g": 0, "ffn1_post_g": 32, "mix_pre_g": 64, "mix_post_g": 96,
        "ffn2_pre_g": 128, "ffn2_post_g": 160}
SP_DNW = 192
SP_SCW = 384
SP_NG = 432
SP_ALOG = 433
SP_DTB = 434
SP_L = 435
CS_ID = 0
CS_MTI = 128
CS_M01 = 640
CS_MLS = 1152
CS_I8 = 1664
CS_N = 2176


class Sem:
    def __init__(self, nc, name):
        self.h = nc.alloc_semaphore(name=name)
        self.val = 0


class Res:
    __slots__ = ("w", "r")

    def __init__(self):
        self.w = None
        self.r = []


class Prog:
    def __init__(self, nc):
        self.nc = nc
        self.q = {e: [] for e in ENGS}
        self.esem = {}
        self.esem_ids = {}
        self.nsem = 0
        self.planning = False
        self.fence_toks = []
        self.last = {}
        self.dmasems = []

    def newsem(self, name):
        self.nsem += 1
        s = Sem(self.nc, f"{name}_{self.nsem}")
        return s

    def _engine_sig(self, eng):
        s = self.esem.get(eng)
        if s is None or s.val >= SEM_ROTATE:
            s = self.newsem("e" + eng)
            self.esem[eng] = s
            self.esem_ids.setdefault(eng, set()).add(id(s))
        s.val += 1
        return (s, s.val)

    def emit(self, eng, fn, reads=(), writes=(), dmasem=None):
        if self.planning:
            return None
        waits = list(self.fence_toks)
        rawset = set((id(s), v) for (s, v) in self.fence_toks)
        for r in reads:
            if r.w is not None:
                waits.append(r.w)
                rawset.add((id(r.w[0]), r.w[1]))
        for w in writes:
            if w.w is not None:
                waits.append(w.w)
            waits.extend(w.r)
        if dmasem is not None:
            dmasem.val += 16
            tok = (dmasem, dmasem.val)
            inc = 16
        else:
            tok = self._engine_sig(eng)
            inc = 1
        best = {}
        mine = self.esem_ids.setdefault(eng, set())
        for (s, v) in waits:
            if id(s) in mine and (id(s), v) not in rawset:
                continue
            if id(s) not in best or best[id(s)][1] < v:
                best[id(s)] = (s, v)
        self.q[eng].append((fn, list(best.values()), tok, inc))
        self.last[id(tok[0])] = tok
        for r in reads:
            r.r.append(tok)
            if len(r.r) > 48:
                r.r = r.r[-48:]
        for w in writes:
            w.w = tok
            w.r = []
        return tok

    def fence(self):
        if self.planning:
            return
        self.fence_toks = list(self.last.values())

    def final_wait(self, engname):
        self.q[engname].append((None, list(self.last.values()), None, 0))

    def run_block(self):
        nc = self.nc
        P = self
        with nc.Block() as block:
            def mk(engname):
                def f(eng):
                    waited = {}
                    for (fn, waits, tok, inc) in P.q[engname]:
                        for (s, v) in waits:
                            if waited.get(id(s), 0) >= v:
                                continue
                            eng.wait_ge(s.h, v)
                            waited[id(s)] = v
                        if fn is None:
                            continue
                        ins = fn(eng)
                        ins.then_inc(tok[0].h, inc)
                return f
            block.sync(mk("sync"))
            block.scalar(mk("act"))
            block.vector(mk("dve"))
            block.tensor(mk("pe"))
            block.gpsimd(mk("pool"))


class WeightStream:
    def __init__(self, prog):
        self.p = prog
        self.buf = prog.nc.alloc_sbuf_tensor("wring", [128, NSLOT * SLOT], BF16)
        self.sems = [prog.newsem(f"w{i}") for i in range(NSLOT)]
        self.res = [Res() for _ in range(NSLOT)]
        self.plan = []
        self.issued = 0
        self.consumed = 0

    def _issue(self, idx):
        s = idx % NSLOT
        base = s * SLOT
        first = True
        tok = None
        for (src, off, kn, cols) in self.plan[idx]:
            dst = self.buf[:, base + off: base + off + kn * cols].rearrange("p (k c) -> p k c", c=cols)
            tok = self.p.emit("pool", (lambda eng, d=dst, sr=src: eng.dma_start(out=d, in_=sr)),
                              writes=[self.res[s]] if first else [], dmasem=self.sems[s])
            first = False
        self.res[s].w = tok
        self.res[s].r = []

    def prefetch(self):
        while self.issued < len(self.plan) and self.issued < self.consumed + NSLOT:
            self._issue(self.issued)
            self.issued += 1

    def get(self, pieces):
        if self.p.planning:
            self.plan.append(pieces)
            return 0, None
        idx = self.consumed
        self.prefetch()
        s = idx % NSLOT
        return s * SLOT, self.res[s]

    def done(self):
        if self.p.planning:
            return
        self.consumed += 1
        self.prefetch()


class Buf:
    def __init__(self, ap):
        self.ap = ap
        self.res = Res()


def build_program(NT, dbg=False):
    nc = bass.Bass("TRN2", target_bir_lowering=False)
    NTOK = NT * T
    L = 2

    def din(name, shape):
        return nc.dram_tensor(name, list(shape), F32, kind="ExternalInput").ap()

    x = din("x", [NTOK, D])
    W = {
        "w_ffn1_in": din("w_ffn1_in", [L, D, 2 * DFF]), "w_ffn1_out": din("w_ffn1_out", [L, DFF, D]),
        "w_in": din("w_in", [L, D, 22560]), "w_dn_out": din("w_dn_out", [L, 2048, D]),
        "w_sc_out": din("w_sc_out", [L, 2048, D]), "w_o": din("w_o", [L, D, D]),
        "w_ffn2_in": din("w_ffn2_in", [L, D, 2 * DFF]), "w_ffn2_out": din("w_ffn2_out", [L, DFF, D]),
    }
    spd = din("sp", [128, L * SP_L])
    cstd = din("cst", [128, CS_N])
    out = nc.dram_tensor("out", [NTOK, D], F32, kind="ExternalOutput").ap()
    dbgd = nc.dram_tensor("dbg", [6, KC, 128, T], F32, kind="ExternalOutput").ap() if dbg else None
    dbg2 = nc.dram_tensor("dbg2", [128, 64 * T], BF16, kind="ExternalOutput").ap() if dbg else None
    hT = nc.dram_tensor("hT_scr", [KC, 128, NTOK], F32).ap()
    fT = nc.dram_tensor("fT_scr", [KC, 128, NTOK], F32).ap()
    hT_res = [[Res() for _ in range(KC)] for _ in range(NT)]
    fT_res = [[Res() for _ in range(KC)] for _ in range(NT)]

    P = Prog(nc)
    ws = WeightStream(P)

    def sb(name, shape, dt=F32):
        return Buf(nc.alloc_sbuf_tensor("sb_" + name, list(shape), dt))

    sp = sb("sp", [128, L * SP_L])
    ident = sb("ident", [128, 128])
    ones_bf = sb("ones_bf", [128, 128], BF16)
    ones_f = sb("ones_f", [128, 128])
    epsc = sb("epsc", [128, 1])
    onec = sb("onec", [128, 1])
    xn = sb("xn", [128, KC, T], BF16)
    NSTG = 8
    stg = [sb(f"stg{i}", [128, T]) for i in range(NSTG)]
    rstd = [sb(f"rstd{i}", [128, T]) for i in range(2)]
    ARENA = 96 * 1024
    arena = nc.alloc_sbuf_tensor("arena", [128, ARENA // 4], F32)
    arena_off = [0]

    def carve(nbytes, dt=F32):
        o = arena_off[0]
        arena_off[0] += (nbytes + 31) // 32 * 32
        assert arena_off[0] <= ARENA, arena_off[0]
        a = arena[:, o // 4:(o + nbytes) // 4]
        if dt == BF16:
            a = a.bitcast(BF16)
        return a

    hidden = Buf(carve(FC * T * 2, BF16).rearrange("p (f t) -> p f t", t=T))
    arena_off[0] = 0
    ogT = Buf(carve(NH * T * 2, BF16).rearrange("p (f t) -> p f t", t=T))
    Sst = [Buf(carve(512)) for _ in range(NH)]
    Stmp = [Buf(carve(512)) for _ in range(2)]
    mk_mti = Buf(carve(2048)); mk_m01 = Buf(carve(2048)); mk_mls = Buf(carve(2048)); mk_i8 = Buf(carve(2048))
    halo_q = Buf(carve(48 * 3 * 4).rearrange("p (c w) -> p c w", w=3))
    halo_s = Buf(carve(16 * 2 * 4).rearrange("p (c w) -> p c w", w=2))
    pf = {n: Buf(carve(512).rearrange("p (c h) -> p c h", h=16)) for n in ("gcP", "btP", "bgP", "kdP")}
    selb = Buf(carve(512))
    raw = [Buf(carve(520 * 4)) for _ in range(2)]
    ff = {n: Buf(carve(2048)) for n in ("gF", "btF")}
    dr_mark = arena_off[0]
    mT = Buf(carve(KC * T * 2, BF16).rearrange("p (f t) -> p f t", t=T))
    scT = Buf(carve(NH * T * 2, BF16).rearrange("p (f t) -> p f t", t=T))
    arena_off[0] = dr_mark
    qT = Buf(carve(2048)); kT = Buf(carve(2048)); vT = Buf(carve(2048)); zs = Buf(carve(2048))
    kP = Buf(carve(4096).rearrange("p (c d) -> p c d", d=128))
    kbgP = Buf(carve(4096).rearrange("p (c d) -> p c d", d=128))
    vP = Buf(carve(4096).rearrange("p (c d) -> p c d", d=128))
    uP = vP
    gcB = Buf(carve(2048)); btB = Buf(carve(2048)); egB = Buf(carve(2048))
    dTm = Buf(carve(2048)); lT = Buf(carve(2048)); lM = Buf(carve(2048)); attnT = Buf(carve(2048))
    PTx = Buf(carve(2048))
    TTb = [Buf(carve(2048)) for _ in range(2)]
    wT = Buf(carve(2048))
    vnew = [Buf(carve(512)) for _ in range(2)]
    ff["gF2"] = TTb[0]; ff["egF"] = TTb[1]; ff["bgF"] = dTm; ff["kdF"] = PTx
    print("arena used (mixer)", arena_off[0], "of", ARENA)

    banks = [Buf(nc.alloc_psum_tensor(f"ps{i}", [128, T], F32)) for i in range(8)]
    bank_ctr = [0]
    stg_ctr = [0]

    def getbanks(n):
        r = []
        for _ in range(n):
            r.append(banks[bank_ctr[0] % 6])
            bank_ctr[0] += 1
        return r

    def getstg():
        b = stg[stg_ctr[0] % NSTG]
        stg_ctr[0] += 1
        return b

    def E(eng, fn, r=(), w=()):
        return P.emit(eng, fn, reads=[b.res for b in r], writes=[b.res for b in w])

    ld_sems = [P.newsem(f"ld{i}") for i in range(NSTG)]
    st_sems = [P.newsem(f"st{i}") for i in range(NSTG)]
    misc_sem = P.newsem("misc")

    def v3(buf, c=64):
        return buf.ap.rearrange("p (a b) -> p a b", b=c)

    def job(groups, evac):
        allbanks = []
        for (pieces, K, rhs_fn) in groups:
            nb_tot = sum(nc_ // 128 for (_, _, nc_) in pieces)
            gb = getbanks(nb_tot) if not P.planning else [None] * nb_tot
            allbanks.extend(gb)
            colsum = sum(nc_ for (_, _, nc_) in pieces)
            kb = min(K, SLOT // colsum)
            k0 = 0
            while k0 < K:
                kn = min(kb, K - k0)
                tiles = []
                off = 0
                offs = []
                for (mat, c0, ncols) in pieces:
                    src = mat[k0 * 128:(k0 + kn) * 128, c0:c0 + ncols].rearrange("(k p) n -> p k n", p=128)
                    tiles.append((src, off, kn, ncols))
                    offs.append(off)
                    off += kn * ncols
                base, wres = ws.get(tiles)
                if not P.planning:
                    bi = 0
                    for pi, (mat, c0, ncols) in enumerate(pieces):
                        for b in range(ncols // 128):
                            bank = gb[bi]
                            bi += 1

                            def pe_fn(e, base=base, o=offs[pi], ncols=ncols, b=b, kn=kn, k0=k0, K=K,
                                      bank=bank, rhs_fn=rhs_fn):
                                ins = None
                                for kk in range(kn):
                                    a = base + o + kk * ncols + b * 128
                                    ins = e.matmul(bank.ap[:, :], lhsT=ws.buf[:, a:a + 128], rhs=rhs_fn(k0 + kk),
                                                   start=(k0 + kk == 0), stop=(k0 + kk == K - 1))
                                return ins
                            rres = [wres] + list(rhs_fn.res)
                            if k0 == 0:
                                P.emit("pe", pe_fn, reads=rres, writes=[bank.res])
                            else:
                                tok = P.emit("pe", pe_fn, reads=rres)
                                bank.res.w = tok
                ws.done()
                bg.run(1)
                k0 += kn
        if not P.planning:
            evac(allbanks)

    def rhs_of(buf):
        def f(k):
            return buf.ap[:, k, :]
        f.res = [buf.res]
        return f

    def tokslice(t):
        return slice(t * T, (t + 1) * T)

    def init_pass(t):
        xv = x[t * T:(t + 1) * T, :].rearrange("(tb p) d -> p tb d", p=128)
        for c in range(KC):
            si = stg_ctr[0] % NSTG
            s_in = getstg()
            P.emit("sync", lambda e, s_in=s_in, c=c: e.dma_start(
                out=s_in.ap.rearrange("p (tb d) -> p tb d", d=128), in_=xv[:, :, c * 128:(c + 1) * 128]),
                writes=[s_in.res], dmasem=ld_sems[si])
            (bk,) = getbanks(1)

            def pe_fn(e, s_in=s_in, bk=bk):
                ins = None
                for tb in range(4):
                    ins = e.transpose(bk.ap[:, tb * 128:(tb + 1) * 128], s_in.ap[:, tb * 128:(tb + 1) * 128], ident.ap[:, :])
                return ins
            E("pe", pe_fn, r=[s_in, ident], w=[bk])
            so = stg_ctr[0] % NSTG
            s_out = getstg()
            E("act", lambda e, s_out=s_out, bk=bk: e.activation(out=s_out.ap[:, :], in_=bk.ap[:, :], func=AF.Copy),
              r=[bk], w=[s_out])
            P.emit("sync", lambda e, s_out=s_out, c=c: e.dma_start(out=hT[c, :, tokslice(t)], in_=s_out.ap[:, :]),
                   reads=[s_out.res], writes=[hT_res[t][c]], dmasem=st_sems[so])

    def final_pass(t):
        ov = out[t * T:(t + 1) * T, :].rearrange("(tb p) d -> p tb d", p=128)
        for c in range(KC):
            si = stg_ctr[0] % NSTG
            s_in = getstg()
            P.emit("sync", lambda e, s_in=s_in, c=c: e.dma_start(out=s_in.ap[:, :], in_=hT[c, :, tokslice(t)]),
                   reads=[hT_res[t][c]], writes=[s_in.res], dmasem=ld_sems[si])
            (bk,) = getbanks(1)

            def pe_fn(e, s_in=s_in, bk=bk):
                ins = None
                for tb in range(4):
                    ins = e.transpose(bk.ap[:, tb * 128:(tb + 1) * 128], s_in.ap[:, tb * 128:(tb + 1) * 128], ident.ap[:, :])
                return ins
            E("pe", pe_fn, r=[s_in, ident], w=[bk])
            so = stg_ctr[0] % NSTG
            s_out = getstg()
            E("act", lambda e, s_out=s_out, bk=bk: e.activation(out=s_out.ap[:, :], in_=bk.ap[:, :], func=AF.Copy),
              r=[bk], w=[s_out])
            P.emit("sync", lambda e, s_out=s_out, c=c: e.dma_start(
                out=ov[:, :, c * 128:(c + 1) * 128], in_=s_out.ap.rearrange("p (tb d) -> p tb d", d=128)),
                reads=[s_out.res], dmasem=st_sems[so])

    def rstd_from_bank(bk, dst, scale):
        E("act", lambda e: e.activation(out=dst.ap[:, :], in_=bk.ap[:, :], func=AF.Sqrt, bias=epsc.ap[:, 0:1], scale=scale),
          r=[bk, epsc], w=[dst])
        E("dve", lambda e: e.reciprocal(out=dst.ap[:, :], in_=dst.ap[:, :]), r=[dst], w=[dst])

    nsq = [sb(f"nsq{i}", [128, T], BF16) for i in range(4)]

    class NormParts:
        def __init__(self, l, t, gname):
            self.t = t
            self.g0 = l * SP_L + SP_G[gname]
            self.sbk = banks[6]

        def pre(self, p):
            t, g0 = self.t, self.g0
            for j in range(4):
                c = p * 4 + j
                si = stg_ctr[0] % NSTG
                hs = getstg()
                P.emit("sync", lambda e, hs=hs, c=c: e.dma_start(out=hs.ap[:, :], in_=hT[c, :, tokslice(t)]),
                       reads=[hT_res[t][c]], writes=[hs.res], dmasem=ld_sems[si])
                E("act", lambda e, hs=hs, c=c: e.activation(out=xn.ap[:, c, :], in_=hs.ap[:, :], func=AF.Copy,
                                                           scale=sp.ap[:, g0 + c:g0 + c + 1]),
                  r=[hs, sp], w=[xn])
                E("act", lambda e, hs=hs, j=j: e.activation(out=nsq[j].ap[:, :], in_=hs.ap[:, :], func=AF.Square),
                  r=[hs], w=[nsq[j]])

        def stat(self, p):
            sbk = self.sbk
            for j in range(4):
                c = p * 4 + j
                tokm = P.emit("pe", lambda e, j=j, c=c: e.matmul(sbk.ap[:, :], lhsT=ones_bf.ap[:, :], rhs=nsq[j].ap[:, :],
                                                               start=(c == 0), stop=(c == KC - 1)),
                              reads=[nsq[j].res, ones_bf.res], writes=[sbk.res] if c == 0 else [])
                if c > 0 and tokm is not None:
                    sbk.res.w = tokm

        def finish(self):
            rs = rstd[0]
            rstd_from_bank(self.sbk, rs, 1.0 / D)
            for c in range(KC):
                E("dve", lambda e, c=c: e.tensor_tensor(out=xn.ap[:, c, :], in0=xn.ap[:, c, :], in1=rs.ap[:, :], op=ALU.mult),
                  r=[xn, rs], w=[xn])

        def all(self):
            for p in range(8):
                self.pre(p)
                self.stat(p)
            self.finish()

    class BG:
        def __init__(self):
            self.q = []

        def run(self, n):
            if P.planning:
                return
            for _ in range(n):
                if not self.q:
                    return
                self.q.pop(0)[1]()

        def drain(self):
            self.run(len(self.q))

        def has_tile(self, t):
            return any(tt == t for (tt, _) in self.q)

    bg = BG()

    class OutStats:
        def __init__(self, t):
            self.t = t
            self.sbk = banks[7]
            self.n = 0

        def flush(self):
            pass

        def evac(self, c, bk):
            so = stg_ctr[0] % NSTG
            fs = getstg()
            E("act", lambda e, fs=fs, bk=bk: e.activation(out=fs.ap[:, :], in_=bk.ap[:, :], func=AF.Copy), r=[bk], w=[fs])
            P.emit("sync", lambda e, fs=fs, c=c: e.dma_start(out=fT[c, :, tokslice(self.t)], in_=fs.ap[:, :]),
                   reads=[fs.res], writes=[fT_res[self.t][c]], dmasem=st_sems[so])
            sq = getstg()
            sqv = sq.ap[:, 0:T // 2].bitcast(BF16)
            E("dve", lambda e, sqv=sqv, fs=fs: e.tensor_tensor(out=sqv, in0=fs.ap[:, :], in1=fs.ap[:, :], op=ALU.mult),
              r=[fs], w=[sq])
            n = self.n
            tokm = P.emit("pe", lambda e, sqv=sqv, n=n: e.matmul(self.sbk.ap[:, :], lhsT=ones_bf.ap[:, :], rhs=sqv,
                                                              start=(n == 0), stop=(n == KC - 1)),
                          reads=[sq.res, ones_bf.res], writes=[self.sbk.res] if n == 0 else [])
            if n > 0 and tokm is not None:
                self.sbk.res.w = tokm
            self.n += 1

    def update_pass(l, t, gname, coef, ost):
        if P.planning:
            return
        bg.drain()
        g0 = l * SP_L + SP_G[gname]
        rs = rstd[1]
        rstd_from_bank(ost.sbk, rs, 1.0 / D)

        def task(c):
            si = stg_ctr[0] % NSTG
            hs = getstg()
            P.emit("sync", lambda e, hs=hs, c=c: e.dma_start(out=hs.ap[:, :], in_=hT[c, :, tokslice(t)]),
                   reads=[hT_res[t][c]], writes=[hs.res], dmasem=ld_sems[si])
            sj = stg_ctr[0] % NSTG
            fs = getstg()
            P.emit("sync", lambda e, fs=fs, c=c: e.dma_start(out=fs.ap[:, :], in_=fT[c, :, tokslice(t)]),
                   reads=[fT_res[t][c]], writes=[fs.res], dmasem=ld_sems[sj])
            E("dve", lambda e, fs=fs, c=c: e.scalar_tensor_tensor(out=fs.ap[:, :], in0=fs.ap[:, :],
                                                                 scalar=sp.ap[:, g0 + c:g0 + c + 1], in1=rs.ap[:, :],
                                                                 op0=ALU.mult, op1=ALU.mult),
              r=[fs, sp, rs], w=[fs])
            E("dve", lambda e, fs=fs, hs=hs: e.scalar_tensor_tensor(out=hs.ap[:, :], in0=fs.ap[:, :], scalar=float(coef),
                                                                   in1=hs.ap[:, :], op0=ALU.mult, op1=ALU.add),
              r=[fs, hs], w=[hs])
            P.emit("sync", lambda e, hs=hs, c=c: e.dma_start(out=hT[c, :, tokslice(t)], in_=hs.ap[:, :]),
                   reads=[hs.res], writes=[hT_res[t][c]], dmasem=st_sems[si])
        for c in range(KC):
            bg.q.append((t, (lambda c=c: task(c))))

    def ffn(l, t, which):
        w_in_ = W[f"w_ffn{which}_in"][l]
        w_out_ = W[f"w_ffn{which}_out"][l]
        rx = rhs_of(xn)
        for fp in range(FC // 2):
            def evac(bks, fp=fp):
                for fi in range(2):
                    f = fp * 2 + fi
                    st_ = getstg()
                    E("act", lambda e, st_=st_, bk=bks[fi]: e.activation(out=st_.ap[:, :], in_=bk.ap[:, :], func=AF.Silu),
                      r=[bks[fi]], w=[st_])
                    E("dve", lambda e, st_=st_, bk=bks[2 + fi], f=f: e.tensor_tensor(out=hidden.ap[:, f, :], in0=bk.ap[:, :],
                                                                                     in1=st_.ap[:, :], op=ALU.mult),
                      r=[bks[2 + fi], st_], w=[hidden])
            job([([(w_in_, fp * 256, 256), (w_in_, DFF + fp * 256, 256)], KC, rx)], evac)
        ost = OutStats(t)
        rh = rhs_of(hidden)
        nn = next_norm[0]
        for og in range(8):
            def evac2(bks, og=og):
                for j in range(4):
                    ost.evac(og * 4 + j, bks[j])
            if nn is not None and not P.planning:
                nn.pre(og)
            job([([(w_out_, og * 512, 512)], FC, rh)], evac2)
            if nn is not None and not P.planning:
                nn.stat(og)
        if nn is not None and not P.planning:
            nn.finish()
        update_pass(l, t, f"ffn{which}_post_g", 0.5, ost)

    def mixer_setup(l):
        o = l * SP_L
        for (b, c0) in ((mk_mti, CS_MTI), (mk_m01, CS_M01), (mk_mls, CS_MLS), (mk_i8, CS_I8)):
            P.emit("sync", lambda e, b=b, c0=c0: e.dma_start(out=b.ap[0:64, :], in_=cstd[0:64, c0:c0 + 512]),
                   writes=[b.res], dmasem=misc_sem)
        for h in range(NH):
            E("dve", lambda e, h=h: e.memset(Sst[h].ap[:, :], 0.0), w=[Sst[h]])
        E("dve", lambda e: e.memset(halo_q.ap[:, :, :], 0.0), w=[halo_q])
        E("dve", lambda e: e.memset(halo_s.ap[:, :, :], 0.0), w=[halo_s])
        E("act", lambda e: e.activation(out=sp.ap[0:16, o + SP_ALOG:o + SP_ALOG + 1], in_=sp.ap[0:16, o + SP_ALOG:o + SP_ALOG + 1],
                                        func=AF.Exp), r=[sp], w=[sp])
        E("dve", lambda e: e.tensor_scalar(out=sp.ap[0:16, o + SP_ALOG:o + SP_ALOG + 1], in0=sp.ap[0:16, o + SP_ALOG:o + SP_ALOG + 1],
                                           scalar1=-1.0, scalar2=None, op0=ALU.mult), r=[sp], w=[sp])
        P.fence()

    def conv_taps(rawb, dst, wcol0, width):
        E("dve", lambda e: e.tensor_scalar(out=dst.ap[:, :], in0=rawb.ap[:, 0:T], scalar1=sp.ap[:, wcol0:wcol0 + 1],
                                           scalar2=None, op0=ALU.mult), r=[rawb, sp], w=[dst])
        for j in range(1, width):
            E("dve", lambda e, j=j: e.scalar_tensor_tensor(out=dst.ap[:, :], in0=rawb.ap[:, j:j + T],
                                                          scalar=sp.ap[:, wcol0 + j:wcol0 + j + 1], in1=dst.ap[:, :],
                                                          op0=ALU.mult, op1=ALU.add), r=[rawb, sp, dst], w=[dst])

    raw_ctr = [0]

    def mixer(l, t):
        o = l * SP_L
        w_in_ = W["w_in"][l]
        rx = rhs_of(xn)

        def evac_ab(bks):
            pass
        abk = getbanks(2) if not P.planning else [None, None]
        tiles = [(w_in_[:, C_A:C_A + 32].rearrange("(k p) n -> p k n", p=128), 0, KC, 32)]
        base, wres = ws.get(tiles)
        if not P.planning:
            for which in range(2):
                def pe_fn(e, which=which, base=base):
                    ins = None
                    for k in range(KC):
                        a = base + k * 32 + which * 16
                        ins = e.matmul(abk[which].ap[0:16, :], lhsT=ws.buf[:, a:a + 16], rhs=xn.ap[:, k, :],
                                       start=(k == 0), stop=(k == KC - 1))
                    return ins
                P.emit("pe", pe_fn, reads=[wres, xn.res], writes=[abk[which].res])
        ws.done()
        if not P.planning:
            gF, gF2, btF, egF, bgF, kdF = (ff[n] for n in ("gF", "gF2", "btF", "egF", "bgF", "kdF"))
            R16 = slice(0, 16)
            E("act", lambda e: e.activation(out=gF.ap[R16, :], in_=abk[0].ap[R16, :], func=AF.Exp,
                                            bias=sp.ap[R16, o + SP_DTB:o + SP_DTB + 1]), r=[abk[0], sp], w=[gF])
            E("act", lambda e: e.activation(out=gF.ap[R16, :], in_=gF.ap[R16, :], func=AF.Ln, bias=onec.ap[R16, 0:1]),
              r=[gF, onec], w=[gF])
            E("dve", lambda e: e.tensor_scalar(out=gF.ap[R16, :], in0=gF.ap[R16, :], scalar1=sp.ap[R16, o + SP_ALOG:o + SP_ALOG + 1],
                                               scalar2=None, op0=ALU.mult), r=[gF, sp], w=[gF])
            E("act", lambda e: e.activation(out=btF.ap[R16, :], in_=abk[1].ap[R16, :], func=AF.Sigmoid), r=[abk[1]], w=[btF])
            src, dst = gF, gF2
            s = 1
            while s < 64:
                sv, dv = v3(src), v3(dst)
                E("dve", lambda e, sv=sv, dv=dv, s=s: e.tensor_copy(out=dv[R16, :, 0:s], in_=sv[R16, :, 0:s]), r=[src], w=[dst])
                E("dve", lambda e, sv=sv, dv=dv, s=s: e.tensor_tensor(out=dv[R16, :, s:64], in0=sv[R16, :, s:64],
                                                                     in1=sv[R16, :, 0:64 - s], op=ALU.add), r=[src], w=[dst])
                src, dst = dst, src
                s *= 2
            gcF = src
            E("act", lambda e: e.activation(out=egF.ap[R16, :], in_=gcF.ap[R16, :], func=AF.Exp), r=[gcF], w=[egF])
            E("dve", lambda e: e.tensor_tensor(out=bgF.ap[R16, :], in0=btF.ap[R16, :], in1=egF.ap[R16, :], op=ALU.mult),
              r=[btF, egF], w=[bgF])
            E("dve", lambda e: e.tensor_tensor(out=v3(kdF)[R16, :, :], in0=v3(gcF)[R16, :, 63:64].to_broadcast([16, 8, 64]),
                                               in1=v3(gcF)[R16, :, :], op=ALU.subtract), r=[gcF], w=[kdF])
            E("act", lambda e: e.activation(out=kdF.ap[R16, :], in_=kdF.ap[R16, :], func=AF.Exp), r=[kdF], w=[kdF])
            for (srcb, name) in ((gcF, "gcP"), (btF, "btP"), (bgF, "bgP"), (kdF, "kdP")):
                (bk,) = getbanks(1)

                def pe_fn(e, srcb=srcb, bk=bk):
                    ins = None
                    for c in range(8):
                        ins = e.transpose(bk.ap[0:64, c * 16:(c + 1) * 16], srcb.ap[R16, c * 64:(c + 1) * 64], ident.ap[0:16, 0:16])
                    return ins
                E("pe", pe_fn, r=[srcb, ident], w=[bk])
                E("act", lambda e, bk=bk, name=name: e.activation(out=pf[name].ap.rearrange("p c h -> p (c h)")[0:64, :],
                                                                 in_=bk.ap[0:64, 0:128], func=AF.Copy), r=[bk], w=[pf[name]])
        else:
            gcF = btF = None

        for h in range(NH):
            def evac_head(bks, h=h):
                R64 = slice(0, 64)
                outs = (qT, kT, vT)
                for xi in range(3):
                    rb = raw[raw_ctr[0] % 2]
                    raw_ctr[0] += 1
                    hc = xi * 16 + h
                    E("act", lambda e, rb=rb, bk=bks[xi]: e.activation(out=rb.ap[:, 3:3 + T], in_=bk.ap[:, :], func=AF.Copy),
                      r=[bks[xi]], w=[rb])
                    E("dve", lambda e, rb=rb, hc=hc: e.tensor_copy(out=rb.ap[:, 0:3], in_=halo_q.ap[:, hc, :]), r=[halo_q], w=[rb])
                    E("dve", lambda e, rb=rb, hc=hc: e.tensor_copy(out=halo_q.ap[:, hc, :], in_=rb.ap[:, T:T + 3]), r=[rb], w=[halo_q])
                    conv_taps(rb, outs[xi], o + SP_DNW + hc * 4, 4)
                    E("act", lambda e, d=outs[xi]: e.activation(out=d.ap[:, :], in_=d.ap[:, :], func=AF.Silu), r=[outs[xi]], w=[outs[xi]])
                E("act", lambda e: e.activation(out=zs.ap[:, :], in_=bks[3].ap[:, :], func=AF.Silu), r=[bks[3]], w=[zs])
                for (d, sc) in ((qT, 128 ** -0.5), (kT, 1.0)):
                    sq = getstg()
                    E("act", lambda e, d=d, sq=sq: e.activation(out=sq.ap[:, :], in_=d.ap[:, :], func=AF.Square), r=[d], w=[sq])
                    (bk,) = getbanks(1)
                    E("pe", lambda e, sq=sq, bk=bk: e.matmul(bk.ap[:, :], lhsT=ones_f.ap[:, :], rhs=sq.ap[:, :], start=True, stop=True),
                      r=[sq, ones_f], w=[bk])
                    rstd_from_bank(bk, sq, 1.0)
                    E("dve", lambda e, d=d, sq=sq, sc=sc: e.scalar_tensor_tensor(out=d.ap[:, :], in0=d.ap[:, :], scalar=float(sc),
                                                                                in1=sq.ap[:, :], op0=ALU.mult, op1=ALU.mult),
                      r=[d, sq], w=[d])
                E("dve", lambda e: e.tensor_copy(out=selb.ap[0:16, :], in_=ident.ap[0:16, h:h + 1].to_broadcast([16, 128])),
                  r=[ident], w=[selb])
                for (srcb, dstb, fn) in ((gcF, gcB, AF.Copy), (gcF, egB, AF.Exp), (btF, btB, AF.Copy)):
                    (bk,) = getbanks(1)
                    E("pe", lambda e, srcb=srcb, bk=bk: e.matmul(bk.ap[:, :], lhsT=selb.ap[0:16, :], rhs=srcb.ap[0:16, :],
                                                                start=True, stop=True), r=[selb, srcb], w=[bk])
                    E("act", lambda e, bk=bk, dstb=dstb, fn=fn: e.activation(out=dstb.ap[:, :], in_=bk.ap[:, :], func=fn),
                      r=[bk], w=[dstb])
                for (srcb, dstb) in ((kT, kP), (vT, vP)):
                    bk2 = getbanks(2)
                    for half in range(2):
                        def pe_fn(e, srcb=srcb, bk=bk2[half], half=half):
                            ins = None
                            for cc in range(4):
                                c = half * 4 + cc
                                ins = e.transpose(bk.ap[0:64, cc * 128:(cc + 1) * 128], srcb.ap[:, c * 64:(c + 1) * 64], ident.ap[:, :])
                            return ins
                        E("pe", pe_fn, r=[srcb, ident], w=[bk2[half]])
                        E("act", lambda e, bk=bk2[half], dstb=dstb, half=half: e.activation(
                            out=dstb.ap[R64, half * 4:(half + 1) * 4, :].rearrange("p c d -> p (c d)"), in_=bk.ap[R64, :], func=AF.Copy),
                          r=[bk2[half]], w=[dstb])
                def bc(pn):
                    return pf[pn].ap[R64, :, h:h + 1].to_broadcast([64, 8, 128])
                E("dve", lambda e: e.tensor_tensor(out=kbgP.ap[R64, :, :], in0=kP.ap[R64, :, :], in1=bc("bgP"), op=ALU.mult),
                  r=[kP, pf["bgP"]], w=[kbgP])
                E("dve", lambda e: e.tensor_tensor(out=kP.ap[R64, :, :], in0=kP.ap[R64, :, :], in1=bc("kdP"), op=ALU.mult),
                  r=[kP, pf["kdP"]], w=[kP])
                E("dve", lambda e: e.tensor_tensor(out=vP.ap[R64, :, :], in0=vP.ap[R64, :, :], in1=bc("btP"), op=ALU.mult),
                  r=[vP, pf["btP"]], w=[vP])
                bkk, bqk = getbanks(2)

                def pe_kk(e):
                    ins = None
                    for c in range(8):
                        cs = slice(c * 64, (c + 1) * 64)
                        ins = e.matmul(bkk.ap[R64, cs], lhsT=kT.ap[:, cs], rhs=kT.ap[:, cs], start=True, stop=True)
                    return ins

                def pe_qk(e):
                    ins = None
                    for c in range(8):
                        cs = slice(c * 64, (c + 1) * 64)
                        ins = e.matmul(bqk.ap[R64, cs], lhsT=kT.ap[:, cs], rhs=qT.ap[:, cs], start=True, stop=True)
                    return ins
                E("pe", pe_kk, r=[kT], w=[bkk])
                E("pe", pe_qk, r=[kT, qT], w=[bqk])
                gcPb = pf["gcP"].ap[R64, :, h:h + 1].to_broadcast([64, 8, 64])
                btPb = pf["btP"].ap[R64, :, h:h + 1].to_broadcast([64, 8, 64])
                E("dve", lambda e: e.tensor_tensor(out=v3(dTm)[R64, :, :], in0=v3(gcB)[R64, :, :], in1=gcPb, op=ALU.subtract),
                  r=[gcB, pf["gcP"]], w=[dTm])
                E("dve", lambda e: e.tensor_tensor(out=dTm.ap[R64, :], in0=dTm.ap[R64, :], in1=mk_mti.ap[R64, :], op=ALU.add),
                  r=[dTm, mk_mti], w=[dTm])
                E("act", lambda e: e.activation(out=dTm.ap[R64, :], in_=dTm.ap[R64, :], func=AF.Exp), r=[dTm], w=[dTm])
                E("dve", lambda e: e.tensor_tensor(out=attnT.ap[R64, :], in0=bqk.ap[R64, :], in1=dTm.ap[R64, :], op=ALU.mult),
                  r=[bqk, dTm], w=[attnT])
                E("dve", lambda e: e.tensor_tensor(out=lT.ap[R64, :], in0=bkk.ap[R64, :], in1=dTm.ap[R64, :], op=ALU.mult),
                  r=[bkk, dTm], w=[lT])
                E("dve", lambda e: e.tensor_tensor(out=lT.ap[R64, :], in0=lT.ap[R64, :], in1=mk_m01.ap[R64, :], op=ALU.mult),
                  r=[lT, mk_m01], w=[lT])
                E("dve", lambda e: e.tensor_tensor(out=lT.ap[R64, :], in0=lT.ap[R64, :], in1=btB.ap[R64, :], op=ALU.mult),
                  r=[lT, btB], w=[lT])
                E("dve", lambda e: e.tensor_tensor(out=v3(lM)[R64, :, :], in0=gcPb, in1=v3(gcB)[R64, :, :], op=ALU.subtract),
                  r=[gcB, pf["gcP"]], w=[lM])
                E("dve", lambda e: e.tensor_tensor(out=lM.ap[R64, :], in0=lM.ap[R64, :], in1=mk_mls.ap[R64, :], op=ALU.add),
                  r=[lM, mk_mls], w=[lM])
                E("act", lambda e: e.activation(out=lM.ap[R64, :], in_=lM.ap[R64, :], func=AF.Exp), r=[lM], w=[lM])
                E("dve", lambda e: e.tensor_tensor(out=v3(lM)[R64, :, :], in0=v3(lM)[R64, :, :], in1=btPb, op=ALU.mult),
                  r=[lM, pf["btP"]], w=[lM])
                E("dve", lambda e: e.tensor_tensor(out=lM.ap[R64, :], in0=lM.ap[R64, :], in1=bkk.ap[R64, :], op=ALU.mult),
                  r=[lM, bkk], w=[lM])
                E("dve", lambda e: e.tensor_tensor(out=TTb[0].ap[R64, :], in0=mk_i8.ap[R64, :], in1=lT.ap[R64, :], op=ALU.subtract),
                  r=[mk_i8, lT], w=[TTb[0]])
                Pc, PTc, TTc = lM, lT, TTb[0]
                for s in range(5):
                    Pn = (dTm, lM)[s % 2]
                    PTn = (PTx, lT)[s % 2]
                    TTn = TTb[(s + 1) % 2]
                    ba, bb_, bc_ = getbanks(3)

                    def pe_sq(e, Pc=Pc, PTc=PTc, bk=ba):
                        ins = None
                        for c in range(8):
                            cs = slice(c * 64, (c + 1) * 64)
                            ins = e.matmul(bk.ap[R64, cs], lhsT=PTc.ap[R64, cs], rhs=Pc.ap[R64, cs], start=True, stop=True)
                        return ins
                    E("pe", pe_sq, r=[Pc, PTc], w=[ba])
                    E("act", lambda e, Pn=Pn, bk=ba: e.activation(out=Pn.ap[R64, :], in_=bk.ap[R64, :], func=AF.Copy), r=[ba], w=[Pn])
                    if s < 4:
                        def pe_sqT(e, Pc=Pc, PTc=PTc, bk=bb_):
                            ins = None
                            for c in range(8):
                                cs = slice(c * 64, (c + 1) * 64)
                                ins = e.matmul(bk.ap[R64, cs], lhsT=Pc.ap[R64, cs], rhs=PTc.ap[R64, cs], start=True, stop=True)
                            return ins
                        E("pe", pe_sqT, r=[Pc, PTc], w=[bb_])
                        E("dve", lambda e, PTn=PTn, bk=bb_: e.tensor_copy(out=PTn.ap[R64, :], in_=bk.ap[R64, :]), r=[bb_], w=[PTn])

                    def pe_tt(e, Pn=Pn, TTc=TTc, bk=bc_):
                        ins = None
                        for c in range(8):
                            cs = slice(c * 64, (c + 1) * 64)
                            ins = e.matmul(bk.ap[R64, cs], lhsT=Pn.ap[R64, cs], rhs=TTc.ap[R64, cs], start=True, stop=True)
                        return ins
                    E("pe", pe_tt, r=[Pn, TTc], w=[bc_])
                    E("dve", lambda e, TTn=TTn, TTc=TTc, bk=bc_: e.tensor_tensor(out=TTn.ap[R64, :], in0=bk.ap[R64, :], in1=TTc.ap[R64, :],
                                                                               op=ALU.add), r=[bc_, TTc], w=[TTn])
                    Pc, PTc, TTc = Pn, PTn, TTn
                TT = TTc
                (bw,) = getbanks(1)

                def pe_w(e):
                    ins = None
                    for c in range(8):
                        cs = slice(c * 64, (c + 1) * 64)
                        ins = e.matmul(bw.ap[:, cs], lhsT=kbgP.ap[R64, c, :], rhs=TT.ap[R64, cs], start=True, stop=True)
                    return ins
                E("pe", pe_w, r=[kbgP, TT], w=[bw])
                E("act", lambda e: e.activation(out=wT.ap[:, :], in_=bw.ap[:, :], func=AF.Copy), r=[bw], w=[wT])
                bu = getbanks(2)
                for half in range(2):
                    def pe_u(e, half=half):
                        ins = None
                        for cc in range(4):
                            c = half * 4 + cc
                            ins = e.matmul(bu[half].ap[R64, cc * 128:(cc + 1) * 128], lhsT=TT.ap[R64, c * 64:(c + 1) * 64],
                                           rhs=vP.ap[R64, c, :], start=True, stop=True)
                        return ins
                    E("pe", pe_u, r=[TT, vP], w=[bu[half]])
                    E("act", lambda e, half=half: e.activation(out=uP.ap[R64, half * 4:(half + 1) * 4, :].rearrange("p c d -> p (c d)"),
                                                              in_=bu[half].ap[R64, :], func=AF.Copy), r=[bu[half]], w=[uP])
                E("dve", lambda e: e.tensor_tensor(out=qT.ap[:, :], in0=qT.ap[:, :], in1=egB.ap[:, :], op=ALU.mult), r=[qT, egB], w=[qT])
                oT = vT
                seq = [Sst[h], Stmp[0], Stmp[1]]
                cur = Sst[h]
                for c in range(8):
                    cs = slice(c * 64, (c + 1) * 64)
                    nxt = Sst[h] if c == 7 else Stmp[c % 2]
                    vn = vnew[c % 2]
                    b1, b2, b3 = getbanks(3)
                    E("pe", lambda e, b1=b1, cs=cs, cur=cur: e.matmul(b1.ap[R64, 0:128], lhsT=wT.ap[:, cs], rhs=cur.ap[:, :], start=True, stop=True),
                      r=[wT, cur], w=[b1])
                    E("dve", lambda e, b1=b1, vn=vn, c=c: e.tensor_tensor(out=vn.ap[R64, :], in0=uP.ap[R64, c, :], in1=b1.ap[R64, 0:128],
                                                                         op=ALU.subtract), r=[uP, b1], w=[vn])

                    def pe_o(e, b2=b2, cs=cs, cur=cur, vn=vn):
                        e.matmul(b2.ap[:, 0:64], lhsT=cur.ap[:, :], rhs=qT.ap[:, cs], start=True, stop=False)
                        return e.matmul(b2.ap[:, 0:64], lhsT=vn.ap[R64, :], rhs=attnT.ap[R64, cs], start=False, stop=True)
                    E("pe", pe_o, r=[cur, qT, vn, attnT], w=[b2])
                    E("act", lambda e, b2=b2, cs=cs: e.activation(out=oT.ap[:, cs], in_=b2.ap[:, 0:64], func=AF.Copy), r=[b2], w=[oT])
                    E("pe", lambda e, b3=b3, c=c, vn=vn: e.matmul(b3.ap[:, 0:128], lhsT=kP.ap[R64, c, :], rhs=vn.ap[R64, :], start=True, stop=True),
                      r=[kP, vn], w=[b3])
                    E("dve", lambda e, b3=b3, cur=cur, nxt=nxt, c=c: e.scalar_tensor_tensor(
                        out=nxt.ap[:, :], in0=cur.ap[:, :], scalar=egB.ap[:, c * 64 + 63:c * 64 + 64], in1=b3.ap[:, 0:128],
                        op0=ALU.mult, op1=ALU.add), r=[cur, egB, b3], w=[nxt])
                    cur = nxt
                sq = getstg()
                E("act", lambda e, sq=sq: e.activation(out=sq.ap[:, :], in_=oT.ap[:, :], func=AF.Square), r=[oT], w=[sq])
                (bk,) = getbanks(1)
                E("pe", lambda e, sq=sq, bk=bk: e.matmul(bk.ap[:, :], lhsT=ones_f.ap[:, :], rhs=sq.ap[:, :], start=True, stop=True),
                  r=[sq, ones_f], w=[bk])
                rstd_from_bank(bk, sq, 1.0 / 128)
                E("dve", lambda e, sq=sq: e.scalar_tensor_tensor(out=sq.ap[:, :], in0=oT.ap[:, :], scalar=sp.ap[:, o + SP_NG:o + SP_NG + 1],
                                                                in1=sq.ap[:, :], op0=ALU.mult, op1=ALU.mult), r=[oT, sp, sq], w=[sq])
                E("dve", lambda e, sq=sq: e.tensor_tensor(out=ogT.ap[:, h, :], in0=sq.ap[:, :], in1=zs.ap[:, :], op=ALU.mult),
                  r=[sq, zs], w=[ogT])
            job([([(w_in_, C_Q + h * 128, 128), (w_in_, C_K + h * 128, 128), (w_in_, C_V + h * 128, 128),
                   (w_in_, C_Z + h * 128, 128)], KC, rx)], evac_head)

        P.fence()
        for c in range(16):
            def evac_sc(bks, c=c):
                tmp = getstg()
                rb = raw[raw_ctr[0] % 2]
                raw_ctr[0] += 1
                E("act", lambda e, tmp=tmp: e.activation(out=tmp.ap[:, :], in_=bks[1].ap[:, :], func=AF.Copy), r=[bks[1]], w=[tmp])
                E("dve", lambda e, tmp=tmp, rb=rb: e.tensor_tensor(out=rb.ap[:, 2:2 + T], in0=tmp.ap[:, :], in1=bks[2].ap[:, :], op=ALU.mult),
                  r=[tmp, bks[2]], w=[rb])
                E("dve", lambda e, rb=rb: e.tensor_copy(out=rb.ap[:, 0:2], in_=halo_s.ap[:, c, :]), r=[halo_s], w=[rb])
                E("dve", lambda e, rb=rb: e.tensor_copy(out=halo_s.ap[:, c, :], in_=rb.ap[:, T:T + 2]), r=[rb], w=[halo_s])
                conv_taps(rb, tmp, o + SP_SCW + c * 3, 3)
                E("dve", lambda e, tmp=tmp: e.tensor_tensor(out=scT.ap[:, c, :], in0=tmp.ap[:, :], in1=bks[0].ap[:, :], op=ALU.mult),
                  r=[tmp, bks[0]], w=[scT])
            job([([(w_in_, C_SB + c * 128, 128), (w_in_, C_SC + c * 128, 128), (w_in_, C_SX + c * 128, 128)], KC, rx)], evac_sc)

        rog, rsc = rhs_of(ogT), rhs_of(scT)
        for c in range(KC):
            def evac_m(bks, c=c):
                t1, t2 = getstg(), getstg()
                E("act", lambda e, t1=t1: e.activation(out=t1.ap[:, :], in_=bks[0].ap[:, :], func=AF.Sigmoid), r=[bks[0]], w=[t1])
                E("dve", lambda e, t1=t1: e.tensor_tensor(out=t1.ap[:, :], in0=t1.ap[:, :], in1=bks[2].ap[:, :], op=ALU.mult),
                  r=[t1, bks[2]], w=[t1])
                E("act", lambda e, t2=t2: e.activation(out=t2.ap[:, :], in_=bks[1].ap[:, :], func=AF.Sigmoid), r=[bks[1]], w=[t2])
                E("dve", lambda e, t2=t2: e.tensor_tensor(out=t2.ap[:, :], in0=t2.ap[:, :], in1=bks[3].ap[:, :], op=ALU.mult),
                  r=[t2, bks[3]], w=[t2])
                E("dve", lambda e, t1=t1, t2=t2: e.tensor_tensor(out=mT.ap[:, c, :], in0=t1.ap[:, :], in1=t2.ap[:, :], op=ALU.add),
                  r=[t1, t2], w=[mT])
            job([([(w_in_, C_GA + c * 128, 128), (w_in_, C_GB + c * 128, 128)], KC, rx),
                 ([(W["w_dn_out"][l], c * 128, 128)], 16, rog),
                 ([(W["w_sc_out"][l], c * 128, 128)], 16, rsc)], evac_m)
        if dbg2 is not None and not P.planning and l == 0 and t == 0:
            P.emit("sync", lambda e: e.dma_start(out=dbg2[:, 0:16 * T], in_=ogT.ap.rearrange("p f t -> p (f t)")), reads=[ogT.res], dmasem=dbg_sem)
            P.emit("sync", lambda e: e.dma_start(out=dbg2[:, 16 * T:32 * T], in_=scT.ap.rearrange("p f t -> p (f t)")), reads=[scT.res], dmasem=dbg_sem)
            P.emit("sync", lambda e: e.dma_start(out=dbg2[:, 32 * T:64 * T], in_=mT.ap.rearrange("p f t -> p (f t)")), reads=[mT.res], dmasem=dbg_sem)
        ost = OutStats(t)
        rm = rhs_of(mT)
        nn = next_norm[0]
        for og in range(8):
            def evac2(bks, og=og):
                for j in range(4):
                    ost.evac(og * 4 + j, bks[j])
            if nn is not None and not P.planning:
                nn.pre(og)
            job([([(W["w_o"][l], og * 512, 512)], KC, rm)], evac2)
            if nn is not None and not P.planning:
                nn.stat(og)
        if nn is not None and not P.planning:
            nn.finish()
        update_pass(l, t, "mix_post_g", 1.0, ost)
        P.fence()

    dbg_sem = P.newsem("dbg")
    dbg_ctr = [0]

    def dump():
        if P.planning or dbgd is None:
            return
        i = dbg_ctr[0]
        dbg_ctr[0] += 1
        for c in range(KC):
            P.emit("sync", lambda e, c=c, i=i: e.dma_start(out=dbgd[i, c], in_=hT[c, :, 0:T]),
                   reads=[hT_res[0][c]], dmasem=dbg_sem)

    next_norm = [None]
    dump_after = set()

    def program():
        if not P.planning:
            P.emit("sync", lambda e: e.dma_start(out=sp.ap[:, :], in_=spd), writes=[sp.res], dmasem=misc_sem)
            P.emit("sync", lambda e: e.dma_start(out=ident.ap[:, :], in_=cstd[:, CS_ID:CS_ID + 128]), writes=[ident.res], dmasem=misc_sem)
            E("dve", lambda e: e.memset(ones_bf.ap[:, :], 1.0), w=[ones_bf])
            E("dve", lambda e: e.memset(ones_f.ap[:, :], 1.0), w=[ones_f])
            E("dve", lambda e: e.memset(epsc.ap[:, :], EPS), w=[epsc])
            E("dve", lambda e: e.memset(onec.ap[:, :], 1.0), w=[onec])
            P.fence()
            for t in range(NT):
                init_pass(t)
        units = []
        for l in range(L):
            units += [("ffn1", l, t) for t in range(NT)] + [("mix", l, t) for t in range(NT)] + [("ffn2", l, t) for t in range(NT)]
        gn = {"ffn1": "ffn1_pre_g", "mix": "mix_pre_g", "ffn2": "ffn2_pre_g"}
        pending_norm = None
        for ui, (kind, l, t) in enumerate(units):
            if ui > 0 and units[ui - 1][0] != kind:
                bg.drain()
                P.fence()
                if kind == "mix" and not P.planning:
                    mixer_setup(l)
            elif kind == "mix" and ui == 0 and not P.planning:
                mixer_setup(l)
            if not P.planning:
                if pending_norm is None:
                    if bg.has_tile(t):
                        bg.drain()
                    NormParts(l, t, gn[kind]).all()
                pending_norm = None
                nxt = units[ui + 1] if ui + 1 < len(units) else None
                if nxt is not None and nxt[2] != t and not bg.has_tile(nxt[2]):
                    next_norm[0] = NormParts(nxt[1], nxt[2], gn[nxt[0]])
                    pending_norm = next_norm[0]
                else:
                    next_norm[0] = None
            if kind == "mix":
                mixer(l, t)
            else:
                ffn(l, t, 1 if kind == "ffn1" else 2)
            if kind == "mix" and ui in dump_after:
                pass
            if ui + 1 == len(units) or units[ui + 1][0] != kind:
                if not P.planning:
                    bg.drain()
                dump()
        if not P.planning:
            bg.drain()
        if not P.planning:
            for t in range(NT):
                final_pass(t)
            P.final_wait("sync")

    P.planning = True
    program()
    P.planning = False
    bank_ctr[0] = 0
    stg_ctr[0] = 0
    raw_ctr[0] = 0
    program()
    assert ws.consumed == len(ws.plan), (ws.consumed, len(ws.plan))
    P.run_block()
    print("ops", {k: len(v) for k, v in P.q.items()}, "sems", P.nsem, "wtiles", len(ws.plan))
    return nc


def _pack_small(inputs):
    L = 2
    sp = np.zeros((128, L * SP_L), np.float32)
    for l in range(L):
        o = l * SP_L
        for n, c0 in SP_G.items():
            sp[:, o + c0:o + c0 + 32] = np.asarray(inputs[n][l], np.float32).reshape(32, 128).T
        sp[:, o + SP_DNW:o + SP_DNW + 192] = np.asarray(inputs["dn_conv_w"][l], np.float32).reshape(48, 128, 4).transpose(1, 0, 2).reshape(128, 192)
        sp[:, o + SP_SCW:o + SP_SCW + 48] = np.asarray(inputs["sc_conv_w"][l], np.float32).reshape(16, 128, 3).transpose(1, 0, 2).reshape(128, 48)
        sp[:, o + SP_NG] = np.asarray(inputs["dn_norm_g"][l], np.float32)
        sp[0:16, o + SP_ALOG] = np.asarray(inputs["dn_a_log"][l], np.float32)
        sp[0:16, o + SP_DTB] = np.asarray(inputs["dn_dt_bias"][l], np.float32)
    return sp


def _consts():
    c = np.zeros((128, CS_N), np.float32)
    c[:, CS_ID:CS_ID + 128] = np.eye(128, dtype=np.float32)
    p = np.arange(64)[:, None]
    f = np.arange(64)[None, :]
    mti = np.where(f >= p, 0.0, NEG).astype(np.float32)
    m01 = np.where(f > p, 1.0, 0.0).astype(np.float32)
    mls = np.where(p > f, 0.0, NEG).astype(np.float32)
    i8 = np.eye(64, dtype=np.float32)
    c[0:64, CS_MTI:CS_MTI + 512] = np.tile(mti, (1, 8))
    c[0:64, CS_M01:CS_M01 + 512] = np.tile(m01, (1, 8))
    c[0:64, CS_MLS:CS_MLS + 512] = np.tile(mls, (1, 8))
    c[0:64, CS_I8:CS_I8 + 512] = np.tile(i8, (1, 8))
    return c


_NT = 4
_DBG = False
_LAST = {}


def kernel(**inputs):
    x = np.asarray(inputs["x"], np.float32)
    B, S, _ = x.shape
    NT = _NT
    nc = build_program(NT, _DBG)
    sp = _pack_small(inputs)
    cst = _consts()
    wnames = ["w_ffn1_in", "w_ffn1_out", "w_in", "w_dn_out", "w_sc_out", "w_o", "w_ffn2_in", "w_ffn2_out"]
    shared = {n: np.ascontiguousarray(np.asarray(inputs[n], np.float32)) for n in wnames}
    in_maps = []
    for b in range(NCORES):
        m = dict(shared)
        m["x"] = np.ascontiguousarray(x[b, :NT * T])
        m["sp"] = sp
        m["cst"] = cst
        in_maps.append(m)
    res = run_bass_kernel_spmd(nc, in_maps, core_ids=list(range(NCORES)))
    out = np.zeros((B, S, D), np.float32)
    for b in range(NCORES):
        out[b, :NT * T] = np.asarray(res.results[b]["out"], np.float32)
    if _DBG:
        _LAST["dbg"] = np.asarray(res.results[0]["dbg"], np.float32)
        _LAST["dbg2"] = np.asarray(res.results[0]["dbg2"]).astype(np.float32)
    return out
```

```python
import numpy as np
import concourse.bass as bass
import concourse.mybir as mybir
from concourse.bass_utils import run_bass_kernel_spmd

F32 = mybir.dt.float32
BF16 = mybir.dt.bfloat16
AF = mybir.ActivationFunctionType
ALU = mybir.AluOpType

NCORES = 4
D = 4096
KC = 32
DFF = 11008
FC = 86
T = 512
NH = 16
EPS = 1e-6
NEG = -30000.0
ENGS = ("sync", "act", "dve", "pe", "pool")
SEM_ROTATE = 30000
SLOT = 8192
NSLOT = 3
OPT_A = True
OPT_B = True

C_Q, C_K, C_V = 0, 2048, 4096
C_A, C_B = 6144, 6160
C_Z = 6176
C_SB, C_SC, C_SX = 8224, 10272, 12320
C_GA, C_GB = 14368, 18464

SP_G = {"ffn1_pre_g": 0, "ffn1_post_g": 32, "mix_pre_g": 64, "mix_post_g": 96,
        "ffn2_pre_g": 128, "ffn2_post_g": 160}
SP_DNW = 192
SP_SCW = 384
SP_NG = 432
SP_ALOG = 433
SP_DTB = 434
SP_L = 435
CS_ID = 0
CS_MTI = 128
CS_M01 = 640
CS_MLS = 1152
CS_I8 = 1664
CS_N = 2176


class Sem:
    def __init__(self, nc, name):
        self.h = nc.alloc_semaphore(name=name)
        self.val = 0


class Res:
    __slots__ = ("w", "r")

    def __init__(self):
        self.w = None
        self.r = []


class Prog:
    def __init__(self, nc):
        self.nc = nc
        self.q = {e: [] for e in ENGS}
        self.esem = {}
        self.esem_ids = {}
        self.nsem = 0
        self.planning = False
        self.fence_toks = []
        self.last = {}
        self.dmasems = []

    def newsem(self, name):
        self.nsem += 1
        s = Sem(self.nc, f"{name}_{self.nsem}")
        return s

    def _engine_sig(self, eng):
        s = self.esem.get(eng)
        if s is None or s.val >= SEM_ROTATE:
            s = self.newsem("e" + eng)
            self.esem[eng] = s
            self.esem_ids.setdefault(eng, set()).add(id(s))
        s.val += 1
        return (s, s.val)

    def emit(self, eng, fn, reads=(), writes=(), dmasem=None):
        if self.planning:
            return None
        waits = list(self.fence_toks)
        rawset = set((id(s), v) for (s, v) in self.fence_toks)
        for r in reads:
            if r.w is not None:
                waits.append(r.w)
                rawset.add((id(r.w[0]), r.w[1]))
        for w in writes:
            if w.w is not None:
                waits.append(w.w)
            waits.extend(w.r)
        if dmasem is not None:
            dmasem.val += 16
            tok = (dmasem, dmasem.val)
            inc = 16
        else:
            tok = self._engine_sig(eng)
            inc = 1
        best = {}
        mine = self.esem_ids.setdefault(eng, set())
        for (s, v) in waits:
            if id(s) in mine and (id(s), v) not in rawset:
                continue
            if id(s) not in best or best[id(s)][1] < v:
                best[id(s)] = (s, v)
        self.q[eng].append((fn, list(best.values()), tok, inc))
        self.last[id(tok[0])] = tok
        for r in reads:
            r.r.append(tok)
            if len(r.r) > 48:
                r.r = r.r[-48:]
        for w in writes:
            w.w = tok
            w.r = []
        return tok

    def fence(self):
        if self.planning:
            return
        self.fence_toks = list(self.last.values())

    def final_wait(self, engname):
        self.q[engname].append((None, list(self.last.values()), None, 0))

    def run_block(self):
        nc = self.nc
        P = self
        with nc.Block() as block:
            def mk(engname):
                def f(eng):
                    waited = {}
                    for (fn, waits, tok, inc) in P.q[engname]:
                        for (s, v) in waits:
                            if waited.get(id(s), 0) >= v:
                                continue
                            eng.wait_ge(s.h, v)
                            waited[id(s)] = v
                        if fn is None:
                            continue
                        ins = fn(eng)
                        ins.then_inc(tok[0].h, inc)
                return f
            block.sync(mk("sync"))
            block.scalar(mk("act"))
            block.vector(mk("dve"))
            block.tensor(mk("pe"))
            block.gpsimd(mk("pool"))


class WeightStream:
    def __init__(self, prog):
        self.p = prog
        self.buf = prog.nc.alloc_sbuf_tensor("wring", [128, NSLOT * SLOT], BF16)
        self.sems = [prog.newsem(f"w{i}") for i in range(NSLOT)]
        self.res = [Res() for _ in range(NSLOT)]
        self.plan = []
        self.issued = 0
        self.consumed = 0

    def _issue(self, idx):
        s = idx % NSLOT
        base = s * SLOT
        first = True
        tok = None
        for (src, off, kn, cols) in self.plan[idx]:
            dst = self.buf[:, base + off: base + off + kn * cols].rearrange("p (k c) -> p k c", c=cols)
            tok = self.p.emit("pool", (lambda eng, d=dst, sr=src: eng.dma_start(out=d, in_=sr)),
                              writes=[self.res[s]] if first else [], dmasem=self.sems[s])
            first = False
        self.res[s].w = tok
        self.res[s].r = []

    def prefetch(self):
        while self.issued < len(self.plan) and self.issued < self.consumed + NSLOT:
            self._issue(self.issued)
            self.issued += 1

    def get(self, pieces):
        if self.p.planning:
            self.plan.append(pieces)
            return 0, None
        idx = self.consumed
        self.prefetch()
        s = idx % NSLOT
        return s * SLOT, self.res[s]

    def done(self):
        if self.p.planning:
            return
        self.consumed += 1
        self.prefetch()


class Buf:
    def __init__(self, ap):
        self.ap = ap
        self.res = Res()


def build_program(NT, dbg=False):
    nc = bass.Bass("TRN2", target_bir_lowering=False)
    NTOK = NT * T
    L = 2

    def din(name, shape):
        return nc.dram_tensor(name, list(shape), F32, kind="ExternalInput").ap()

    x = din("x", [NTOK, D])
    W = {
        "w_ffn1_in": din("w_ffn1_in", [L, D, 2 * DFF]), "w_ffn1_out": din("w_ffn1_out", [L, DFF, D]),
        "w_in": din("w_in", [L, D, 22560]), "w_dn_out": din("w_dn_out", [L, 2048, D]),
        "w_sc_out": din("w_sc_out", [L, 2048, D]), "w_o": din("w_o", [L, D, D]),
        "w_ffn2_in": din("w_ffn2_in", [L, D, 2 * DFF]), "w_ffn2_out": din("w_ffn2_out", [L, DFF, D]),
    }
    spd = din("sp", [128, L * SP_L])
    cstd = din("cst", [128, CS_N])
    out = nc.dram_tensor("out", [NTOK, D], F32, kind="ExternalOutput").ap()
    dbgd = nc.dram_tensor("dbg", [6, KC, 128, T], F32, kind="ExternalOutput").ap() if dbg else None
    dbg2 = nc.dram_tensor("dbg2", [128, 64 * T], BF16, kind="ExternalOutput").ap() if dbg else None
    hT = nc.dram_tensor("hT_scr", [KC, 128, NTOK], F32).ap()
    fT = nc.dram_tensor("fT_scr", [KC, 128, NTOK], F32).ap()
    hT_res = [[Res() for _ in range(KC)] for _ in range(NT)]
    fT_res = [[Res() for _ in range(KC)] for _ in range(NT)]

    P = Prog(nc)
    ws = WeightStream(P)

    def sb(name, shape, dt=F32):
        return Buf(nc.alloc_sbuf_tensor("sb_" + name, list(shape), dt))

    sp = sb("sp", [128, L * SP_L])
    ident = sb("ident", [128, 128])
    ones_bf = sb("ones_bf", [128, 128], BF16)
    ones_f = sb("ones_f", [128, 128])
    epsc = sb("epsc", [128, 1])
    onec = sb("onec", [128, 1])
    xn = sb("xn", [128, KC, T], BF16)
    NSTG = 8
    stg = [sb(f"stg{i}", [128, T]) for i in range(NSTG)]
    rstd = [sb(f"rstd{i}", [128, T]) for i in range(2)]
    ARENA = 96 * 1024
    arena = nc.alloc_sbuf_tensor("arena", [128, ARENA // 4], F32)
    arena_off = [0]

    def carve(nbytes, dt=F32):
        o = arena_off[0]
        arena_off[0] += (nbytes + 31) // 32 * 32
        assert arena_off[0] <= ARENA, arena_off[0]
        a = arena[:, o // 4:(o + nbytes) // 4]
        if dt == BF16:
            a = a.bitcast(BF16)
        return a

    hidden = Buf(carve(FC * T * 2, BF16).rearrange("p (f t) -> p f t", t=T))
    arena_off[0] = 0
    ogT = Buf(carve(NH * T * 2, BF16).rearrange("p (f t) -> p f t", t=T))
    Sst = [Buf(carve(512)) for _ in range(NH)]
    Stmp = [Buf(carve(512)) for _ in range(2)]
    mk_mti = Buf(carve(2048)); mk_m01 = Buf(carve(2048)); mk_mls = Buf(carve(2048)); mk_i8 = Buf(carve(2048))
    halo_q = Buf(carve(48 * 3 * 4).rearrange("p (c w) -> p c w", w=3))
    halo_s = Buf(carve(16 * 2 * 4).rearrange("p (c w) -> p c w", w=2))
    pf = {n: Buf(carve(512).rearrange("p (c h) -> p c h", h=16)) for n in ("gcP", "btP", "bgP", "kdP")}
    selb = Buf(carve(512))
    raw = [Buf(carve(520 * 4)) for _ in range(2)]
    ff = {n: Buf(carve(2048)) for n in ("gF", "btF")}
    dr_mark = arena_off[0]
    mT = Buf(carve(KC * T * 2, BF16).rearrange("p (f t) -> p f t", t=T))
    scT = Buf(carve(NH * T * 2, BF16).rearrange("p (f t) -> p f t", t=T))
    arena_off[0] = dr_mark
    qT = Buf(carve(2048)); kT = Buf(carve(2048)); vT = Buf(carve(2048)); zs = Buf(carve(2048))
    kP = Buf(carve(4096).rearrange("p (c d) -> p c d", d=128))
    kbgP = Buf(carve(4096).rearrange("p (c d) -> p c d", d=128))
    vP = Buf(carve(4096).rearrange("p (c d) -> p c d", d=128))
    uP = vP
    gcB = Buf(carve(2048)); btB = Buf(carve(2048)); egB = Buf(carve(2048))
    dTm = Buf(carve(2048)); lT = Buf(carve(2048)); lM = Buf(carve(2048)); attnT = Buf(carve(2048))
    PTx = Buf(carve(2048))
    TTb = [Buf(carve(2048)) for _ in range(2)]
    wT = Buf(carve(2048))
    vnew = [Buf(carve(512)) for _ in range(2)]
    ff["gF2"] = TTb[0]; ff["egF"] = TTb[1]; ff["bgF"] = dTm; ff["kdF"] = PTx
    print("arena used (mixer)", arena_off[0], "of", ARENA)

    banks = [Buf(nc.alloc_psum_tensor(f"ps{i}", [128, T], F32)) for i in range(8)]
    bank_ctr = [0]
    stg_ctr = [0]

    bank_pool = [[0, 1, 2, 3, 4, 5]]

    def getbanks(n):
        r = []
        for _ in range(n):
            pool = bank_pool[0]
            r.append(banks[pool[bank_ctr[0] % len(pool)]])
            bank_ctr[0] += 1
        return r

    def getstg():
        b = stg[stg_ctr[0] % NSTG]
        stg_ctr[0] += 1
        return b

    def E(eng, fn, r=(), w=()):
        return P.emit(eng, fn, reads=[b.res for b in r], writes=[b.res for b in w])

    ld_sems = [P.newsem(f"ld{i}") for i in range(NSTG)]
    st_sems = [P.newsem(f"st{i}") for i in range(NSTG)]
    misc_sem = P.newsem("misc")

    def v3(buf, c=64):
        return buf.ap.rearrange("p (a b) -> p a b", b=c)

    def job(groups, evac):
        allbanks = []
        for (pieces, K, rhs_fn) in groups:
            nb_tot = sum(nc_ // 128 for (_, _, nc_) in pieces)
            gb = getbanks(nb_tot) if not P.planning else [None] * nb_tot
            allbanks.extend(gb)
            colsum = sum(nc_ for (_, _, nc_) in pieces)
            kb = min(K, SLOT // colsum)
            k0 = 0
            while k0 < K:
                kn = min(kb, K - k0)
                tiles = []
                off = 0
                offs = []
                for (mat, c0, ncols) in pieces:
                    src = mat[k0 * 128:(k0 + kn) * 128, c0:c0 + ncols].rearrange("(k p) n -> p k n", p=128)
                    tiles.append((src, off, kn, ncols))
                    offs.append(off)
                    off += kn * ncols
                base, wres = ws.get(tiles)
                if not P.planning:
                    bi = 0
                    for pi, (mat, c0, ncols) in enumerate(pieces):
                        for b in range(ncols // 128):
                            bank = gb[bi]
                            bi += 1

                            def pe_fn(e, base=base, o=offs[pi], ncols=ncols, b=b, kn=kn, k0=k0, K=K,
                                      bank=bank, rhs_fn=rhs_fn):
                                ins = None
                                for kk in range(kn):
                                    a = base + o + kk * ncols + b * 128
                                    ins = e.matmul(bank.ap[:, :], lhsT=ws.buf[:, a:a + 128], rhs=rhs_fn(k0 + kk),
                                                   start=(k0 + kk == 0), stop=(k0 + kk == K - 1))
                                return ins
                            rres = [wres] + list(rhs_fn.res)
                            if k0 == 0:
                                P.emit("pe", pe_fn, reads=rres, writes=[bank.res])
                            else:
                                tok = P.emit("pe", pe_fn, reads=rres)
                                bank.res.w = tok
                ws.done()
                bg.run(1)
                k0 += kn
        if not P.planning:
            evac(allbanks)

    def rhs_of(buf):
        def f(k):
            return buf.ap[:, k, :]
        f.res = [buf.res]
        return f

    def tokslice(t):
        return slice(t * T, (t + 1) * T)

    def init_pass(t):
        xv = x[t * T:(t + 1) * T, :].rearrange("(tb p) d -> p tb d", p=128)
        for c in range(KC):
            si = stg_ctr[0] % NSTG
            s_in = getstg()
            P.emit("sync", lambda e, s_in=s_in, c=c: e.dma_start(
                out=s_in.ap.rearrange("p (tb d) -> p tb d", d=128), in_=xv[:, :, c * 128:(c + 1) * 128]),
                writes=[s_in.res], dmasem=ld_sems[si])
            (bk,) = getbanks(1)

            def pe_fn(e, s_in=s_in, bk=bk):
                ins = None
                for tb in range(4):
                    ins = e.transpose(bk.ap[:, tb * 128:(tb + 1) * 128], s_in.ap[:, tb * 128:(tb + 1) * 128], ident.ap[:, :])
                return ins
            E("pe", pe_fn, r=[s_in, ident], w=[bk])
            so = stg_ctr[0] % NSTG
            s_out = getstg()
            E("act", lambda e, s_out=s_out, bk=bk: e.activation(out=s_out.ap[:, :], in_=bk.ap[:, :], func=AF.Copy),
              r=[bk], w=[s_out])
            P.emit("sync", lambda e, s_out=s_out, c=c: e.dma_start(out=hT[c, :, tokslice(t)], in_=s_out.ap[:, :]),
                   reads=[s_out.res], writes=[hT_res[t][c]], dmasem=st_sems[so])

    def final_pass(t):
        ov = out[t * T:(t + 1) * T, :].rearrange("(tb p) d -> p tb d", p=128)
        for c in range(KC):
            si = stg_ctr[0] % NSTG
            s_in = getstg()
            P.emit("sync", lambda e, s_in=s_in, c=c: e.dma_start(out=s_in.ap[:, :], in_=hT[c, :, tokslice(t)]),
                   reads=[hT_res[t][c]], writes=[s_in.res], dmasem=ld_sems[si])
            (bk,) = getbanks(1)

            def pe_fn(e, s_in=s_in, bk=bk):
                ins = None
                for tb in range(4):
                    ins = e.transpose(bk.ap[:, tb * 128:(tb + 1) * 128], s_in.ap[:, tb * 128:(tb + 1) * 128], ident.ap[:, :])
                return ins
            E("pe", pe_fn, r=[s_in, ident], w=[bk])
            so = stg_ctr[0] % NSTG
            s_out = getstg()
            E("act", lambda e, s_out=s_out, bk=bk: e.activation(out=s_out.ap[:, :], in_=bk.ap[:, :], func=AF.Copy),
              r=[bk], w=[s_out])
            P.emit("sync", lambda e, s_out=s_out, c=c: e.dma_start(
                out=ov[:, :, c * 128:(c + 1) * 128], in_=s_out.ap.rearrange("p (tb d) -> p tb d", d=128)),
                reads=[s_out.res], dmasem=st_sems[so])

    def rstd_from_bank(bk, dst, scale):
        E("act", lambda e: e.activation(out=dst.ap[:, :], in_=bk.ap[:, :], func=AF.Sqrt, bias=epsc.ap[:, 0:1], scale=scale),
          r=[bk, epsc], w=[dst])
        E("dve", lambda e: e.reciprocal(out=dst.ap[:, :], in_=dst.ap[:, :]), r=[dst], w=[dst])

    nsq = [sb(f"nsq{i}", [128, T], BF16) for i in range(4)]

    class NormParts:
        def __init__(self, l, t, gname):
            self.t = t
            self.g0 = l * SP_L + SP_G[gname]
            self.sbk = banks[6]

        def pre(self, p):
            t, g0 = self.t, self.g0
            for j in range(4):
                c = p * 4 + j
                si = stg_ctr[0] % NSTG
                hs = getstg()
                P.emit("sync", lambda e, hs=hs, c=c: e.dma_start(out=hs.ap[:, :], in_=hT[c, :, tokslice(t)]),
                       reads=[hT_res[t][c]], writes=[hs.res], dmasem=ld_sems[si])
                E("act", lambda e, hs=hs, c=c: e.activation(out=xn.ap[:, c, :], in_=hs.ap[:, :], func=AF.Copy,
                                                           scale=sp.ap[:, g0 + c:g0 + c + 1]),
                  r=[hs, sp], w=[xn])
                E("act", lambda e, hs=hs, j=j: e.activation(out=nsq[j].ap[:, :], in_=hs.ap[:, :], func=AF.Square),
                  r=[hs], w=[nsq[j]])

        def stat(self, p):
            sbk = self.sbk
            for j in range(4):
                c = p * 4 + j
                tokm = P.emit("pe", lambda e, j=j, c=c: e.matmul(sbk.ap[:, :], lhsT=ones_bf.ap[:, :], rhs=nsq[j].ap[:, :],
                                                               start=(c == 0), stop=(c == KC - 1)),
                              reads=[nsq[j].res, ones_bf.res], writes=[sbk.res] if c == 0 else [])
                if c > 0 and tokm is not None:
                    sbk.res.w = tokm

        def finish(self):
            rs = rstd[0]
            rstd_from_bank(self.sbk, rs, 1.0 / D)
            for c in range(KC):
                E("dve", lambda e, c=c: e.tensor_tensor(out=xn.ap[:, c, :], in0=xn.ap[:, c, :], in1=rs.ap[:, :], op=ALU.mult),
                  r=[xn, rs], w=[xn])

        def all(self):
            for p in range(8):
                self.pre(p)
                self.stat(p)
            self.finish()

    class BG:
        def __init__(self):
            self.q = []

        def run(self, n):
            if P.planning:
                return
            for _ in range(n):
                if not self.q:
                    return
                self.q.pop(0)[1]()

        def drain(self):
            self.run(len(self.q))

        def has_tile(self, t):
            return any(tt == t for (tt, _) in self.q)

    bg = BG()

    class OutStats:
        def __init__(self, t):
            self.t = t
            self.sbk = banks[7]
            self.n = 0

        def flush(self):
            pass

        def evac(self, c, bk):
            so = stg_ctr[0] % NSTG
            fs = getstg()
            E("act", lambda e, fs=fs, bk=bk: e.activation(out=fs.ap[:, :], in_=bk.ap[:, :], func=AF.Copy), r=[bk], w=[fs])
            P.emit("sync", lambda e, fs=fs, c=c: e.dma_start(out=fT[c, :, tokslice(self.t)], in_=fs.ap[:, :]),
                   reads=[fs.res], writes=[fT_res[self.t][c]], dmasem=st_sems[so])
            sq = getstg()
            sqv = sq.ap[:, 0:T // 2].bitcast(BF16)
            E("dve", lambda e, sqv=sqv, fs=fs: e.tensor_tensor(out=sqv, in0=fs.ap[:, :], in1=fs.ap[:, :], op=ALU.mult),
              r=[fs], w=[sq])
            n = self.n
            tokm = P.emit("pe", lambda e, sqv=sqv, n=n: e.matmul(self.sbk.ap[:, :], lhsT=ones_bf.ap[:, :], rhs=sqv,
                                                              start=(n == 0), stop=(n == KC - 1)),
                          reads=[sq.res, ones_bf.res], writes=[self.sbk.res] if n == 0 else [])
            if n > 0 and tokm is not None:
                self.sbk.res.w = tokm
            self.n += 1

    def update_pass(l, t, gname, coef, ost):
        if P.planning:
            return
        bg.drain()
        g0 = l * SP_L + SP_G[gname]
        rs = rstd[1]
        rstd_from_bank(ost.sbk, rs, 1.0 / D)

        def task(c):
            si = stg_ctr[0] % NSTG
            hs = getstg()
            P.emit("sync", lambda e, hs=hs, c=c: e.dma_start(out=hs.ap[:, :], in_=hT[c, :, tokslice(t)]),
                   reads=[hT_res[t][c]], writes=[hs.res], dmasem=ld_sems[si])
            sj = stg_ctr[0] % NSTG
            fs = getstg()
            P.emit("sync", lambda e, fs=fs, c=c: e.dma_start(out=fs.ap[:, :], in_=fT[c, :, tokslice(t)]),
                   reads=[fT_res[t][c]], writes=[fs.res], dmasem=ld_sems[sj])
            E("dve", lambda e, fs=fs, c=c: e.scalar_tensor_tensor(out=fs.ap[:, :], in0=fs.ap[:, :],
                                                                 scalar=sp.ap[:, g0 + c:g0 + c + 1], in1=rs.ap[:, :],
                                                                 op0=ALU.mult, op1=ALU.mult),
              r=[fs, sp, rs], w=[fs])
            E("dve", lambda e, fs=fs, hs=hs: e.scalar_tensor_tensor(out=hs.ap[:, :], in0=fs.ap[:, :], scalar=float(coef),
                                                                   in1=hs.ap[:, :], op0=ALU.mult, op1=ALU.add),
              r=[fs, hs], w=[hs])
            P.emit("sync", lambda e, hs=hs, c=c: e.dma_start(out=hT[c, :, tokslice(t)], in_=hs.ap[:, :]),
                   reads=[hs.res], writes=[hT_res[t][c]], dmasem=st_sems[si])
        for c in range(KC):
            bg.q.append((t, (lambda c=c: task(c))))

    def ffn(l, t, which):
        w_in_ = W[f"w_ffn{which}_in"][l]
        w_out_ = W[f"w_ffn{which}_out"][l]
        rx = rhs_of(xn)
        for fp in range(FC // 2):
            def evac(bks, fp=fp):
                for fi in range(2):
                    f = fp * 2 + fi
                    st_ = getstg()
                    E("act", lambda e, st_=st_, bk=bks[fi]: e.activation(out=st_.ap[:, :], in_=bk.ap[:, :], func=AF.Silu),
                      r=[bks[fi]], w=[st_])
                    E("dve", lambda e, st_=st_, bk=bks[2 + fi], f=f: e.tensor_tensor(out=hidden.ap[:, f, :], in0=bk.ap[:, :],
                                                                                     in1=st_.ap[:, :], op=ALU.mult),
                      r=[bks[2 + fi], st_], w=[hidden])
            job([([(w_in_, fp * 256, 256), (w_in_, DFF + fp * 256, 256)], KC, rx)], evac)
        ost = OutStats(t)
        rh = rhs_of(hidden)
        nn = next_norm[0]
        for og in range(8):
            def evac2(bks, og=og):
                for j in range(4):
                    ost.evac(og * 4 + j, bks[j])
            if nn is not None and not P.planning:
                nn.pre(og)
            job([([(w_out_, og * 512, 512)], FC, rh)], evac2)
            if nn is not None and not P.planning:
                nn.stat(og)
        if nn is not None and not P.planning:
            nn.finish()
        update_pass(l, t, f"ffn{which}_post_g", 0.5, ost)

    def mixer_setup(l):
        o = l * SP_L
        for (b, c0) in ((mk_mti, CS_MTI), (mk_m01, CS_M01), (mk_mls, CS_MLS), (mk_i8, CS_I8)):
            P.emit("sync", lambda e, b=b, c0=c0: e.dma_start(out=b.ap[0:64, :], in_=cstd[0:64, c0:c0 + 512]),
                   writes=[b.res], dmasem=misc_sem)
        for h in range(NH):
            E("dve", lambda e, h=h: e.memset(Sst[h].ap[:, :], 0.0), w=[Sst[h]])
        E("dve", lambda e: e.memset(halo_q.ap[:, :, :], 0.0), w=[halo_q])
        E("dve", lambda e: e.memset(halo_s.ap[:, :, :], 0.0), w=[halo_s])
        E("act", lambda e: e.activation(out=sp.ap[0:16, o + SP_ALOG:o + SP_ALOG + 1], in_=sp.ap[0:16, o + SP_ALOG:o + SP_ALOG + 1],
                                        func=AF.Exp), r=[sp], w=[sp])
        E("dve", lambda e: e.tensor_scalar(out=sp.ap[0:16, o + SP_ALOG:o + SP_ALOG + 1], in0=sp.ap[0:16, o + SP_ALOG:o + SP_ALOG + 1],
                                           scalar1=-1.0, scalar2=None, op0=ALU.mult), r=[sp], w=[sp])
        P.fence()

    def conv_taps(rawb, dst, wcol0, width):
        E("dve", lambda e: e.tensor_scalar(out=dst.ap[:, :], in0=rawb.ap[:, 0:T], scalar1=sp.ap[:, wcol0:wcol0 + 1],
                                           scalar2=None, op0=ALU.mult), r=[rawb, sp], w=[dst])
        for j in range(1, width):
            E("dve", lambda e, j=j: e.scalar_tensor_tensor(out=dst.ap[:, :], in0=rawb.ap[:, j:j + T],
                                                          scalar=sp.ap[:, wcol0 + j:wcol0 + j + 1], in1=dst.ap[:, :],
                                                          op0=ALU.mult, op1=ALU.add), r=[rawb, sp, dst], w=[dst])

    raw_ctr = [0]

    def mixer(l, t):
        o = l * SP_L
        w_in_ = W["w_in"][l]
        rx = rhs_of(xn)

        def evac_ab(bks):
            pass
        abk = getbanks(2) if not P.planning else [None, None]
        tiles = [(w_in_[:, C_A:C_A + 32].rearrange("(k p) n -> p k n", p=128), 0, KC, 32)]
        base, wres = ws.get(tiles)
        if not P.planning:
            for which in range(2):
                def pe_fn(e, which=which, base=base):
                    ins = None
                    for k in range(KC):
                        a = base + k * 32 + which * 16
                        ins = e.matmul(abk[which].ap[0:16, :], lhsT=ws.buf[:, a:a + 16], rhs=xn.ap[:, k, :],
                                       start=(k == 0), stop=(k == KC - 1))
                    return ins
                P.emit("pe", pe_fn, reads=[wres, xn.res], writes=[abk[which].res])
        ws.done()
        if not P.planning:
            gF, gF2, btF, egF, bgF, kdF = (ff[n] for n in ("gF", "gF2", "btF", "egF", "bgF", "kdF"))
            R16 = slice(0, 16)
            E("act", lambda e: e.activation(out=gF.ap[R16, :], in_=abk[0].ap[R16, :], func=AF.Exp,
                                            bias=sp.ap[R16, o + SP_DTB:o + SP_DTB + 1]), r=[abk[0], sp], w=[gF])
            E("act", lambda e: e.activation(out=gF.ap[R16, :], in_=gF.ap[R16, :], func=AF.Ln, bias=onec.ap[R16, 0:1]),
              r=[gF, onec], w=[gF])
            E("dve", lambda e: e.tensor_scalar(out=gF.ap[R16, :], in0=gF.ap[R16, :], scalar1=sp.ap[R16, o + SP_ALOG:o + SP_ALOG + 1],
                                               scalar2=None, op0=ALU.mult), r=[gF, sp], w=[gF])
            E("act", lambda e: e.activation(out=btF.ap[R16, :], in_=abk[1].ap[R16, :], func=AF.Sigmoid), r=[abk[1]], w=[btF])
            src, dst = gF, gF2
            s = 1
            while s < 64:
                sv, dv = v3(src), v3(dst)
                E("dve", lambda e, sv=sv, dv=dv, s=s: e.tensor_copy(out=dv[R16, :, 0:s], in_=sv[R16, :, 0:s]), r=[src], w=[dst])
                E("dve", lambda e, sv=sv, dv=dv, s=s: e.tensor_tensor(out=dv[R16, :, s:64], in0=sv[R16, :, s:64],
                                                                     in1=sv[R16, :, 0:64 - s], op=ALU.add), r=[src], w=[dst])
                src, dst = dst, src
                s *= 2
            gcF = src
            E("act", lambda e: e.activation(out=egF.ap[R16, :], in_=gcF.ap[R16, :], func=AF.Exp), r=[gcF], w=[egF])
            E("dve", lambda e: e.tensor_tensor(out=bgF.ap[R16, :], in0=btF.ap[R16, :], in1=egF.ap[R16, :], op=ALU.mult),
              r=[btF, egF], w=[bgF])
            E("dve", lambda e: e.tensor_tensor(out=v3(kdF)[R16, :, :], in0=v3(gcF)[R16, :, 63:64].to_broadcast([16, 8, 64]),
                                               in1=v3(gcF)[R16, :, :], op=ALU.subtract), r=[gcF], w=[kdF])
            E("act", lambda e: e.activation(out=kdF.ap[R16, :], in_=kdF.ap[R16, :], func=AF.Exp), r=[kdF], w=[kdF])
            for (srcb, name) in ((gcF, "gcP"), (btF, "btP"), (bgF, "bgP"), (kdF, "kdP")):
                (bk,) = getbanks(1)

                def pe_fn(e, srcb=srcb, bk=bk):
                    ins = None
                    for c in range(8):
                        ins = e.transpose(bk.ap[0:64, c * 16:(c + 1) * 16], srcb.ap[R16, c * 64:(c + 1) * 64], ident.ap[0:16, 0:16])
                    return ins
                E("pe", pe_fn, r=[srcb, ident], w=[bk])
                E("act", lambda e, bk=bk, name=name: e.activation(out=pf[name].ap.rearrange("p c h -> p (c h)")[0:64, :],
                                                                 in_=bk.ap[0:64, 0:128], func=AF.Copy), r=[bk], w=[pf[name]])
        else:
            gcF = btF = None

        spk_gen = [None]

        def spk():
            if spk_gen[0] is not None:
                next(spk_gen[0], None)

        def proj_gen(h, bks):
            cols = (C_Q + h * 128, C_K + h * 128, C_V + h * 128, C_Z + h * 128)
            for k0 in (0, 16):
                tiles = []
                for pi, c0 in enumerate(cols):
                    src = w_in_[k0 * 128:(k0 + 16) * 128, c0:c0 + 128].rearrange("(k p) n -> p k n", p=128)
                    tiles.append((src, pi * 2048, 16, 128))
                base, wres = ws.get(tiles)
                if not P.planning:
                    for pi in range(4):
                        def pe_fn(e, base=base, pi=pi, k0=k0, bank=bks[pi]):
                            ins = None
                            for kk in range(16):
                                a = base + pi * 2048 + kk * 128
                                ins = e.matmul(bank.ap[:, :], lhsT=ws.buf[:, a:a + 128], rhs=xn.ap[:, k0 + kk, :],
                                               start=(k0 + kk == 0), stop=(k0 + kk == KC - 1))
                            return ins
                        if k0 == 0:
                            P.emit("pe", pe_fn, reads=[wres, xn.res], writes=[bks[pi].res])
                        else:
                            tok = P.emit("pe", pe_fn, reads=[wres, xn.res])
                            bks[pi].res.w = tok
                        yield
                ws.done()
                bg.run(1)

        def exhaust(g):
            if g is not None:
                for _ in g:
                    pass

        if True:
            def evac_head(bks, h):
                R64 = slice(0, 64)
                outs = (qT, kT, vT)
                for xi in range(3):
                    rb = raw[raw_ctr[0] % 2]
                    raw_ctr[0] += 1
                    hc = xi * 16 + h
                    E("act", lambda e, rb=rb, bk=bks[xi]: e.activation(out=rb.ap[:, 3:3 + T], in_=bk.ap[:, :], func=AF.Copy),
                      r=[bks[xi]], w=[rb])
                    E("dve", lambda e, rb=rb, hc=hc: e.tensor_copy(out=rb.ap[:, 0:3], in_=halo_q.ap[:, hc, :]), r=[halo_q], w=[rb])
                    E("dve", lambda e, rb=rb, hc=hc: e.tensor_copy(out=halo_q.ap[:, hc, :], in_=rb.ap[:, T:T + 3]), r=[rb], w=[halo_q])
                    conv_taps(rb, outs[xi], o + SP_DNW + hc * 4, 4)
                    E("act", lambda e, d=outs[xi]: e.activation(out=d.ap[:, :], in_=d.ap[:, :], func=AF.Silu), r=[outs[xi]], w=[outs[xi]])
                E("act", lambda e: e.activation(out=zs.ap[:, :], in_=bks[3].ap[:, :], func=AF.Silu), r=[bks[3]], w=[zs])
                for (d, sc) in ((qT, 128 ** -0.5), (kT, 1.0)):
                    sq = getstg()
                    E("act", lambda e, d=d, sq=sq: e.activation(out=sq.ap[:, :], in_=d.ap[:, :], func=AF.Square), r=[d], w=[sq])
                    (bk,) = getbanks(1)
                    E("pe", lambda e, sq=sq, bk=bk: e.matmul(bk.ap[:, :], lhsT=ones_f.ap[:, :], rhs=sq.ap[:, :], start=True, stop=True),
                      r=[sq, ones_f], w=[bk])
                    rstd_from_bank(bk, sq, 1.0)
                    E("dve", lambda e, d=d, sq=sq, sc=sc: e.scalar_tensor_tensor(out=d.ap[:, :], in0=d.ap[:, :], scalar=float(sc),
                                                                                in1=sq.ap[:, :], op0=ALU.mult, op1=ALU.mult),
                      r=[d, sq], w=[d])
                E("dve", lambda e: e.tensor_copy(out=selb.ap[0:16, :], in_=ident.ap[0:16, h:h + 1].to_broadcast([16, 128])),
                  r=[ident], w=[selb])
                for (srcb, dstb, fn) in ((gcF, gcB, AF.Copy), (gcF, egB, AF.Exp), (btF, btB, AF.Copy)):
                    (bk,) = getbanks(1)
                    E("pe", lambda e, srcb=srcb, bk=bk: e.matmul(bk.ap[:, :], lhsT=selb.ap[0:16, :], rhs=srcb.ap[0:16, :],
                                                                start=True, stop=True), r=[selb, srcb], w=[bk])
                    E("act", lambda e, bk=bk, dstb=dstb, fn=fn: e.activation(out=dstb.ap[:, :], in_=bk.ap[:, :], func=fn),
                      r=[bk], w=[dstb])
                for (srcb, dstb) in ((kT, kP), (vT, vP)):
                    bk2 = getbanks(2)
                    for half in range(2):
                        def pe_fn(e, srcb=srcb, bk=bk2[half], half=half):
                            ins = None
                            for cc in range(4):
                                c = half * 4 + cc
                                ins = e.transpose(bk.ap[0:64, cc * 128:(cc + 1) * 128], srcb.ap[:, c * 64:(c + 1) * 64], ident.ap[:, :])
                            return ins
                        E("pe", pe_fn, r=[srcb, ident], w=[bk2[half]])
                        E("act", lambda e, bk=bk2[half], dstb=dstb, half=half: e.activation(
                            out=dstb.ap[R64, half * 4:(half + 1) * 4, :].rearrange("p c d -> p (c d)"), in_=bk.ap[R64, :], func=AF.Copy),
                          r=[bk2[half]], w=[dstb])
                def bc(pn):
                    return pf[pn].ap[R64, :, h:h + 1].to_broadcast([64, 8, 128])
                E("dve", lambda e: e.tensor_tensor(out=kbgP.ap[R64, :, :], in0=kP.ap[R64, :, :], in1=bc("bgP"), op=ALU.mult),
                  r=[kP, pf["bgP"]], w=[kbgP])
                E("dve", lambda e: e.tensor_tensor(out=kP.ap[R64, :, :], in0=kP.ap[R64, :, :], in1=bc("kdP"), op=ALU.mult),
                  r=[kP, pf["kdP"]], w=[kP])
                E("dve", lambda e: e.tensor_tensor(out=vP.ap[R64, :, :], in0=vP.ap[R64, :, :], in1=bc("btP"), op=ALU.mult),
                  r=[vP, pf["btP"]], w=[vP])
                bkk, bqk = getbanks(2)

                def pe_kk(e):
                    ins = None
                    for c in range(8):
                        cs = slice(c * 64, (c + 1) * 64)
                        ins = e.matmul(bkk.ap[R64, cs], lhsT=kT.ap[:, cs], rhs=kT.ap[:, cs], start=True, stop=True)
                    return ins

                def pe_qk(e):
                    ins = None
                    for c in range(8):
                        cs = slice(c * 64, (c + 1) * 64)
                        ins = e.matmul(bqk.ap[R64, cs], lhsT=kT.ap[:, cs], rhs=qT.ap[:, cs], start=True, stop=True)
                    return ins
                E("pe", pe_kk, r=[kT], w=[bkk])
                E("pe", pe_qk, r=[kT, qT], w=[bqk])
                gcPb = pf["gcP"].ap[R64, :, h:h + 1].to_broadcast([64, 8, 64])
                btPb = pf["btP"].ap[R64, :, h:h + 1].to_broadcast([64, 8, 64])
                E("dve", lambda e: e.tensor_tensor(out=v3(dTm)[R64, :, :], in0=v3(gcB)[R64, :, :], in1=gcPb, op=ALU.subtract),
                  r=[gcB, pf["gcP"]], w=[dTm])
                E("dve", lambda e: e.tensor_tensor(out=dTm.ap[R64, :], in0=dTm.ap[R64, :], in1=mk_mti.ap[R64, :], op=ALU.add),
                  r=[dTm, mk_mti], w=[dTm])
                E("act", lambda e: e.activation(out=dTm.ap[R64, :], in_=dTm.ap[R64, :], func=AF.Exp), r=[dTm], w=[dTm])
                E("dve", lambda e: e.tensor_tensor(out=attnT.ap[R64, :], in0=bqk.ap[R64, :], in1=dTm.ap[R64, :], op=ALU.mult),
                  r=[bqk, dTm], w=[attnT])
                E("dve", lambda e: e.tensor_tensor(out=lT.ap[R64, :], in0=bkk.ap[R64, :], in1=dTm.ap[R64, :], op=ALU.mult),
                  r=[bkk, dTm], w=[lT])
                E("dve", lambda e: e.tensor_tensor(out=lT.ap[R64, :], in0=lT.ap[R64, :], in1=mk_m01.ap[R64, :], op=ALU.mult),
                  r=[lT, mk_m01], w=[lT])
                E("dve", lambda e: e.tensor_tensor(out=lT.ap[R64, :], in0=lT.ap[R64, :], in1=btB.ap[R64, :], op=ALU.mult),
                  r=[lT, btB], w=[lT])
                E("dve", lambda e: e.tensor_tensor(out=v3(lM)[R64, :, :], in0=gcPb, in1=v3(gcB)[R64, :, :], op=ALU.subtract),
                  r=[gcB, pf["gcP"]], w=[lM])
                E("dve", lambda e: e.tensor_tensor(out=lM.ap[R64, :], in0=lM.ap[R64, :], in1=mk_mls.ap[R64, :], op=ALU.add),
                  r=[lM, mk_mls], w=[lM])
                E("act", lambda e: e.activation(out=lM.ap[R64, :], in_=lM.ap[R64, :], func=AF.Exp), r=[lM], w=[lM])
                E("dve", lambda e: e.tensor_tensor(out=v3(lM)[R64, :, :], in0=v3(lM)[R64, :, :], in1=btPb, op=ALU.mult),
                  r=[lM, pf["btP"]], w=[lM])
                E("dve", lambda e: e.tensor_tensor(out=lM.ap[R64, :], in0=lM.ap[R64, :], in1=bkk.ap[R64, :], op=ALU.mult),
                  r=[lM, bkk], w=[lM])
                E("dve", lambda e: e.tensor_tensor(out=TTb[0].ap[R64, :], in0=mk_i8.ap[R64, :], in1=lT.ap[R64, :], op=ALU.subtract),
                  r=[mk_i8, lT], w=[TTb[0]])
                Pc, PTc, TTc = lM, lT, TTb[0]
                for s in range(5):
                    Pn = (dTm, lM)[s % 2]
                    PTn = (PTx, lT)[s % 2]
                    TTn = TTb[(s + 1) % 2]
                    ba, bb_, bc_ = getbanks(3)

                    def pe_sq(e, Pc=Pc, PTc=PTc, bk=ba):
                        ins = None
                        for c in range(8):
                            cs = slice(c * 64, (c + 1) * 64)
                            ins = e.matmul(bk.ap[R64, cs], lhsT=PTc.ap[R64, cs], rhs=Pc.ap[R64, cs], start=True, stop=True)
                        return ins
                    E("pe", pe_sq, r=[Pc, PTc], w=[ba])
                    E("act", lambda e, Pn=Pn, bk=ba: e.activation(out=Pn.ap[R64, :], in_=bk.ap[R64, :], func=AF.Copy), r=[ba], w=[Pn])
                    if s < 4:
                        def pe_sqT(e, Pc=Pc, PTc=PTc, bk=bb_):
                            ins = None
                            for c in range(8):
                                cs = slice(c * 64, (c + 1) * 64)
                                ins = e.matmul(bk.ap[R64, cs], lhsT=Pc.ap[R64, cs], rhs=PTc.ap[R64, cs], start=True, stop=True)
                            return ins
                        E("pe", pe_sqT, r=[Pc, PTc], w=[bb_])
                        E("dve", lambda e, PTn=PTn, bk=bb_: e.tensor_copy(out=PTn.ap[R64, :], in_=bk.ap[R64, :]), r=[bb_], w=[PTn])

                    def pe_tt(e, Pn=Pn, TTc=TTc, bk=bc_):
                        ins = None
                        for c in range(8):
                            cs = slice(c * 64, (c + 1) * 64)
                            ins = e.matmul(bk.ap[R64, cs], lhsT=Pn.ap[R64, cs], rhs=TTc.ap[R64, cs], start=True, stop=True)
                        return ins
                    E("pe", pe_tt, r=[Pn, TTc], w=[bc_])
                    spk()
                    E("dve", lambda e, TTn=TTn, TTc=TTc, bk=bc_: e.tensor_tensor(out=TTn.ap[R64, :], in0=bk.ap[R64, :], in1=TTc.ap[R64, :],
                                                                               op=ALU.add), r=[bc_, TTc], w=[TTn])
                    Pc, PTc, TTc = Pn, PTn, TTn
                TT = TTc
                (bw,) = getbanks(1)

                def pe_w(e):
                    ins = None
                    for c in range(8):
                        cs = slice(c * 64, (c + 1) * 64)
                        ins = e.matmul(bw.ap[:, cs], lhsT=kbgP.ap[R64, c, :], rhs=TT.ap[R64, cs], start=True, stop=True)
                    return ins
                E("pe", pe_w, r=[kbgP, TT], w=[bw])
                E("act", lambda e: e.activation(out=wT.ap[:, :], in_=bw.ap[:, :], func=AF.Copy), r=[bw], w=[wT])
                bu = getbanks(2)
                for half in range(2):
                    def pe_u(e, half=half):
                        ins = None
                        for cc in range(4):
                            c = half * 4 + cc
                            ins = e.matmul(bu[half].ap[R64, cc * 128:(cc + 1) * 128], lhsT=TT.ap[R64, c * 64:(c + 1) * 64],
                                           rhs=vP.ap[R64, c, :], start=True, stop=True)
                        return ins
                    E("pe", pe_u, r=[TT, vP], w=[bu[half]])
                    E("act", lambda e, half=half: e.activation(out=uP.ap[R64, half * 4:(half + 1) * 4, :].rearrange("p c d -> p (c d)"),
                                                              in_=bu[half].ap[R64, :], func=AF.Copy), r=[bu[half]], w=[uP])
                E("dve", lambda e: e.tensor_tensor(out=qT.ap[:, :], in0=qT.ap[:, :], in1=egB.ap[:, :], op=ALU.mult), r=[qT, egB], w=[qT])
                oT = vT
                seq = [Sst[h], Stmp[0], Stmp[1]]
                cur = Sst[h]
                for c in range(8):
                    cs = slice(c * 64, (c + 1) * 64)
                    nxt = Sst[h] if c == 7 else Stmp[c % 2]
                    vn = vnew[c % 2]
                    b1, b2, b3 = getbanks(3)
                    E("pe", lambda e, b1=b1, cs=cs, cur=cur: e.matmul(b1.ap[R64, 0:128], lhsT=wT.ap[:, cs], rhs=cur.ap[:, :], start=True, stop=True),
                      r=[wT, cur], w=[b1])
                    E("dve", lambda e, b1=b1, vn=vn, c=c: e.tensor_tensor(out=vn.ap[R64, :], in0=uP.ap[R64, c, :], in1=b1.ap[R64, 0:128],
                                                                         op=ALU.subtract), r=[uP, b1], w=[vn])

                    def pe_o(e, b2=b2, cs=cs, cur=cur, vn=vn):
                        e.matmul(b2.ap[:, 0:64], lhsT=cur.ap[:, :], rhs=qT.ap[:, cs], start=True, stop=False)
                        return e.matmul(b2.ap[:, 0:64], lhsT=vn.ap[R64, :], rhs=attnT.ap[R64, cs], start=False, stop=True)
                    E("pe", pe_o, r=[cur, qT, vn, attnT], w=[b2])
                    E("act", lambda e, b2=b2, cs=cs: e.activation(out=oT.ap[:, cs], in_=b2.ap[:, 0:64], func=AF.Copy), r=[b2], w=[oT])
                    E("pe", lambda e, b3=b3, c=c, vn=vn: e.matmul(b3.ap[:, 0:128], lhsT=kP.ap[R64, c, :], rhs=vn.ap[R64, :], start=True, stop=True),
                      r=[kP, vn], w=[b3])
                    spk()
                    E("dve", lambda e, b3=b3, cur=cur, nxt=nxt, c=c: e.scalar_tensor_tensor(
                        out=nxt.ap[:, :], in0=cur.ap[:, :], scalar=egB.ap[:, c * 64 + 63:c * 64 + 64], in1=b3.ap[:, 0:128],
                        op0=ALU.mult, op1=ALU.add), r=[cur, egB, b3], w=[nxt])
                    cur = nxt
                sq = getstg()
                E("act", lambda e, sq=sq: e.activation(out=sq.ap[:, :], in_=oT.ap[:, :], func=AF.Square), r=[oT], w=[sq])
                (bk,) = getbanks(1)
                E("pe", lambda e, sq=sq, bk=bk: e.matmul(bk.ap[:, :], lhsT=ones_f.ap[:, :], rhs=sq.ap[:, :], start=True, stop=True),
                  r=[sq, ones_f], w=[bk])
                rstd_from_bank(bk, sq, 1.0 / 128)
                E("dve", lambda e, sq=sq: e.scalar_tensor_tensor(out=sq.ap[:, :], in0=oT.ap[:, :], scalar=sp.ap[:, o + SP_NG:o + SP_NG + 1],
                                                                in1=sq.ap[:, :], op0=ALU.mult, op1=ALU.mult), r=[oT, sp, sq], w=[sq])
                E("dve", lambda e, sq=sq: e.tensor_tensor(out=ogT.ap[:, h, :], in0=sq.ap[:, :], in1=zs.ap[:, :], op=ALU.mult),
                  r=[sq, zs], w=[ogT])
        if OPT_A:
            hb = banks[0:4]
            bank_pool[0] = [4, 5, 6]
            exhaust(proj_gen(0, hb))
            for h in range(NH):
                g = proj_gen(h + 1, hb) if h + 1 < NH else None
                spk_gen[0] = g
                if not P.planning:
                    evac_head(hb, h)
                spk_gen[0] = None
                exhaust(g)
            bank_pool[0] = [0, 1, 2, 3, 4, 5]
        else:
            for h in range(NH):
                job([([(w_in_, C_Q + h * 128, 128), (w_in_, C_K + h * 128, 128), (w_in_, C_V + h * 128, 128),
                       (w_in_, C_Z + h * 128, 128)], KC, rx)], (lambda bks, h=h: evac_head(bks, h)))

        P.fence()
        for c in range(16):
            def evac_sc(bks, c=c):
                tmp = getstg()
                rb = raw[raw_ctr[0] % 2]
                raw_ctr[0] += 1
                E("act", lambda e, tmp=tmp: e.activation(out=tmp.ap[:, :], in_=bks[1].ap[:, :], func=AF.Copy), r=[bks[1]], w=[tmp])
                E("dve", lambda e, tmp=tmp, rb=rb: e.tensor_tensor(out=rb.ap[:, 2:2 + T], in0=tmp.ap[:, :], in1=bks[2].ap[:, :], op=ALU.mult),
                  r=[tmp, bks[2]], w=[rb])
                E("dve", lambda e, rb=rb: e.tensor_copy(out=rb.ap[:, 0:2], in_=halo_s.ap[:, c, :]), r=[halo_s], w=[rb])
                E("dve", lambda e, rb=rb: e.tensor_copy(out=halo_s.ap[:, c, :], in_=rb.ap[:, T:T + 2]), r=[rb], w=[halo_s])
                conv_taps(rb, tmp, o + SP_SCW + c * 3, 3)
                E("dve", lambda e, tmp=tmp: e.tensor_tensor(out=scT.ap[:, c, :], in0=tmp.ap[:, :], in1=bks[0].ap[:, :], op=ALU.mult),
                  r=[tmp, bks[0]], w=[scT])
            job([([(w_in_, C_SB + c * 128, 128), (w_in_, C_SC + c * 128, 128), (w_in_, C_SX + c * 128, 128)], KC, rx)], evac_sc)

        rog, rsc = rhs_of(ogT), rhs_of(scT)
        if OPT_B:
            for cp in range(KC // 2):
                c0 = cp * 2

                def evac_g(bks):
                    for j in range(4):
                        E("act", lambda e, j=j: e.activation(out=nsq[j].ap[:, :], in_=bks[j].ap[:, :], func=AF.Sigmoid),
                          r=[bks[j]], w=[nsq[j]])

                def evac_y(bks, c0=c0):
                    for j in range(2):
                        t1, t2 = getstg(), getstg()
                        E("dve", lambda e, t1=t1, j=j: e.tensor_tensor(out=t1.ap[:, :], in0=bks[j].ap[:, :], in1=nsq[j].ap[:, :], op=ALU.mult),
                          r=[bks[j], nsq[j]], w=[t1])
                        E("dve", lambda e, t2=t2, j=j: e.tensor_tensor(out=t2.ap[:, :], in0=bks[2 + j].ap[:, :], in1=nsq[2 + j].ap[:, :], op=ALU.mult),
                          r=[bks[2 + j], nsq[2 + j]], w=[t2])
                        E("dve", lambda e, t1=t1, t2=t2, j=j: e.tensor_tensor(out=mT.ap[:, c0 + j, :], in0=t1.ap[:, :], in1=t2.ap[:, :], op=ALU.add),
                          r=[t1, t2], w=[mT])
                job([([(w_in_, C_GA + c0 * 128, 256), (w_in_, C_GB + c0 * 128, 256)], KC, rx)], evac_g)
                job([([(W["w_dn_out"][l], c0 * 128, 256)], 16, rog),
                     ([(W["w_sc_out"][l], c0 * 128, 256)], 16, rsc)], evac_y)
        else:
            for c in range(KC):
                def evac_m(bks, c=c):
                    t1, t2 = getstg(), getstg()
                    E("act", lambda e, t1=t1: e.activation(out=t1.ap[:, :], in_=bks[0].ap[:, :], func=AF.Sigmoid), r=[bks[0]], w=[t1])
                    E("dve", lambda e, t1=t1: e.tensor_tensor(out=t1.ap[:, :], in0=t1.ap[:, :], in1=bks[2].ap[:, :], op=ALU.mult),
                      r=[t1, bks[2]], w=[t1])
                    E("act", lambda e, t2=t2: e.activation(out=t2.ap[:, :], in_=bks[1].ap[:, :], func=AF.Sigmoid), r=[bks[1]], w=[t2])
                    E("dve", lambda e, t2=t2: e.tensor_tensor(out=t2.ap[:, :], in0=t2.ap[:, :], in1=bks[3].ap[:, :], op=ALU.mult),
                      r=[t2, bks[3]], w=[t2])
                    E("dve", lambda e, t1=t1, t2=t2: e.tensor_tensor(out=mT.ap[:, c, :], in0=t1.ap[:, :], in1=t2.ap[:, :], op=ALU.add),
                      r=[t1, t2], w=[mT])
                job([([(w_in_, C_GA + c * 128, 128), (w_in_, C_GB + c * 128, 128)], KC, rx),
                     ([(W["w_dn_out"][l], c * 128, 128)], 16, rog),
                     ([(W["w_sc_out"][l], c * 128, 128)], 16, rsc)], evac_m)
        if dbg2 is not None and not P.planning and l == 0 and t == 0:
            P.emit("sync", lambda e: e.dma_start(out=dbg2[:, 0:16 * T], in_=ogT.ap.rearrange("p f t -> p (f t)")), reads=[ogT.res], dmasem=dbg_sem)
            P.emit("sync", lambda e: e.dma_start(out=dbg2[:, 16 * T:32 * T], in_=scT.ap.rearrange("p f t -> p (f t)")), reads=[scT.res], dmasem=dbg_sem)
            P.emit("sync", lambda e: e.dma_start(out=dbg2[:, 32 * T:64 * T], in_=mT.ap.rearrange("p f t -> p (f t)")), reads=[mT.res], dmasem=dbg_sem)
        ost = OutStats(t)
        rm = rhs_of(mT)
        nn = next_norm[0]
        for og in range(8):
            def evac2(bks, og=og):
                for j in range(4):
                    ost.evac(og * 4 + j, bks[j])
            if nn is not None and not P.planning:
                nn.pre(og)
            job([([(W["w_o"][l], og * 512, 512)], KC, rm)], evac2)
            if nn is not None and not P.planning:
                nn.stat(og)
        if nn is not None and not P.planning:
            nn.finish()
        update_pass(l, t, "mix_post_g", 1.0, ost)
        P.fence()

    dbg_sem = P.newsem("dbg")
    dbg_ctr = [0]

    def dump():
        if P.planning or dbgd is None:
            return
        i = dbg_ctr[0]
        dbg_ctr[0] += 1
        for c in range(KC):
            P.emit("sync", lambda e, c=c, i=i: e.dma_start(out=dbgd[i, c], in_=hT[c, :, 0:T]),
                   reads=[hT_res[0][c]], dmasem=dbg_sem)

    next_norm = [None]
    dump_after = set()

    def program():
        if not P.planning:
            P.emit("sync", lambda e: e.dma_start(out=sp.ap[:, :], in_=spd), writes=[sp.res], dmasem=misc_sem)
            P.emit("sync", lambda e: e.dma_start(out=ident.ap[:, :], in_=cstd[:, CS_ID:CS_ID + 128]), writes=[ident.res], dmasem=misc_sem)
            E("dve", lambda e: e.memset(ones_bf.ap[:, :], 1.0), w=[ones_bf])
            E("dve", lambda e: e.memset(ones_f.ap[:, :], 1.0), w=[ones_f])
            E("dve", lambda e: e.memset(epsc.ap[:, :], EPS), w=[epsc])
            E("dve", lambda e: e.memset(onec.ap[:, :], 1.0), w=[onec])
            P.fence()
            for t in range(NT):
                init_pass(t)
        units = []
        for l in range(L):
            units += [("ffn1", l, t) for t in range(NT)] + [("mix", l, t) for t in range(NT)] + [("ffn2", l, t) for t in range(NT)]
        gn = {"ffn1": "ffn1_pre_g", "mix": "mix_pre_g", "ffn2": "ffn2_pre_g"}
        pending_norm = None
        for ui, (kind, l, t) in enumerate(units):
            if ui > 0 and units[ui - 1][0] != kind:
                bg.drain()
                P.fence()
                if kind == "mix" and not P.planning:
                    mixer_setup(l)
            elif kind == "mix" and ui == 0 and not P.planning:
                mixer_setup(l)
            if not P.planning:
                if pending_norm is None:
                    if bg.has_tile(t):
                        bg.drain()
                    NormParts(l, t, gn[kind]).all()
                pending_norm = None
                nxt = units[ui + 1] if ui + 1 < len(units) else None
                if nxt is not None and nxt[2] != t and not bg.has_tile(nxt[2]):
                    next_norm[0] = NormParts(nxt[1], nxt[2], gn[nxt[0]])
                    pending_norm = next_norm[0]
                else:
                    next_norm[0] = None
            if kind == "mix":
                mixer(l, t)
            else:
                ffn(l, t, 1 if kind == "ffn1" else 2)
            if kind == "mix" and ui in dump_after:
                pass
            if ui + 1 == len(units) or units[ui + 1][0] != kind:
                if not P.planning:
                    bg.drain()
                dump()
        if not P.planning:
            bg.drain()
        if not P.planning:
            for t in range(NT):
                final_pass(t)
            P.final_wait("sync")

    P.planning = True
    program()
    P.planning = False
    bank_ctr[0] = 0
    stg_ctr[0] = 0
    raw_ctr[0] = 0
    program()
    assert ws.consumed == len(ws.plan), (ws.consumed, len(ws.plan))
    P.run_block()
    print("ops", {k: len(v) for k, v in P.q.items()}, "sems", P.nsem, "wtiles", len(ws.plan))
    return nc


def _pack_small(inputs):
    L = 2
    sp = np.zeros((128, L * SP_L), np.float32)
    for l in range(L):
        o = l * SP_L
        for n, c0 in SP_G.items():
            sp[:, o + c0:o + c0 + 32] = np.asarray(inputs[n][l], np.float32).reshape(32, 128).T
        sp[:, o + SP_DNW:o + SP_DNW + 192] = np.asarray(inputs["dn_conv_w"][l], np.float32).reshape(48, 128, 4).transpose(1, 0, 2).reshape(128, 192)
        sp[:, o + SP_SCW:o + SP_SCW + 48] = np.asarray(inputs["sc_conv_w"][l], np.float32).reshape(16, 128, 3).transpose(1, 0, 2).reshape(128, 48)
        sp[:, o + SP_NG] = np.asarray(inputs["dn_norm_g"][l], np.float32)
        sp[0:16, o + SP_ALOG] = np.asarray(inputs["dn_a_log"][l], np.float32)
        sp[0:16, o + SP_DTB] = np.asarray(inputs["dn_dt_bias"][l], np.float32)
    return sp


def _consts():
    c = np.zeros((128, CS_N), np.float32)
    c[:, CS_ID:CS_ID + 128] = np.eye(128, dtype=np.float32)
    p = np.arange(64)[:, None]
    f = np.arange(64)[None, :]
    mti = np.where(f >= p, 0.0, NEG).astype(np.float32)
    m01 = np.where(f > p, 1.0, 0.0).astype(np.float32)
    mls = np.where(p > f, 0.0, NEG).astype(np.float32)
    i8 = np.eye(64, dtype=np.float32)
    c[0:64, CS_MTI:CS_MTI + 512] = np.tile(mti, (1, 8))
    c[0:64, CS_M01:CS_M01 + 512] = np.tile(m01, (1, 8))
    c[0:64, CS_MLS:CS_MLS + 512] = np.tile(mls, (1, 8))
    c[0:64, CS_I8:CS_I8 + 512] = np.tile(i8, (1, 8))
    return c


_NT = 4
_DBG = False
_LAST = {}


def kernel(**inputs):
    x = np.asarray(inputs["x"], np.float32)
    B, S, _ = x.shape
    NT = _NT
    nc = build_program(NT, _DBG)
    sp = _pack_small(inputs)
    cst = _consts()
    wnames = ["w_ffn1_in", "w_ffn1_out", "w_in", "w_dn_out", "w_sc_out", "w_o", "w_ffn2_in", "w_ffn2_out"]
    shared = {n: np.ascontiguousarray(np.asarray(inputs[n], np.float32)) for n in wnames}
    in_maps = []
    for b in range(NCORES):
        m = dict(shared)
        m["x"] = np.ascontiguousarray(x[b, :NT * T])
        m["sp"] = sp
        m["cst"] = cst
        in_maps.append(m)
    res = run_bass_kernel_spmd(nc, in_maps, core_ids=list(range(NCORES)))
    out = np.zeros((B, S, D), np.float32)
    for b in range(NCORES):
        out[b, :NT * T] = np.asarray(res.results[b]["out"], np.float32)
    if _DBG:
        _LAST["dbg"] = np.asarray(res.results[0]["dbg"], np.float32)
        _LAST["dbg2"] = np.asarray(res.results[0]["dbg2"]).astype(np.float32)
    return out
```

```python
import numpy as np
import concourse.bass as bass
import concourse.mybir as mybir
from concourse.bass_utils import run_bass_kernel_spmd

F32 = mybir.dt.float32
BF16 = mybir.dt.bfloat16
AF = mybir.ActivationFunctionType
ALU = mybir.AluOpType

NCORES = 4
D = 4096
KC = 32
DFF = 11008
FC = 86
T = 512
NH = 16
EPS = 1e-6
NEG = -30000.0
ENGS = ("sync", "act", "dve", "pe", "pool")
SEM_ROTATE = 30000
SLOT = 8192
NSLOT = 3
OPT_A = True
OPT_B = True

C_Q, C_K, C_V = 0, 2048, 4096
C_A, C_B = 6144, 6160
C_Z = 6176
C_SB, C_SC, C_SX = 8224, 10272, 12320
C_GA, C_GB = 14368, 18464

SP_G = {"ffn1_pre_g": 0, "ffn1_post_g": 32, "mix_pre_g": 64, "mix_post_g": 96,
        "ffn2_pre_g": 128, "ffn2_post_g": 160}
SP_DNW = 192
SP_SCW = 384
SP_NG = 432
SP_ALOG = 433
SP_DTB = 434
SP_L = 435
CS_ID = 0
CS_MTI = 128
CS_M01 = 640
CS_MLS = 1152
CS_I8 = 1664
CS_N = 2176


class Sem:
    def __init__(self, nc, name):
        self.h = nc.alloc_semaphore(name=name)
        self.val = 0


class Res:
    __slots__ = ("w", "r")

    def __init__(self):
        self.w = None
        self.r = []


class Prog:
    def __init__(self, nc):
        self.nc = nc
        self.q = {e: [] for e in ENGS}
        self.esem = {}
        self.esem_ids = {}
        self.nsem = 0
        self.planning = False
        self.fence_toks = []
        self.last = {}
        self.dmasems = []

    def newsem(self, name):
        self.nsem += 1
        s = Sem(self.nc, f"{name}_{self.nsem}")
        return s

    def _engine_sig(self, eng):
        s = self.esem.get(eng)
        if s is None or s.val >= SEM_ROTATE:
            s = self.newsem("e" + eng)
            self.esem[eng] = s
            self.esem_ids.setdefault(eng, set()).add(id(s))
        s.val += 1
        return (s, s.val)

    def emit(self, eng, fn, reads=(), writes=(), dmasem=None):
        if self.planning:
            return None
        waits = list(self.fence_toks)
        rawset = set((id(s), v) for (s, v) in self.fence_toks)
        for r in reads:
            if r.w is not None:
                waits.append(r.w)
                rawset.add((id(r.w[0]), r.w[1]))
        for w in writes:
            if w.w is not None:
                waits.append(w.w)
            waits.extend(w.r)
        if dmasem is not None:
            dmasem.val += 16
            tok = (dmasem, dmasem.val)
            inc = 16
        else:
            tok = self._engine_sig(eng)
            inc = 1
        best = {}
        mine = self.esem_ids.setdefault(eng, set())
        for (s, v) in waits:
            if id(s) in mine and (id(s), v) not in rawset:
                continue
            if id(s) not in best or best[id(s)][1] < v:
                best[id(s)] = (s, v)
        self.q[eng].append((fn, list(best.values()), tok, inc))
        self.last[id(tok[0])] = tok
        for r in reads:
            r.r.append(tok)
            if len(r.r) > 48:
                r.r = r.r[-48:]
        for w in writes:
            w.w = tok
            w.r = []
        return tok

    def fence(self):
        if self.planning:
            return
        self.fence_toks = list(self.last.values())

    def final_wait(self, engname):
        self.q[engname].append((None, list(self.last.values()), None, 0))

    def run_block(self):
        nc = self.nc
        P = self
        with nc.Block() as block:
            def mk(engname):
                def f(eng):
                    waited = {}
                    for (fn, waits, tok, inc) in P.q[engname]:
                        for (s, v) in waits:
                            if waited.get(id(s), 0) >= v:
                                continue
                            eng.wait_ge(s.h, v)
                            waited[id(s)] = v
                        if fn is None:
                            continue
                        ins = fn(eng)
                        ins.then_inc(tok[0].h, inc)
                return f
            block.sync(mk("sync"))
            block.scalar(mk("act"))
            block.vector(mk("dve"))
            block.tensor(mk("pe"))
            block.gpsimd(mk("pool"))


class WeightStream:
    def __init__(self, prog):
        self.p = prog
        self.buf = prog.nc.alloc_sbuf_tensor("wring", [128, NSLOT * SLOT], BF16)
        self.sems = [prog.newsem(f"w{i}") for i in range(NSLOT)]
        self.res = [Res() for _ in range(NSLOT)]
        self.plan = []
        self.issued = 0
        self.consumed = 0

    def _issue(self, idx):
        s = idx % NSLOT
        base = s * SLOT
        first = True
        tok = None
        for (src, off, kn, cols) in self.plan[idx]:
            dst = self.buf[:, base + off: base + off + kn * cols].rearrange("p (k c) -> p k c", c=cols)
            tok = self.p.emit("pool", (lambda eng, d=dst, sr=src: eng.dma_start(out=d, in_=sr)),
                              writes=[self.res[s]] if first else [], dmasem=self.sems[s])
            first = False
        self.res[s].w = tok
        self.res[s].r = []

    def prefetch(self):
        while self.issued < len(self.plan) and self.issued < self.consumed + NSLOT:
            self._issue(self.issued)
            self.issued += 1

    def get(self, pieces):
        if self.p.planning:
            self.plan.append(pieces)
            return 0, None
        idx = self.consumed
        self.prefetch()
        s = idx % NSLOT
        return s * SLOT, self.res[s]

    def done(self):
        if self.p.planning:
            return
        self.consumed += 1
        self.prefetch()


class Buf:
    def __init__(self, ap):
        self.ap = ap
        self.res = Res()


def build_program(NT, dbg=False):
    nc = bass.Bass("TRN2", target_bir_lowering=False)
    NTOK = NT * T
    L = 2

    def din(name, shape):
        return nc.dram_tensor(name, list(shape), F32, kind="ExternalInput").ap()

    x = din("x", [NTOK, D])
    W = {
        "w_ffn1_in": din("w_ffn1_in", [L, D, 2 * DFF]), "w_ffn1_out": din("w_ffn1_out", [L, DFF, D]),
        "w_in": din("w_in", [L, D, 22560]), "w_dn_out": din("w_dn_out", [L, 2048, D]),
        "w_sc_out": din("w_sc_out", [L, 2048, D]), "w_o": din("w_o", [L, D, D]),
        "w_ffn2_in": din("w_ffn2_in", [L, D, 2 * DFF]), "w_ffn2_out": din("w_ffn2_out", [L, DFF, D]),
    }
    spd = din("sp", [128, L * SP_L])
    cstd = din("cst", [128, CS_N])
    out = nc.dram_tensor("out", [NTOK, D], F32, kind="ExternalOutput").ap()
    dbgd = nc.dram_tensor("dbg", [6, KC, 128, T], F32, kind="ExternalOutput").ap() if dbg else None
    dbg2 = nc.dram_tensor("dbg2", [128, 64 * T], BF16, kind="ExternalOutput").ap() if dbg else None
    hT = nc.dram_tensor("hT_scr", [KC, 128, NTOK], F32).ap()
    fT = nc.dram_tensor("fT_scr", [KC, 128, NTOK], F32).ap()
    hT_res = [[Res() for _ in range(KC)] for _ in range(NT)]
    fT_res = [[Res() for _ in range(KC)] for _ in range(NT)]

    P = Prog(nc)
    ws = WeightStream(P)

    def sb(name, shape, dt=F32):
        return Buf(nc.alloc_sbuf_tensor("sb_" + name, list(shape), dt))

    sp = sb("sp", [128, L * SP_L])
    ident = sb("ident", [128, 128])
    ones_bf = sb("ones_bf", [128, 128], BF16)
    ones_f = sb("ones_f", [128, 128])
    epsc = sb("epsc", [128, 1])
    onec = sb("onec", [128, 1])
    xn = sb("xn", [128, KC, T], BF16)
    NSTG = 8
    stg = [sb(f"stg{i}", [128, T]) for i in range(NSTG)]
    rstd = [sb(f"rstd{i}", [128, T]) for i in range(2)]
    ARENA = 96 * 1024
    arena = nc.alloc_sbuf_tensor("arena", [128, ARENA // 4], F32)
    arena_off = [0]

    def carve(nbytes, dt=F32):
        o = arena_off[0]
        arena_off[0] += (nbytes + 31) // 32 * 32
        assert arena_off[0] <= ARENA, arena_off[0]
        a = arena[:, o // 4:(o + nbytes) // 4]
        if dt == BF16:
            a = a.bitcast(BF16)
        return a

    hidden = Buf(carve(FC * T * 2, BF16).rearrange("p (f t) -> p f t", t=T))
    arena_off[0] = 0
    ogT = Buf(carve(NH * T * 2, BF16).rearrange("p (f t) -> p f t", t=T))
    Sst = [Buf(carve(512)) for _ in range(NH)]
    Stmp = [Buf(carve(512)) for _ in range(2)]
    mk_mti = Buf(carve(2048)); mk_m01 = Buf(carve(2048)); mk_mls = Buf(carve(2048)); mk_i8 = Buf(carve(2048))
    halo_q = Buf(carve(48 * 3 * 4).rearrange("p (c w) -> p c w", w=3))
    halo_s = Buf(carve(16 * 2 * 4).rearrange("p (c w) -> p c w", w=2))
    pf = {n: Buf(carve(512).rearrange("p (c h) -> p c h", h=16)) for n in ("gcP", "btP", "bgP", "kdP")}
    selb = Buf(carve(512))
    raw = [Buf(carve(520 * 4)) for _ in range(2)]
    ff = {n: Buf(carve(2048)) for n in ("gF", "btF")}
    dr_mark = arena_off[0]
    mT = Buf(carve(KC * T * 2, BF16).rearrange("p (f t) -> p f t", t=T))
    scT = Buf(carve(NH * T * 2, BF16).rearrange("p (f t) -> p f t", t=T))
    arena_off[0] = dr_mark
    qT = Buf(carve(2048)); kT = Buf(carve(2048)); vT = Buf(carve(2048)); zs = Buf(carve(2048))
    kP = Buf(carve(4096).rearrange("p (c d) -> p c d", d=128))
    kbgP = Buf(carve(4096).rearrange("p (c d) -> p c d", d=128))
    vP = Buf(carve(4096).rearrange("p (c d) -> p c d", d=128))
    uP = vP
    gcB = Buf(carve(2048)); btB = Buf(carve(2048)); egB = Buf(carve(2048))
    dTm = Buf(carve(2048)); lT = Buf(carve(2048)); lM = Buf(carve(2048)); attnT = Buf(carve(2048))
    PTx = Buf(carve(2048))
    TTb = [Buf(carve(2048)) for _ in range(2)]
    wT = Buf(carve(2048))
    vnew = [Buf(carve(512)) for _ in range(2)]
    ff["gF2"] = TTb[0]; ff["egF"] = TTb[1]; ff["bgF"] = dTm; ff["kdF"] = PTx
    print("arena used (mixer)", arena_off[0], "of", ARENA)

    banks = [Buf(nc.alloc_psum_tensor(f"ps{i}", [128, T], F32)) for i in range(8)]
    bank_ctr = [0]
    stg_ctr = [0]

    bank_pool = [[0, 1, 2, 3, 4, 5]]

    def getbanks(n):
        r = []
        for _ in range(n):
            pool = bank_pool[0]
            r.append(banks[pool[bank_ctr[0] % len(pool)]])
            bank_ctr[0] += 1
        return r

    def getstg():
        b = stg[stg_ctr[0] % NSTG]
        stg_ctr[0] += 1
        return b

    def E(eng, fn, r=(), w=()):
        return P.emit(eng, fn, reads=[b.res for b in r], writes=[b.res for b in w])

    ld_sems = [P.newsem(f"ld{i}") for i in range(NSTG)]
    st_sems = [P.newsem(f"st{i}") for i in range(NSTG)]
    misc_sem = P.newsem("misc")

    def v3(buf, c=64):
        return buf.ap.rearrange("p (a b) -> p a b", b=c)

    def job(groups, evac):
        allbanks = []
        for (pieces, K, rhs_fn) in groups:
            nb_tot = sum(nc_ // 128 for (_, _, nc_) in pieces)
            gb = getbanks(nb_tot) if not P.planning else [None] * nb_tot
            allbanks.extend(gb)
            colsum = sum(nc_ for (_, _, nc_) in pieces)
            kb = min(K, SLOT // colsum)
            k0 = 0
            while k0 < K:
                kn = min(kb, K - k0)
                tiles = []
                off = 0
                offs = []
                for (mat, c0, ncols) in pieces:
                    src = mat[k0 * 128:(k0 + kn) * 128, c0:c0 + ncols].rearrange("(k p) n -> p k n", p=128)
                    tiles.append((src, off, kn, ncols))
                    offs.append(off)
                    off += kn * ncols
                base, wres = ws.get(tiles)
                if not P.planning:
                    bi = 0
                    for pi, (mat, c0, ncols) in enumerate(pieces):
                        for b in range(ncols // 128):
                            bank = gb[bi]
                            bi += 1

                            def pe_fn(e, base=base, o=offs[pi], ncols=ncols, b=b, kn=kn, k0=k0, K=K,
                                      bank=bank, rhs_fn=rhs_fn):
                                ins = None
                                for kk in range(kn):
                                    a = base + o + kk * ncols + b * 128
                                    ins = e.matmul(bank.ap[:, :], lhsT=ws.buf[:, a:a + 128], rhs=rhs_fn(k0 + kk),
                                                   start=(k0 + kk == 0), stop=(k0 + kk == K - 1))
                                return ins
                            rres = [wres] + list(rhs_fn.res)
                            if k0 == 0:
                                P.emit("pe", pe_fn, reads=rres, writes=[bank.res])
                            else:
                                tok = P.emit("pe", pe_fn, reads=rres)
                                bank.res.w = tok
                ws.done()
                bg.run(1)
                k0 += kn
        if not P.planning:
            evac(allbanks)

    def rhs_of(buf):
        def f(k):
            return buf.ap[:, k, :]
        f.res = [buf.res]
        return f

    def tokslice(t):
        return slice(t * T, (t + 1) * T)

    def init_pass(t):
        xv = x[t * T:(t + 1) * T, :].rearrange("(tb p) d -> p tb d", p=128)
        for c in range(KC):
            si = stg_ctr[0] % NSTG
            s_in = getstg()
            P.emit("sync", lambda e, s_in=s_in, c=c: e.dma_start(
                out=s_in.ap.rearrange("p (tb d) -> p tb d", d=128), in_=xv[:, :, c * 128:(c + 1) * 128]),
                writes=[s_in.res], dmasem=ld_sems[si])
            (bk,) = getbanks(1)

            def pe_fn(e, s_in=s_in, bk=bk):
                ins = None
                for tb in range(4):
                    ins = e.transpose(bk.ap[:, tb * 128:(tb + 1) * 128], s_in.ap[:, tb * 128:(tb + 1) * 128], ident.ap[:, :])
                return ins
            E("pe", pe_fn, r=[s_in, ident], w=[bk])
            so = stg_ctr[0] % NSTG
            s_out = getstg()
            E("act", lambda e, s_out=s_out, bk=bk: e.activation(out=s_out.ap[:, :], in_=bk.ap[:, :], func=AF.Copy),
              r=[bk], w=[s_out])
            P.emit("sync", lambda e, s_out=s_out, c=c: e.dma_start(out=hT[c, :, tokslice(t)], in_=s_out.ap[:, :]),
                   reads=[s_out.res], writes=[hT_res[t][c]], dmasem=st_sems[so])

    def final_pass(t):
        ov = out[t * T:(t + 1) * T, :].rearrange("(tb p) d -> p tb d", p=128)
        for c in range(KC):
            si = stg_ctr[0] % NSTG
            s_in = getstg()
            P.emit("sync", lambda e, s_in=s_in, c=c: e.dma_start(out=s_in.ap[:, :], in_=hT[c, :, tokslice(t)]),
                   reads=[hT_res[t][c]], writes=[s_in.res], dmasem=ld_sems[si])
            (bk,) = getbanks(1)

            def pe_fn(e, s_in=s_in, bk=bk):
                ins = None
                for tb in range(4):
                    ins = e.transpose(bk.ap[:, tb * 128:(tb + 1) * 128], s_in.ap[:, tb * 128:(tb + 1) * 128], ident.ap[:, :])
                return ins
            E("pe", pe_fn, r=[s_in, ident], w=[bk])
            so = stg_ctr[0] % NSTG
            s_out = getstg()
            E("act", lambda e, s_out=s_out, bk=bk: e.activation(out=s_out.ap[:, :], in_=bk.ap[:, :], func=AF.Copy),
              r=[bk], w=[s_out])
            P.emit("sync", lambda e, s_out=s_out, c=c: e.dma_start(
                out=ov[:, :, c * 128:(c + 1) * 128], in_=s_out.ap.rearrange("p (tb d) -> p tb d", d=128)),
                reads=[s_out.res], dmasem=st_sems[so])

    def rstd_from_bank(bk, dst, scale):
        E("act", lambda e: e.activation(out=dst.ap[:, :], in_=bk.ap[:, :], func=AF.Sqrt, bias=epsc.ap[:, 0:1], scale=scale),
          r=[bk, epsc], w=[dst])
        E("dve", lambda e: e.reciprocal(out=dst.ap[:, :], in_=dst.ap[:, :]), r=[dst], w=[dst])

    nsq = [sb(f"nsq{i}", [128, T], BF16) for i in range(4)]

    class NormParts:
        def __init__(self, l, t, gname):
            self.t = t
            self.g0 = l * SP_L + SP_G[gname]
            self.sbk = banks[6]

        def pre(self, p):
            t, g0 = self.t, self.g0
            for j in range(4):
                c = p * 4 + j
                si = stg_ctr[0] % NSTG
                hs = getstg()
                P.emit("sync", lambda e, hs=hs, c=c: e.dma_start(out=hs.ap[:, :], in_=hT[c, :, tokslice(t)]),
                       reads=[hT_res[t][c]], writes=[hs.res], dmasem=ld_sems[si])
                E("act", lambda e, hs=hs, c=c: e.activation(out=xn.ap[:, c, :], in_=hs.ap[:, :], func=AF.Copy,
                                                           scale=sp.ap[:, g0 + c:g0 + c + 1]),
                  r=[hs, sp], w=[xn])
                E("act", lambda e, hs=hs, j=j: e.activation(out=nsq[j].ap[:, :], in_=hs.ap[:, :], func=AF.Square),
                  r=[hs], w=[nsq[j]])

        def stat(self, p):
            sbk = self.sbk
            for j in range(4):
                c = p * 4 + j
                tokm = P.emit("pe", lambda e, j=j, c=c: e.matmul(sbk.ap[:, :], lhsT=ones_bf.ap[:, :], rhs=nsq[j].ap[:, :],
                                                               start=(c == 0), stop=(c == KC - 1)),
                              reads=[nsq[j].res, ones_bf.res], writes=[sbk.res] if c == 0 else [])
                if c > 0 and tokm is not None:
                    sbk.res.w = tokm

        def finish(self):
            rs = rstd[0]
            rstd_from_bank(self.sbk, rs, 1.0 / D)
            for c in range(KC):
                E("dve", lambda e, c=c: e.tensor_tensor(out=xn.ap[:, c, :], in0=xn.ap[:, c, :], in1=rs.ap[:, :], op=ALU.mult),
                  r=[xn, rs], w=[xn])

        def all(self):
            for p in range(8):
                self.pre(p)
                self.stat(p)
            self.finish()

    class BG:
        def __init__(self):
            self.q = []

        def run(self, n):
            if P.planning:
                return
            for _ in range(n):
                if not self.q:
                    return
                self.q.pop(0)[1]()

        def drain(self):
            self.run(len(self.q))

        def has_tile(self, t):
            return any(tt == t for (tt, _) in self.q)

    bg = BG()

    class OutStats:
        def __init__(self, t):
            self.t = t
            self.sbk = banks[7]
            self.n = 0

        def flush(self):
            pass

        def evac(self, c, bk):
            so = stg_ctr[0] % NSTG
            fs = getstg()
            E("act", lambda e, fs=fs, bk=bk: e.activation(out=fs.ap[:, :], in_=bk.ap[:, :], func=AF.Copy), r=[bk], w=[fs])
            P.emit("sync", lambda e, fs=fs, c=c: e.dma_start(out=fT[c, :, tokslice(self.t)], in_=fs.ap[:, :]),
                   reads=[fs.res], writes=[fT_res[self.t][c]], dmasem=st_sems[so])
            sq = getstg()
            sqv = sq.ap[:, 0:T // 2].bitcast(BF16)
            E("dve", lambda e, sqv=sqv, fs=fs: e.tensor_tensor(out=sqv, in0=fs.ap[:, :], in1=fs.ap[:, :], op=ALU.mult),
              r=[fs], w=[sq])
            n = self.n
            tokm = P.emit("pe", lambda e, sqv=sqv, n=n: e.matmul(self.sbk.ap[:, :], lhsT=ones_bf.ap[:, :], rhs=sqv,
                                                              start=(n == 0), stop=(n == KC - 1)),
                          reads=[sq.res, ones_bf.res], writes=[self.sbk.res] if n == 0 else [])
            if n > 0 and tokm is not None:
                self.sbk.res.w = tokm
            self.n += 1

    def update_pass(l, t, gname, coef, ost):
        if P.planning:
            return
        bg.drain()
        g0 = l * SP_L + SP_G[gname]
        rs = rstd[1]
        rstd_from_bank(ost.sbk, rs, 1.0 / D)

        def task(c):
            si = stg_ctr[0] % NSTG
            hs = getstg()
            P.emit("sync", lambda e, hs=hs, c=c: e.dma_start(out=hs.ap[:, :], in_=hT[c, :, tokslice(t)]),
                   reads=[hT_res[t][c]], writes=[hs.res], dmasem=ld_sems[si])
            sj = stg_ctr[0] % NSTG
            fs = getstg()
            P.emit("sync", lambda e, fs=fs, c=c: e.dma_start(out=fs.ap[:, :], in_=fT[c, :, tokslice(t)]),
                   reads=[fT_res[t][c]], writes=[fs.res], dmasem=ld_sems[sj])
            E("dve", lambda e, fs=fs, c=c: e.scalar_tensor_tensor(out=fs.ap[:, :], in0=fs.ap[:, :],
                                                                 scalar=sp.ap[:, g0 + c:g0 + c + 1], in1=rs.ap[:, :],
                                                                 op0=ALU.mult, op1=ALU.mult),
              r=[fs, sp, rs], w=[fs])
            E("dve", lambda e, fs=fs, hs=hs: e.scalar_tensor_tensor(out=hs.ap[:, :], in0=fs.ap[:, :], scalar=float(coef),
                                                                   in1=hs.ap[:, :], op0=ALU.mult, op1=ALU.add),
              r=[fs, hs], w=[hs])
            P.emit("sync", lambda e, hs=hs, c=c: e.dma_start(out=hT[c, :, tokslice(t)], in_=hs.ap[:, :]),
                   reads=[hs.res], writes=[hT_res[t][c]], dmasem=st_sems[si])
        for c in range(KC):
            bg.q.append((t, (lambda c=c: task(c))))

    def ffn(l, t, which):
        w_in_ = W[f"w_ffn{which}_in"][l]
        w_out_ = W[f"w_ffn{which}_out"][l]
        rx = rhs_of(xn)
        for fp in range(FC // 2):
            def evac(bks, fp=fp):
                for fi in range(2):
                    f = fp * 2 + fi
                    st_ = getstg()
                    E("act", lambda e, st_=st_, bk=bks[fi]: e.activation(out=st_.ap[:, :], in_=bk.ap[:, :], func=AF.Silu),
                      r=[bks[fi]], w=[st_])
                    E("dve", lambda e, st_=st_, bk=bks[2 + fi], f=f: e.tensor_tensor(out=hidden.ap[:, f, :], in0=bk.ap[:, :],
                                                                                     in1=st_.ap[:, :], op=ALU.mult),
                      r=[bks[2 + fi], st_], w=[hidden])
            job([([(w_in_, fp * 256, 256), (w_in_, DFF + fp * 256, 256)], KC, rx)], evac)
        ost = OutStats(t)
        rh = rhs_of(hidden)
        nn = next_norm[0]
        for og in range(8):
            def evac2(bks, og=og):
                for j in range(4):
                    ost.evac(og * 4 + j, bks[j])
            if nn is not None and not P.planning:
                nn.pre(og)
            job([([(w_out_, og * 512, 512)], FC, rh)], evac2)
            if nn is not None and not P.planning:
                nn.stat(og)
        if nn is not None and not P.planning:
            nn.finish()
        update_pass(l, t, f"ffn{which}_post_g", 0.5, ost)

    def mixer_setup(l):
        o = l * SP_L
        for (b, c0) in ((mk_mti, CS_MTI), (mk_m01, CS_M01), (mk_mls, CS_MLS), (mk_i8, CS_I8)):
            P.emit("sync", lambda e, b=b, c0=c0: e.dma_start(out=b.ap[0:64, :], in_=cstd[0:64, c0:c0 + 512]),
                   writes=[b.res], dmasem=misc_sem)
        for h in range(NH):
            E("dve", lambda e, h=h: e.memset(Sst[h].ap[:, :], 0.0), w=[Sst[h]])
        E("dve", lambda e: e.memset(halo_q.ap[:, :, :], 0.0), w=[halo_q])
        E("dve", lambda e: e.memset(halo_s.ap[:, :, :], 0.0), w=[halo_s])
        E("act", lambda e: e.activation(out=sp.ap[0:16, o + SP_ALOG:o + SP_ALOG + 1], in_=sp.ap[0:16, o + SP_ALOG:o + SP_ALOG + 1],
                                        func=AF.Exp), r=[sp], w=[sp])
        E("dve", lambda e: e.tensor_scalar(out=sp.ap[0:16, o + SP_ALOG:o + SP_ALOG + 1], in0=sp.ap[0:16, o + SP_ALOG:o + SP_ALOG + 1],
                                           scalar1=-1.0, scalar2=None, op0=ALU.mult), r=[sp], w=[sp])
        P.fence()

    def conv_taps(rawb, dst, wcol0, width):
        E("dve", lambda e: e.tensor_scalar(out=dst.ap[:, :], in0=rawb.ap[:, 0:T], scalar1=sp.ap[:, wcol0:wcol0 + 1],
                                           scalar2=None, op0=ALU.mult), r=[rawb, sp], w=[dst])
        for j in range(1, width):
            E("dve", lambda e, j=j: e.scalar_tensor_tensor(out=dst.ap[:, :], in0=rawb.ap[:, j:j + T],
                                                          scalar=sp.ap[:, wcol0 + j:wcol0 + j + 1], in1=dst.ap[:, :],
                                                          op0=ALU.mult, op1=ALU.add), r=[rawb, sp, dst], w=[dst])

    raw_ctr = [0]

    def mixer(l, t):
        o = l * SP_L
        w_in_ = W["w_in"][l]
        rx = rhs_of(xn)

        def evac_ab(bks):
            pass
        abk = getbanks(2) if not P.planning else [None, None]
        tiles = [(w_in_[:, C_A:C_A + 32].rearrange("(k p) n -> p k n", p=128), 0, KC, 32)]
        base, wres = ws.get(tiles)
        if not P.planning:
            for which in range(2):
                def pe_fn(e, which=which, base=base):
                    ins = None
                    for k in range(KC):
                        a = base + k * 32 + which * 16
                        ins = e.matmul(abk[which].ap[0:16, :], lhsT=ws.buf[:, a:a + 16], rhs=xn.ap[:, k, :],
                                       start=(k == 0), stop=(k == KC - 1))
                    return ins
                P.emit("pe", pe_fn, reads=[wres, xn.res], writes=[abk[which].res])
        ws.done()
        if not P.planning:
            gF, gF2, btF, egF, bgF, kdF = (ff[n] for n in ("gF", "gF2", "btF", "egF", "bgF", "kdF"))
            R16 = slice(0, 16)
            E("act", lambda e: e.activation(out=gF.ap[R16, :], in_=abk[0].ap[R16, :], func=AF.Exp,
                                            bias=sp.ap[R16, o + SP_DTB:o + SP_DTB + 1]), r=[abk[0], sp], w=[gF])
            E("act", lambda e: e.activation(out=gF.ap[R16, :], in_=gF.ap[R16, :], func=AF.Ln, bias=onec.ap[R16, 0:1]),
              r=[gF, onec], w=[gF])
            E("dve", lambda e: e.tensor_scalar(out=gF.ap[R16, :], in0=gF.ap[R16, :], scalar1=sp.ap[R16, o + SP_ALOG:o + SP_ALOG + 1],
                                               scalar2=None, op0=ALU.mult), r=[gF, sp], w=[gF])
            E("act", lambda e: e.activation(out=btF.ap[R16, :], in_=abk[1].ap[R16, :], func=AF.Sigmoid), r=[abk[1]], w=[btF])
            src, dst = gF, gF2
            s = 1
            while s < 64:
                sv, dv = v3(src), v3(dst)
                E("dve", lambda e, sv=sv, dv=dv, s=s: e.tensor_copy(out=dv[R16, :, 0:s], in_=sv[R16, :, 0:s]), r=[src], w=[dst])
                E("dve", lambda e, sv=sv, dv=dv, s=s: e.tensor_tensor(out=dv[R16, :, s:64], in0=sv[R16, :, s:64],
                                                                     in1=sv[R16, :, 0:64 - s], op=ALU.add), r=[src], w=[dst])
                src, dst = dst, src
                s *= 2
            gcF = src
            E("act", lambda e: e.activation(out=egF.ap[R16, :], in_=gcF.ap[R16, :], func=AF.Exp), r=[gcF], w=[egF])
            E("dve", lambda e: e.tensor_tensor(out=bgF.ap[R16, :], in0=btF.ap[R16, :], in1=egF.ap[R16, :], op=ALU.mult),
              r=[btF, egF], w=[bgF])
            E("dve", lambda e: e.tensor_tensor(out=v3(kdF)[R16, :, :], in0=v3(gcF)[R16, :, 63:64].to_broadcast([16, 8, 64]),
                                               in1=v3(gcF)[R16, :, :], op=ALU.subtract), r=[gcF], w=[kdF])
            E("act", lambda e: e.activation(out=kdF.ap[R16, :], in_=kdF.ap[R16, :], func=AF.Exp), r=[kdF], w=[kdF])
            for (srcb, name) in ((gcF, "gcP"), (btF, "btP"), (bgF, "bgP"), (kdF, "kdP")):
                (bk,) = getbanks(1)

                def pe_fn(e, srcb=srcb, bk=bk):
                    ins = None
                    for c in range(8):
                        ins = e.transpose(bk.ap[0:64, c * 16:(c + 1) * 16], srcb.ap[R16, c * 64:(c + 1) * 64], ident.ap[0:16, 0:16])
                    return ins
                E("pe", pe_fn, r=[srcb, ident], w=[bk])
                E("act", lambda e, bk=bk, name=name: e.activation(out=pf[name].ap.rearrange("p c h -> p (c h)")[0:64, :],
                                                                 in_=bk.ap[0:64, 0:128], func=AF.Copy), r=[bk], w=[pf[name]])
        else:
            gcF = btF = None

        spk_gen = [None]

        def spk():
            if spk_gen[0] is not None:
                next(spk_gen[0], None)

        def proj_gen(h, bks):
            cols = (C_Q + h * 128, C_K + h * 128, C_V + h * 128, C_Z + h * 128)
            for k0 in (0, 16):
                tiles = []
                for pi, c0 in enumerate(cols):
                    src = w_in_[k0 * 128:(k0 + 16) * 128, c0:c0 + 128].rearrange("(k p) n -> p k n", p=128)
                    tiles.append((src, pi * 2048, 16, 128))
                base, wres = ws.get(tiles)
                if not P.planning:
                    for pi in range(4):
                        def pe_fn(e, base=base, pi=pi, k0=k0, bank=bks[pi]):
                            ins = None
                            for kk in range(16):
                                a = base + pi * 2048 + kk * 128
                                ins = e.matmul(bank.ap[:, :], lhsT=ws.buf[:, a:a + 128], rhs=xn.ap[:, k0 + kk, :],
                                               start=(k0 + kk == 0), stop=(k0 + kk == KC - 1))
                            return ins
                        if k0 == 0:
                            P.emit("pe", pe_fn, reads=[wres, xn.res], writes=[bks[pi].res])
                        else:
                            tok = P.emit("pe", pe_fn, reads=[wres, xn.res])
                            bks[pi].res.w = tok
                        yield
                ws.done()
                bg.run(1)

        def exhaust(g):
            if g is not None:
                for _ in g:
                    pass

        if True:
            def evac_head(bks, h):
                R64 = slice(0, 64)
                outs = (qT, kT, vT)
                for xi in range(3):
                    rb = raw[raw_ctr[0] % 2]
                    raw_ctr[0] += 1
                    hc = xi * 16 + h
                    E("act", lambda e, rb=rb, bk=bks[xi]: e.activation(out=rb.ap[:, 3:3 + T], in_=bk.ap[:, :], func=AF.Copy),
                      r=[bks[xi]], w=[rb])
                    E("dve", lambda e, rb=rb, hc=hc: e.tensor_copy(out=rb.ap[:, 0:3], in_=halo_q.ap[:, hc, :]), r=[halo_q], w=[rb])
                    E("dve", lambda e, rb=rb, hc=hc: e.tensor_copy(out=halo_q.ap[:, hc, :], in_=rb.ap[:, T:T + 3]), r=[rb], w=[halo_q])
                    conv_taps(rb, outs[xi], o + SP_DNW + hc * 4, 4)
                    E("act", lambda e, d=outs[xi]: e.activation(out=d.ap[:, :], in_=d.ap[:, :], func=AF.Silu), r=[outs[xi]], w=[outs[xi]])
                E("act", lambda e: e.activation(out=zs.ap[:, :], in_=bks[3].ap[:, :], func=AF.Silu), r=[bks[3]], w=[zs])
                for (d, sc) in ((qT, 128 ** -0.5), (kT, 1.0)):
                    sq = getstg()
                    E("act", lambda e, d=d, sq=sq: e.activation(out=sq.ap[:, :], in_=d.ap[:, :], func=AF.Square), r=[d], w=[sq])
                    (bk,) = getbanks(1)
                    E("pe", lambda e, sq=sq, bk=bk: e.matmul(bk.ap[:, :], lhsT=ones_f.ap[:, :], rhs=sq.ap[:, :], start=True, stop=True),
                      r=[sq, ones_f], w=[bk])
                    rstd_from_bank(bk, sq, 1.0)
                    E("dve", lambda e, d=d, sq=sq, sc=sc: e.scalar_tensor_tensor(out=d.ap[:, :], in0=d.ap[:, :], scalar=float(sc),
                                                                                in1=sq.ap[:, :], op0=ALU.mult, op1=ALU.mult),
                      r=[d, sq], w=[d])
                E("dve", lambda e: e.tensor_copy(out=selb.ap[0:16, :], in_=ident.ap[0:16, h:h + 1].to_broadcast([16, 128])),
                  r=[ident], w=[selb])
                for (srcb, dstb, fn) in ((gcF, gcB, AF.Copy), (gcF, egB, AF.Exp), (btF, btB, AF.Copy)):
                    (bk,) = getbanks(1)
                    E("pe", lambda e, srcb=srcb, bk=bk: e.matmul(bk.ap[:, :], lhsT=selb.ap[0:16, :], rhs=srcb.ap[0:16, :],
                                                                start=True, stop=True), r=[selb, srcb], w=[bk])
                    E("act", lambda e, bk=bk, dstb=dstb, fn=fn: e.activation(out=dstb.ap[:, :], in_=bk.ap[:, :], func=fn),
                      r=[bk], w=[dstb])
                for (srcb, dstb) in ((kT, kP), (vT, vP)):
                    bk2 = getbanks(2)
                    for half in range(2):
                        def pe_fn(e, srcb=srcb, bk=bk2[half], half=half):
                            ins = None
                            for cc in range(4):
                                c = half * 4 + cc
                                ins = e.transpose(bk.ap[0:64, cc * 128:(cc + 1) * 128], srcb.ap[:, c * 64:(c + 1) * 64], ident.ap[:, :])
                            return ins
                        E("pe", pe_fn, r=[srcb, ident], w=[bk2[half]])
                        E("act", lambda e, bk=bk2[half], dstb=dstb, half=half: e.activation(
                            out=dstb.ap[R64, half * 4:(half + 1) * 4, :].rearrange("p c d -> p (c d)"), in_=bk.ap[R64, :], func=AF.Copy),
                          r=[bk2[half]], w=[dstb])
                def bc(pn):
                    return pf[pn].ap[R64, :, h:h + 1].to_broadcast([64, 8, 128])
                E("dve", lambda e: e.tensor_tensor(out=kbgP.ap[R64, :, :], in0=kP.ap[R64, :, :], in1=bc("bgP"), op=ALU.mult),
                  r=[kP, pf["bgP"]], w=[kbgP])
                E("dve", lambda e: e.tensor_tensor(out=kP.ap[R64, :, :], in0=kP.ap[R64, :, :], in1=bc("kdP"), op=ALU.mult),
                  r=[kP, pf["kdP"]], w=[kP])
                E("dve", lambda e: e.tensor_tensor(out=vP.ap[R64, :, :], in0=vP.ap[R64, :, :], in1=bc("btP"), op=ALU.mult),
                  r=[vP, pf["btP"]], w=[vP])
                bkk, bqk = getbanks(2)

                def pe_kk(e):
                    ins = None
                    for c in range(8):
                        cs = slice(c * 64, (c + 1) * 64)
                        ins = e.matmul(bkk.ap[R64, cs], lhsT=kT.ap[:, cs], rhs=kT.ap[:, cs], start=True, stop=True)
                    return ins

                def pe_qk(e):
                    ins = None
                    for c in range(8):
                        cs = slice(c * 64, (c + 1) * 64)
                        ins = e.matmul(bqk.ap[R64, cs], lhsT=kT.ap[:, cs], rhs=qT.ap[:, cs], start=True, stop=True)
                    return ins
                E("pe", pe_kk, r=[kT], w=[bkk])
                E("pe", pe_qk, r=[kT, qT], w=[bqk])
                gcPb = pf["gcP"].ap[R64, :, h:h + 1].to_broadcast([64, 8, 64])
                btPb = pf["btP"].ap[R64, :, h:h + 1].to_broadcast([64, 8, 64])
                E("dve", lambda e: e.tensor_tensor(out=v3(dTm)[R64, :, :], in0=v3(gcB)[R64, :, :], in1=gcPb, op=ALU.subtract),
                  r=[gcB, pf["gcP"]], w=[dTm])
                E("dve", lambda e: e.tensor_tensor(out=dTm.ap[R64, :], in0=dTm.ap[R64, :], in1=mk_mti.ap[R64, :], op=ALU.add),
                  r=[dTm, mk_mti], w=[dTm])
                E("act", lambda e: e.activation(out=dTm.ap[R64, :], in_=dTm.ap[R64, :], func=AF.Exp), r=[dTm], w=[dTm])
                E("dve", lambda e: e.tensor_tensor(out=attnT.ap[R64, :], in0=bqk.ap[R64, :], in1=dTm.ap[R64, :], op=ALU.mult),
                  r=[bqk, dTm], w=[attnT])
                E("dve", lambda e: e.tensor_tensor(out=lT.ap[R64, :], in0=bkk.ap[R64, :], in1=dTm.ap[R64, :], op=ALU.mult),
                  r=[bkk, dTm], w=[lT])
                E("dve", lambda e: e.tensor_tensor(out=lT.ap[R64, :], in0=lT.ap[R64, :], in1=mk_m01.ap[R64, :], op=ALU.mult),
                  r=[lT, mk_m01], w=[lT])
                E("dve", lambda e: e.tensor_tensor(out=lT.ap[R64, :], in0=lT.ap[R64, :], in1=btB.ap[R64, :], op=ALU.mult),
                  r=[lT, btB], w=[lT])
                E("dve", lambda e: e.tensor_tensor(out=v3(lM)[R64, :, :], in0=gcPb, in1=v3(gcB)[R64, :, :], op=ALU.subtract),
                  r=[gcB, pf["gcP"]], w=[lM])
                E("dve", lambda e: e.tensor_tensor(out=lM.ap[R64, :], in0=lM.ap[R64, :], in1=mk_mls.ap[R64, :], op=ALU.add),
                  r=[lM, mk_mls], w=[lM])
                E("act", lambda e: e.activation(out=lM.ap[R64, :], in_=lM.ap[R64, :], func=AF.Exp), r=[lM], w=[lM])
                E("dve", lambda e: e.tensor_tensor(out=v3(lM)[R64, :, :], in0=v3(lM)[R64, :, :], in1=btPb, op=ALU.mult),
                  r=[lM, pf["btP"]], w=[lM])
                E("dve", lambda e: e.tensor_tensor(out=lM.ap[R64, :], in0=lM.ap[R64, :], in1=bkk.ap[R64, :], op=ALU.mult),
                  r=[lM, bkk], w=[lM])
                E("dve", lambda e: e.tensor_tensor(out=TTb[0].ap[R64, :], in0=mk_i8.ap[R64, :], in1=lT.ap[R64, :], op=ALU.subtract),
                  r=[mk_i8, lT], w=[TTb[0]])
                Pc, PTc, TTc = lM, lT, TTb[0]
                for s in range(5):
                    Pn = (dTm, lM)[s % 2]
                    PTn = (PTx, lT)[s % 2]
                    TTn = TTb[(s + 1) % 2]
                    ba, bb_, bc_ = getbanks(3)

                    def pe_sq(e, Pc=Pc, PTc=PTc, bk=ba):
                        ins = None
                        for c in range(8):
                            cs = slice(c * 64, (c + 1) * 64)
                            ins = e.matmul(bk.ap[R64, cs], lhsT=PTc.ap[R64, cs], rhs=Pc.ap[R64, cs], start=True, stop=True)
                        return ins
                    E("pe", pe_sq, r=[Pc, PTc], w=[ba])
                    E("act", lambda e, Pn=Pn, bk=ba: e.activation(out=Pn.ap[R64, :], in_=bk.ap[R64, :], func=AF.Copy), r=[ba], w=[Pn])
                    if s < 4:
                        def pe_sqT(e, Pc=Pc, PTc=PTc, bk=bb_):
                            ins = None
                            for c in range(8):
                                cs = slice(c * 64, (c + 1) * 64)
                                ins = e.matmul(bk.ap[R64, cs], lhsT=Pc.ap[R64, cs], rhs=PTc.ap[R64, cs], start=True, stop=True)
                            return ins
                        E("pe", pe_sqT, r=[Pc, PTc], w=[bb_])
                        E("dve", lambda e, PTn=PTn, bk=bb_: e.tensor_copy(out=PTn.ap[R64, :], in_=bk.ap[R64, :]), r=[bb_], w=[PTn])

                    def pe_tt(e, Pn=Pn, TTc=TTc, bk=bc_):
                        ins = None
                        for c in range(8):
                            cs = slice(c * 64, (c + 1) * 64)
                            ins = e.matmul(bk.ap[R64, cs], lhsT=Pn.ap[R64, cs], rhs=TTc.ap[R64, cs], start=True, stop=True)
                        return ins
                    E("pe", pe_tt, r=[Pn, TTc], w=[bc_])
                    spk()
                    E("dve", lambda e, TTn=TTn, TTc=TTc, bk=bc_: e.tensor_tensor(out=TTn.ap[R64, :], in0=bk.ap[R64, :], in1=TTc.ap[R64, :],
                                                                               op=ALU.add), r=[bc_, TTc], w=[TTn])
                    Pc, PTc, TTc = Pn, PTn, TTn
                TT = TTc
                (bw,) = getbanks(1)

                def pe_w(e):
                    ins = None
                    for c in range(8):
                        cs = slice(c * 64, (c + 1) * 64)
                        ins = e.matmul(bw.ap[:, cs], lhsT=kbgP.ap[R64, c, :], rhs=TT.ap[R64, cs], start=True, stop=True)
                    return ins
                E("pe", pe_w, r=[kbgP, TT], w=[bw])
                E("act", lambda e: e.activation(out=wT.ap[:, :], in_=bw.ap[:, :], func=AF.Copy), r=[bw], w=[wT])
                bu = getbanks(2)
                for half in range(2):
                    def pe_u(e, half=half):
                        ins = None
                        for cc in range(4):
                            c = half * 4 + cc
                            ins = e.matmul(bu[half].ap[R64, cc * 128:(cc + 1) * 128], lhsT=TT.ap[R64, c * 64:(c + 1) * 64],
                                           rhs=vP.ap[R64, c, :], start=True, stop=True)
                        return ins
                    E("pe", pe_u, r=[TT, vP], w=[bu[half]])
                    E("act", lambda e, half=half: e.activation(out=uP.ap[R64, half * 4:(half + 1) * 4, :].rearrange("p c d -> p (c d)"),
                                                              in_=bu[half].ap[R64, :], func=AF.Copy), r=[bu[half]], w=[uP])
                E("dve", lambda e: e.tensor_tensor(out=qT.ap[:, :], in0=qT.ap[:, :], in1=egB.ap[:, :], op=ALU.mult), r=[qT, egB], w=[qT])
                oT = vT
                seq = [Sst[h], Stmp[0], Stmp[1]]
                cur = Sst[h]
                for c in range(8):
                    cs = slice(c * 64, (c + 1) * 64)
                    nxt = Sst[h] if c == 7 else Stmp[c % 2]
                    vn = vnew[c % 2]
                    b1, b2, b3 = getbanks(3)
                    E("pe", lambda e, b1=b1, cs=cs, cur=cur: e.matmul(b1.ap[R64, 0:128], lhsT=wT.ap[:, cs], rhs=cur.ap[:, :], start=True, stop=True),
                      r=[wT, cur], w=[b1])
                    E("dve", lambda e, b1=b1, vn=vn, c=c: e.tensor_tensor(out=vn.ap[R64, :], in0=uP.ap[R64, c, :], in1=b1.ap[R64, 0:128],
                                                                         op=ALU.subtract), r=[uP, b1], w=[vn])

                    def pe_o(e, b2=b2, cs=cs, cur=cur, vn=vn):
                        e.matmul(b2.ap[:, 0:64], lhsT=cur.ap[:, :], rhs=qT.ap[:, cs], start=True, stop=False)
                        return e.matmul(b2.ap[:, 0:64], lhsT=vn.ap[R64, :], rhs=attnT.ap[R64, cs], start=False, stop=True)
                    E("pe", pe_o, r=[cur, qT, vn, attnT], w=[b2])
                    E("act", lambda e, b2=b2, cs=cs: e.activation(out=oT.ap[:, cs], in_=b2.ap[:, 0:64], func=AF.Copy), r=[b2], w=[oT])
                    E("pe", lambda e, b3=b3, c=c, vn=vn: e.matmul(b3.ap[:, 0:128], lhsT=kP.ap[R64, c, :], rhs=vn.ap[R64, :], start=True, stop=True),
                      r=[kP, vn], w=[b3])
                    spk()
                    E("dve", lambda e, b3=b3, cur=cur, nxt=nxt, c=c: e.scalar_tensor_tensor(
                        out=nxt.ap[:, :], in0=cur.ap[:, :], scalar=egB.ap[:, c * 64 + 63:c * 64 + 64], in1=b3.ap[:, 0:128],
                        op0=ALU.mult, op1=ALU.add), r=[cur, egB, b3], w=[nxt])
                    cur = nxt
                sq = getstg()
                E("act", lambda e, sq=sq: e.activation(out=sq.ap[:, :], in_=oT.ap[:, :], func=AF.Square), r=[oT], w=[sq])
                (bk,) = getbanks(1)
                E("pe", lambda e, sq=sq, bk=bk: e.matmul(bk.ap[:, :], lhsT=ones_f.ap[:, :], rhs=sq.ap[:, :], start=True, stop=True),
                  r=[sq, ones_f], w=[bk])
                rstd_from_bank(bk, sq, 1.0 / 128)
                E("dve", lambda e, sq=sq: e.scalar_tensor_tensor(out=sq.ap[:, :], in0=oT.ap[:, :], scalar=sp.ap[:, o + SP_NG:o + SP_NG + 1],
                                                                in1=sq.ap[:, :], op0=ALU.mult, op1=ALU.mult), r=[oT, sp, sq], w=[sq])
                E("dve", lambda e, sq=sq: e.tensor_tensor(out=ogT.ap[:, h, :], in0=sq.ap[:, :], in1=zs.ap[:, :], op=ALU.mult),
                  r=[sq, zs], w=[ogT])
        if OPT_A:
            hb = banks[0:4]
            bank_pool[0] = [4, 5, 6]
            exhaust(proj_gen(0, hb))
            for h in range(NH):
                g = proj_gen(h + 1, hb) if h + 1 < NH else None
                spk_gen[0] = g
                if not P.planning:
                    evac_head(hb, h)
                spk_gen[0] = None
                exhaust(g)
            bank_pool[0] = [0, 1, 2, 3, 4, 5]
        else:
            for h in range(NH):
                job([([(w_in_, C_Q + h * 128, 128), (w_in_, C_K + h * 128, 128), (w_in_, C_V + h * 128, 128),
                       (w_in_, C_Z + h * 128, 128)], KC, rx)], (lambda bks, h=h: evac_head(bks, h)))

        P.fence()
        for c in range(16):
            def evac_sc(bks, c=c):
                tmp = getstg()
                rb = raw[raw_ctr[0] % 2]
                raw_ctr[0] += 1
                E("act", lambda e, tmp=tmp: e.activation(out=tmp.ap[:, :], in_=bks[1].ap[:, :], func=AF.Copy), r=[bks[1]], w=[tmp])
                E("dve", lambda e, tmp=tmp, rb=rb: e.tensor_tensor(out=rb.ap[:, 2:2 + T], in0=tmp.ap[:, :], in1=bks[2].ap[:, :], op=ALU.mult),
                  r=[tmp, bks[2]], w=[rb])
                E("dve", lambda e, rb=rb: e.tensor_copy(out=rb.ap[:, 0:2], in_=halo_s.ap[:, c, :]), r=[halo_s], w=[rb])
                E("dve", lambda e, rb=rb: e.tensor_copy(out=halo_s.ap[:, c, :], in_=rb.ap[:, T:T + 2]), r=[rb], w=[halo_s])
                conv_taps(rb, tmp, o + SP_SCW + c * 3, 3)
                E("dve", lambda e, tmp=tmp: e.tensor_tensor(out=scT.ap[:, c, :], in0=tmp.ap[:, :], in1=bks[0].ap[:, :], op=ALU.mult),
                  r=[tmp, bks[0]], w=[scT])
            job([([(w_in_, C_SB + c * 128, 128), (w_in_, C_SC + c * 128, 128), (w_in_, C_SX + c * 128, 128)], KC, rx)], evac_sc)

        rog, rsc = rhs_of(ogT), rhs_of(scT)
        if OPT_B:
            for cp in range(KC // 2):
                c0 = cp * 2

                def evac_g(bks):
                    for j in range(4):
                        E("act", lambda e, j=j: e.activation(out=nsq[j].ap[:, :], in_=bks[j].ap[:, :], func=AF.Sigmoid),
                          r=[bks[j]], w=[nsq[j]])

                def evac_y(bks, c0=c0):
                    for j in range(2):
                        t1, t2 = getstg(), getstg()
                        E("dve", lambda e, t1=t1, j=j: e.tensor_tensor(out=t1.ap[:, :], in0=bks[j].ap[:, :], in1=nsq[j].ap[:, :], op=ALU.mult),
                          r=[bks[j], nsq[j]], w=[t1])
                        E("dve", lambda e, t2=t2, j=j: e.tensor_tensor(out=t2.ap[:, :], in0=bks[2 + j].ap[:, :], in1=nsq[2 + j].ap[:, :], op=ALU.mult),
                          r=[bks[2 + j], nsq[2 + j]], w=[t2])
                        E("dve", lambda e, t1=t1, t2=t2, j=j: e.tensor_tensor(out=mT.ap[:, c0 + j, :], in0=t1.ap[:, :], in1=t2.ap[:, :], op=ALU.add),
                          r=[t1, t2], w=[mT])
                job([([(w_in_, C_GA + c0 * 128, 256), (w_in_, C_GB + c0 * 128, 256)], KC, rx)], evac_g)
                job([([(W["w_dn_out"][l], c0 * 128, 256)], 16, rog),
                     ([(W["w_sc_out"][l], c0 * 128, 256)], 16, rsc)], evac_y)
        else:
            for c in range(KC):
                def evac_m(bks, c=c):
                    t1, t2 = getstg(), getstg()
                    E("act", lambda e, t1=t1: e.activation(out=t1.ap[:, :], in_=bks[0].ap[:, :], func=AF.Sigmoid), r=[bks[0]], w=[t1])
                    E("dve", lambda e, t1=t1: e.tensor_tensor(out=t1.ap[:, :], in0=t1.ap[:, :], in1=bks[2].ap[:, :], op=ALU.mult),
                      r=[t1, bks[2]], w=[t1])
                    E("act", lambda e, t2=t2: e.activation(out=t2.ap[:, :], in_=bks[1].ap[:, :], func=AF.Sigmoid), r=[bks[1]], w=[t2])
                    E("dve", lambda e, t2=t2: e.tensor_tensor(out=t2.ap[:, :], in0=t2.ap[:, :], in1=bks[3].ap[:, :], op=ALU.mult),
                      r=[t2, bks[3]], w=[t2])
                    E("dve", lambda e, t1=t1, t2=t2: e.tensor_tensor(out=mT.ap[:, c, :], in0=t1.ap[:, :], in1=t2.ap[:, :], op=ALU.add),
                      r=[t1, t2], w=[mT])
                job([([(w_in_, C_GA + c * 128, 128), (w_in_, C_GB + c * 128, 128)], KC, rx),
                     ([(W["w_dn_out"][l], c * 128, 128)], 16, rog),
                     ([(W["w_sc_out"][l], c * 128, 128)], 16, rsc)], evac_m)
        if dbg2 is not None and not P.planning and l == 0 and t == 0:
            P.emit("sync", lambda e: e.dma_start(out=dbg2[:, 0:16 * T], in_=ogT.ap.rearrange("p f t -> p (f t)")), reads=[ogT.res], dmasem=dbg_sem)
            P.emit("sync", lambda e: e.dma_start(out=dbg2[:, 16 * T:32 * T], in_=scT.ap.rearrange("p f t -> p (f t)")), reads=[scT.res], dmasem=dbg_sem)
            P.emit("sync", lambda e: e.dma_start(out=dbg2[:, 32 * T:64 * T], in_=mT.ap.rearrange("p f t -> p (f t)")), reads=[mT.res], dmasem=dbg_sem)
        ost = OutStats(t)
        rm = rhs_of(mT)
        nn = next_norm[0]
        for og in range(8):
            def evac2(bks, og=og):
                for j in range(4):
                    ost.evac(og * 4 + j, bks[j])
            if nn is not None and not P.planning:
                nn.pre(og)
            job([([(W["w_o"][l], og * 512, 512)], KC, rm)], evac2)
            if nn is not None and not P.planning:
                nn.stat(og)
        if nn is not None and not P.planning:
            nn.finish()
        update_pass(l, t, "mix_post_g", 1.0, ost)
        P.fence()

    dbg_sem = P.newsem("dbg")
    dbg_ctr = [0]

    def dump():
        if P.planning or dbgd is None:
            return
        i = dbg_ctr[0]
        dbg_ctr[0] += 1
        for c in range(KC):
            P.emit("sync", lambda e, c=c, i=i: e.dma_start(out=dbgd[i, c], in_=hT[c, :, 0:T]),
                   reads=[hT_res[0][c]], dmasem=dbg_sem)

    next_norm = [None]
    dump_after = set()

    def program():
        if not P.planning:
            P.emit("sync", lambda e: e.dma_start(out=sp.ap[:, :], in_=spd), writes=[sp.res], dmasem=misc_sem)
            P.emit("sync", lambda e: e.dma_start(out=ident.ap[:, :], in_=cstd[:, CS_ID:CS_ID + 128]), writes=[ident.res], dmasem=misc_sem)
            E("dve", lambda e: e.memset(ones_bf.ap[:, :], 1.0), w=[ones_bf])
            E("dve", lambda e: e.memset(ones_f.ap[:, :], 1.0), w=[ones_f])
            E("dve", lambda e: e.memset(epsc.ap[:, :], EPS), w=[epsc])
            E("dve", lambda e: e.memset(onec.ap[:, :], 1.0), w=[onec])
            P.fence()
            for t in range(NT):
                init_pass(t)
        units = []
        for l in range(L):
            units += [("ffn1", l, t) for t in range(NT)] + [("mix", l, t) for t in range(NT)] + [("ffn2", l, t) for t in range(NT)]
        gn = {"ffn1": "ffn1_pre_g", "mix": "mix_pre_g", "ffn2": "ffn2_pre_g"}
        pending_norm = None
        for ui, (kind, l, t) in enumerate(units):
            if ui > 0 and units[ui - 1][0] != kind:
                P.fence()
                if kind == "mix" and not P.planning:
                    mixer_setup(l)
            elif kind == "mix" and ui == 0 and not P.planning:
                mixer_setup(l)
            if not P.planning:
                if pending_norm is None:
                    if bg.has_tile(t):
                        bg.drain()
                    NormParts(l, t, gn[kind]).all()
                pending_norm = None
                nxt = units[ui + 1] if ui + 1 < len(units) else None
                if nxt is not None and nxt[2] != t and not bg.has_tile(nxt[2]):
                    next_norm[0] = NormParts(nxt[1], nxt[2], gn[nxt[0]])
                    pending_norm = next_norm[0]
                else:
                    next_norm[0] = None
            if kind == "mix":
                mixer(l, t)
            else:
                ffn(l, t, 1 if kind == "ffn1" else 2)
            if kind == "mix" and ui in dump_after:
                pass
            if ui + 1 == len(units) or units[ui + 1][0] != kind:
                if not P.planning and dbgd is not None:
                    bg.drain()
                dump()
        if not P.planning:
            bg.drain()
        if not P.planning:
            for t in range(NT):
                final_pass(t)
            P.final_wait("sync")

    P.planning = True
    program()
    P.planning = False
    bank_ctr[0] = 0
    stg_ctr[0] = 0
    raw_ctr[0] = 0
    program()
    assert ws.consumed == len(ws.plan), (ws.consumed, len(ws.plan))
    P.run_block()
    print("ops", {k: len(v) for k, v in P.q.items()}, "sems", P.nsem, "wtiles", len(ws.plan))
    return nc


def _pack_small(inputs):
    L = 2
    sp = np.zeros((128, L * SP_L), np.float32)
    for l in range(L):
        o = l * SP_L
        for n, c0 in SP_G.items():
            sp[:, o + c0:o + c0 + 32] = np.asarray(inputs[n][l], np.float32).reshape(32, 128).T
        sp[:, o + SP_DNW:o + SP_DNW + 192] = np.asarray(inputs["dn_conv_w"][l], np.float32).reshape(48, 128, 4).transpose(1, 0, 2).reshape(128, 192)
        sp[:, o + SP_SCW:o + SP_SCW + 48] = np.asarray(inputs["sc_conv_w"][l], np.float32).reshape(16, 128, 3).transpose(1, 0, 2).reshape(128, 48)
        sp[:, o + SP_NG] = np.asarray(inputs["dn_norm_g"][l], np.float32)
        sp[0:16, o + SP_ALOG] = np.asarray(inputs["dn_a_log"][l], np.float32)
        sp[0:16, o + SP_DTB] = np.asarray(inputs["dn_dt_bias"][l], np.float32)
    return sp


def _consts():
    c = np.zeros((128, CS_N), np.float32)
    c[:, CS_ID:CS_ID + 128] = np.eye(128, dtype=np.float32)
    p = np.arange(64)[:, None]
    f = np.arange(64)[None, :]
    mti = np.where(f >= p, 0.0, NEG).astype(np.float32)
    m01 = np.where(f > p, 1.0, 0.0).astype(np.float32)
    mls = np.where(p > f, 0.0, NEG).astype(np.float32)
    i8 = np.eye(64, dtype=np.float32)
    c[0:64, CS_MTI:CS_MTI + 512] = np.tile(mti, (1, 8))
    c[0:64, CS_M01:CS_M01 + 512] = np.tile(m01, (1, 8))
    c[0:64, CS_MLS:CS_MLS + 512] = np.tile(mls, (1, 8))
    c[0:64, CS_I8:CS_I8 + 512] = np.tile(i8, (1, 8))
    return c


_NT = 4
_DBG = False
_LAST = {}


def kernel(**inputs):
    x = np.asarray(inputs["x"], np.float32)
    B, S, _ = x.shape
    NT = _NT
    nc = build_program(NT, _DBG)
    sp = _pack_small(inputs)
    cst = _consts()
    wnames = ["w_ffn1_in", "w_ffn1_out", "w_in", "w_dn_out", "w_sc_out", "w_o", "w_ffn2_in", "w_ffn2_out"]
    shared = {n: np.ascontiguousarray(np.asarray(inputs[n], np.float32)) for n in wnames}
    in_maps = []
    for b in range(NCORES):
        m = dict(shared)
        m["x"] = np.ascontiguousarray(x[b, :NT * T])
        m["sp"] = sp
        m["cst"] = cst
        in_maps.append(m)
    res = run_bass_kernel_spmd(nc, in_maps, core_ids=list(range(NCORES)))
    out = np.zeros((B, S, D), np.float32)
    for b in range(NCORES):
        out[b, :NT * T] = np.asarray(res.results[b]["out"], np.float32)
    if _DBG:
        _LAST["dbg"] = np.asarray(res.results[0]["dbg"], np.float32)
        _LAST["dbg2"] = np.asarray(res.results[0]["dbg2"]).astype(np.float32)
    return out
```
